# Optimizing a Trainium2 kernel written in Bass

```python
import math
import jax, jax.numpy as jnp
from jax import lax
import numpy as np

D_MODEL = 2048
BATCH = 2
SEQ = 4096
DEPTH = 2

HEAD_DIM = 128
N_HEADS_TOTAL = D_MODEL // HEAD_DIM
N_HEADS_SB = N_HEADS_TOTAL // 4
N_HEADS_DIFF = N_HEADS_TOTAL // 4
N_HEADS_DIL = N_HEADS_TOTAL - N_HEADS_SB - N_HEADS_DIFF
DIFF_QK_DIM = HEAD_DIM // 2
W_SB = N_HEADS_SB * HEAD_DIM
W_DIFF = N_HEADS_DIFF * HEAD_DIM
W_DIL = N_HEADS_DIL * HEAD_DIM
MIX_WIDTH = W_SB + W_DIFF + W_DIL
IN_PROJ_WIDTH = 3 * MIX_WIDTH
D_FF = ((8 * D_MODEL // 3 + 255) // 256) * 256
ROPE_THETA = 500000.0
ROPE_FRACTION = 0.25
DILATED_PATTERNS = ((128, 1), (512, 4), (2048, 16))
BLOCK_Q = 128
NORM_EPS = 1e-6

kernel_name = "hybrid_sb_diff_dilated_parallel_heads"


def rmsnorm(x, g):
    x32 = x.astype(jnp.float32)
    y = x32 * lax.rsqrt(jnp.mean(x32 * x32, axis=-1, keepdims=True) + NORM_EPS)
    return y.astype(x.dtype) * g


def partial_rope(x, positions):
    rot = int(x.shape[-1] * ROPE_FRACTION)
    half = rot // 2
    inv_freq = ROPE_THETA ** (-jnp.arange(half, dtype=jnp.float32) / half)
    ang = positions.astype(jnp.float32)[:, None, :, None] * inv_freq
    cos, sin = jnp.cos(ang), jnp.sin(ang)
    xr = x[..., :rot].astype(jnp.float32)
    x1, x2 = xr[..., :half], xr[..., half:]
    rotated = jnp.concatenate([x1 * cos - x2 * sin, x2 * cos + x1 * sin], axis=-1)
    return jnp.concatenate([rotated.astype(x.dtype), x[..., rot:]], axis=-1)


def to_heads(t, n_heads, dh):
    b, s, _ = t.shape
    return t.reshape(b, s, n_heads, dh).transpose(0, 2, 1, 3)


def merge_heads(t):
    b, h, s, dh = t.shape
    return t.transpose(0, 2, 1, 3).reshape(b, s, h * dh)


def query_blocks(q):
    *lead, s, dh = q.shape
    nb = s // BLOCK_Q
    qb = q.reshape(*lead, nb, BLOCK_Q, dh)
    return jnp.moveaxis(qb, -3, 0), nb


def unblock(o):
    o = jnp.moveaxis(o, 0, -3)
    *lead, nb, bq, dh = o.shape
    return o.reshape(*lead, nb * bq, dh)


def stick_breaking_attention(q, k, v):
    s_len, dh = q.shape[-2], q.shape[-1]
    scale = dh ** -0.5
    qb, nb = query_blocks(q)
    key_pos = jnp.arange(s_len)

    def one_block(args):
        blk, qblk = args
        q_pos = blk * BLOCK_Q + jnp.arange(BLOCK_Q)
        z = jnp.einsum('bhqd,bhkd->bhqk', qblk, k, preferred_element_type=jnp.float32) * scale
        strict = key_pos[None, :] < q_pos[:, None]
        neg_log_keep = jnp.where(strict, jax.nn.softplus(z), 0.0)
        between = lax.cumsum(neg_log_keep, axis=3, reverse=True) - neg_log_keep
        a = jnp.where(strict, jnp.exp(jax.nn.log_sigmoid(z) - between), 0.0)
        return jnp.einsum('bhqk,bhkd->bhqd', a.astype(v.dtype), v)

    return unblock(lax.map(one_block, (jnp.arange(nb), qb)))


def differential_attention(q2, k2, v, lam):
    s_len, dqk = q2.shape[-2], q2.shape[-1]
    scale = dqk ** -0.5
    qb, nb = query_blocks(q2)
    key_pos = jnp.arange(s_len)

    def one_block(args):
        blk, qblk = args
        q_pos = blk * BLOCK_Q + jnp.arange(BLOCK_Q)
        sc = jnp.einsum('cbhqd,cbhkd->cbhqk', qblk, k2, preferred_element_type=jnp.float32) * scale
        causal = key_pos[None, :] <= q_pos[:, None]
        p = jax.nn.softmax(jnp.where(causal, sc, -jnp.inf), axis=-1)
        attn = p[0] - lam * p[1]
        return jnp.einsum('bhqk,bhkd->bhqd', attn.astype(v.dtype), v)

    return unblock(lax.map(one_block, (jnp.arange(nb), qb)))


def dilated_attention(q, k, v):
    dh = q.shape[-1]
    scale = dh ** -0.5
    qb, nb = query_blocks(q)

    def one_block(args):
        blk, qblk = args
        q_pos = blk * BLOCK_Q + jnp.arange(BLOCK_Q)
        outs, lses = [], []
        for window, dilation in DILATED_PATTERNS:
            n_keys = window // dilation + 1
            idx = q_pos[:, None] - dilation * jnp.arange(n_keys)[None, :]
            valid = idx >= 0
            idx_c = jnp.maximum(idx, 0)
            kg = jnp.take(k, idx_c, axis=2)
            vg = jnp.take(v, idx_c, axis=2)
            sc = jnp.einsum('bhqd,bhqmd->bhqm', qblk, kg, preferred_element_type=jnp.float32) * scale
            sc = jnp.where(valid, sc, -jnp.inf)
            m = jnp.max(sc, axis=-1, keepdims=True)
            p = jnp.exp(sc - m)
            l = jnp.sum(p, axis=-1, keepdims=True)
            outs.append(jnp.einsum('bhqm,bhqmd->bhqd', (p / l).astype(v.dtype), vg))
            lses.append(m + jnp.log(l))
        w = jax.nn.softmax(jnp.concatenate(lses, axis=-1), axis=-1)
        o = jnp.einsum('pbhqd,bhqp->bhqd', jnp.stack(outs).astype(jnp.float32), w)
        return o.astype(v.dtype)

    return unblock(lax.map(one_block, (jnp.arange(nb), qb)))


def mixing_sublayer(h, positions, layer_idx, w_in, lambda_q1, lambda_k1, lambda_q2, lambda_k2,
                    g_sb_out, g_diff_out, g_dil_out, w_out):
    b, s, _ = h.shape
    proj = h @ w_in
    o0 = 0
    sb_q, sb_k, sb_v = (proj[..., o0 + i * W_SB:o0 + (i + 1) * W_SB] for i in range(3))
    o0 += 3 * W_SB
    df_q, df_k, df_v = (proj[..., o0 + i * W_DIFF:o0 + (i + 1) * W_DIFF] for i in range(3))
    o0 += 3 * W_DIFF
    dl_q, dl_k, dl_v = (proj[..., o0 + i * W_DIL:o0 + (i + 1) * W_DIL] for i in range(3))

    o_sb = stick_breaking_attention(to_heads(sb_q, N_HEADS_SB, HEAD_DIM),
                                    to_heads(sb_k, N_HEADS_SB, HEAD_DIM),
                                    to_heads(sb_v, N_HEADS_SB, HEAD_DIM))
    o_sb = rmsnorm(o_sb, g_sb_out)

    def two_component(t):
        return t.reshape(b, s, N_HEADS_DIFF, 2, DIFF_QK_DIM).transpose(3, 0, 2, 1, 4)
    dq = partial_rope(two_component(df_q), positions)
    dk = partial_rope(two_component(df_k), positions)
    lambda_init = 0.8 - 0.6 * math.exp(-0.3 * layer_idx)
    lam = (jnp.exp(jnp.sum(lambda_q1.astype(jnp.float32) * lambda_k1.astype(jnp.float32)))
           - jnp.exp(jnp.sum(lambda_q2.astype(jnp.float32) * lambda_k2.astype(jnp.float32)))
           + lambda_init)
    o_df = differential_attention(dq, dk, to_heads(df_v, N_HEADS_DIFF, HEAD_DIM), lam)
    o_df = rmsnorm(o_df, g_diff_out) * (1.0 - lambda_init)

    cq = partial_rope(to_heads(dl_q, N_HEADS_DIL, HEAD_DIM), positions)
    ck = partial_rope(to_heads(dl_k, N_HEADS_DIL, HEAD_DIM), positions)
    o_dl = dilated_attention(cq, ck, to_heads(dl_v, N_HEADS_DIL, HEAD_DIM))
    o_dl = rmsnorm(o_dl, g_dil_out)

    mixed = jnp.concatenate([merge_heads(o_sb), merge_heads(o_df), merge_heads(o_dl)], axis=-1)
    return mixed @ w_out


def swiglu_ffn(h, w_gate, w_up, w_down):
    return (jax.nn.silu(h @ w_gate) * (h @ w_up)) @ w_down


def setup_inputs(seed: int = 0) -> dict:
    key = jax.random.key(seed)
    ks = jax.random.split(key, 16)
    f32 = jnp.float32

    def nrm(k, shape, scale):
        return jax.random.normal(k, shape, f32) * scale

    def gain(k, shape):
        return 1.0 + 0.02 * jax.random.normal(k, shape, f32)

    x = jax.random.normal(ks[0], (BATCH, SEQ, D_MODEL), f32)
    positions = jnp.broadcast_to(jnp.arange(SEQ, dtype=jnp.int32)[None, :], (BATCH, SEQ))
    return {
        "x": x,
        "positions": positions,
        "norm_mix_g": gain(ks[1], (DEPTH, D_MODEL)),
        "w_in": nrm(ks[2], (DEPTH, D_MODEL, IN_PROJ_WIDTH), D_MODEL ** -0.5),
        "lambda_q1": nrm(ks[3], (DEPTH, DIFF_QK_DIM), 0.1),
        "lambda_k1": nrm(ks[4], (DEPTH, DIFF_QK_DIM), 0.1),
        "lambda_q2": nrm(ks[5], (DEPTH, DIFF_QK_DIM), 0.1),
        "lambda_k2": nrm(ks[6], (DEPTH, DIFF_QK_DIM), 0.1),
        "g_sb_out": gain(ks[7], (DEPTH, HEAD_DIM)),
        "g_diff_out": gain(ks[8], (DEPTH, HEAD_DIM)),
        "g_dil_out": gain(ks[9], (DEPTH, HEAD_DIM)),
        "w_out": nrm(ks[10], (DEPTH, MIX_WIDTH, D_MODEL), MIX_WIDTH ** -0.5),
        "norm_ffn_g": gain(ks[11], (DEPTH, D_MODEL)),
        "w_gate": nrm(ks[12], (DEPTH, D_MODEL, D_FF), D_MODEL ** -0.5),
        "w_up": nrm(ks[13], (DEPTH, D_MODEL, D_FF), D_MODEL ** -0.5),
        "w_down": nrm(ks[14], (DEPTH, D_FF, D_MODEL), D_FF ** -0.5),
        "norm_final_g": gain(ks[15], (D_MODEL,)),
    }


def reference(x, positions, norm_mix_g, w_in, lambda_q1, lambda_k1, lambda_q2, lambda_k2,
              g_sb_out, g_diff_out, g_dil_out, w_out, norm_ffn_g, w_gate, w_up, w_down,
              norm_final_g):
    for layer in range(DEPTH):
        h = rmsnorm(x, norm_mix_g[layer])
        x = x + mixing_sublayer(h, positions, layer, w_in[layer],
                                lambda_q1[layer], lambda_k1[layer], lambda_q2[layer], lambda_k2[layer],
                                g_sb_out[layer], g_diff_out[layer], g_dil_out[layer], w_out[layer])
        h = rmsnorm(x, norm_ffn_g[layer])
        x = x + swiglu_ffn(h, w_gate[layer], w_up[layer], w_down[layer])
    return rmsnorm(x, norm_final_g)
```

```python
import math
import os
from contextlib import ExitStack

import numpy as np
import ml_dtypes

import concourse.bass as bass
import concourse.mybir as mybir
from concourse.bass_utils import run_bass_kernel_spmd

F32 = mybir.dt.float32
BF16 = mybir.dt.bfloat16
I32 = mybir.dt.int32
AF = mybir.ActivationFunctionType
ALU = mybir.AluOpType
AX = mybir.AxisListType

D = 2048
S = 4096
T = 1024
NL = 2
DFF = 5632
NFF = DFF // 128
NC = 16
EPS = 1e-6
THETA = 500000.0
PI = math.pi
GFF = 4

CB_ID, CB_ONES, CB_NUINC, CB_NONES, CB_PDIL, CB_PDIFF = 0, 128, 256, 384, 512, 640
CB_TOEP = 768
TOEP_W = 2944
CB_CINC = CB_TOEP + TOEP_W
CB_CSTR = CB_CINC + 896
NB = CB_CSTR + 896
CF_ID = 0
CF_GMIX = 128
CF_GFFN = 160
CF_GFIN = 192
CF_GSB = 208
CF_GDF = 210
CF_GDL = 212
CF_INVF = 214
CF_SGN = 216
CF_EPS = 218
CF_ONE = 219
CF_LNA = 220
CF_LAM = 224
NF = CF_LAM + NL * 4 * 64


def lambda_init(l):
    return 0.8 - 0.6 * math.exp(-0.3 * l)


class StopBuild(Exception):
    pass


class Tk:
    __slots__ = ("name", "w", "r", "multi")

    def __init__(self, name="", multi=False):
        self.name = name
        self.w = [] if multi else None
        self.r = []
        self.multi = multi


class Op:
    __slots__ = ("eng", "fn", "deps", "dma", "semkey", "val", "inc", "target", "sem")

    def __init__(self, eng, fn, dma, inc):
        self.eng = eng
        self.fn = fn
        self.deps = set()
        self.dma = dma
        self.semkey = None
        self.val = 0
        self.inc = inc
        self.target = False
        self.sem = None


class Prog:
    ENGS = ["pe", "act", "dve", "pool", "sp"]

    def __init__(self):
        self.stream = {e: [] for e in self.ENGS}
        self.semcount = {}
        self.since_barrier = []

    def op(self, eng, fn, reads=(), writes=(), dma=False, inc=16, semkey=None, bar=True):
        o = Op(eng, fn, dma, inc)
        deps = set()
        for t in list(reads) + list(writes):
            if t.multi:
                if t not in writes:
                    deps.update(t.w)
            elif t.w is not None:
                deps.add(t.w)
        for t in writes:
            deps.update(t.r)
        o.deps = deps
        for t in writes:
            if t.multi:
                t.w.append(o)
            else:
                t.w = o
                t.r = []
        for t in reads:
            t.r.append(o)
        if dma:
            if semkey is None:
                semkey = id(writes[0])
            o.semkey = semkey
            self.semcount[semkey] = self.semcount.get(semkey, 0) + inc
            o.val = self.semcount[semkey]
        self.stream[eng].append(o)
        if bar:
            self.since_barrier.append(o)
        return o

    def barrier(self):
        lasts = []
        for e in self.ENGS:
            for o in reversed(self.stream[e]):
                if not o.dma and o.fn is not None:
                    lasts.append(o)
                    break
        dmas = [o for o in self.since_barrier if o.dma]
        self.since_barrier = []
        for e in self.ENGS:
            o = Op(e, None, False, 0)
            o.deps = set(lasts) | set(dmas)
            self.stream[e].append(o)

    def emit(self, nc, stack):
        for e in self.ENGS:
            for o in self.stream[e]:
                for d in o.deps:
                    if not d.dma:
                        d.target = True
        engsem = {e: stack.enter_context(nc.semaphore("es_" + e)) for e in self.ENGS}
        keysem = {}
        for e in self.ENGS:
            cnt = 0
            for o in self.stream[e]:
                if o.dma:
                    if o.semkey not in keysem:
                        keysem[o.semkey] = stack.enter_context(nc.semaphore("ds%d" % len(keysem)))
                    o.sem = keysem[o.semkey]
                else:
                    if o.target:
                        cnt += 1
                    o.val = cnt
                    o.sem = engsem[e]
        self.nsem = len(keysem) + len(engsem)
        print('semaphores used:', self.nsem)
        block = stack.enter_context(nc.Block())
        prog = self

        def run(e):
            def body(eng):
                seen = {}
                for o in prog.stream[e]:
                    need = {}
                    for d in o.deps:
                        if e == "pe" and d.eng == "pe" and not d.dma:
                            continue
                        k = id(d.sem)
                        if k not in need or need[k][1] < d.val:
                            need[k] = (d.sem, d.val)
                    for k, (sem, val) in need.items():
                        if seen.get(k, 0) < val:
                            eng.wait_ge(sem, val)
                            seen[k] = val
                    if o.fn is None:
                        continue
                    inst = o.fn(eng)
                    if o.dma:
                        if o.inc == 1:
                            inst.then_inc(o.sem)
                        else:
                            inst.then_inc(o.sem, o.inc)
                    elif o.target:
                        inst.then_inc(o.sem, 1)
            return body

        block.tensor(run("pe"))
        block.scalar(run("act"))
        block.vector(run("dve"))
        block.gpsimd(run("pool"))
        block.sync(run("sp"))


def _dil_count(delta):
    c = np.zeros_like(delta, dtype=np.float32)
    c += ((delta >= 0) & (delta <= 128)).astype(np.float32)
    c += ((delta >= 0) & (delta <= 512) & (delta % 4 == 0)).astype(np.float32)
    c += ((delta >= 0) & (delta <= 2048) & (delta % 16 == 0)).astype(np.float32)
    return c


def build_cb():
    cb = np.zeros((128, NB), np.float32)
    j = np.arange(128)[:, None]
    k = np.arange(128)[None, :]
    cb[:, CB_ID:CB_ID + 128] = np.eye(128)
    cb[:, CB_ONES:CB_ONES + 128] = 1.0
    cb[:, CB_NUINC:CB_NUINC + 128] = -(j >= k).astype(np.float32)
    cb[:, CB_NONES:CB_NONES + 128] = -1.0
    pd = np.zeros((128, 128), np.float32)
    for d in range(32):
        partner = d + 16 if d < 16 else d - 16
        pd[partner, d] = 1.0
    cb[:, CB_PDIL:CB_PDIL + 128] = pd
    pf = np.zeros((128, 128), np.float32)
    for base in (0, 64):
        for d in range(16):
            partner = d + 8 if d < 8 else d - 8
            pf[base + partner, base + d] = 1.0
    cb[:, CB_PDIFF:CB_PDIFF + 128] = pf
    ki = np.arange(128)[:, None]
    xx = np.arange(TOEP_W)[None, :]
    cb[:, CB_TOEP:CB_TOEP + TOEP_W] = _dil_count(xx - 384 - ki)
    xx = np.arange(896)[None, :]
    cb[:, CB_CINC:CB_CINC + 896] = ((xx - 384 - ki) >= 0).astype(np.float32)
    cb[:, CB_CSTR:CB_CSTR + 896] = ((xx - 384 - ki) >= 1).astype(np.float32)
    return cb.astype(ml_dtypes.bfloat16)


def build_cf(inp):
    cf = np.zeros((128, NF), np.float32)
    cf[:, CF_ID:CF_ID + 128] = np.eye(128)

    def cols(v):
        return np.ascontiguousarray(np.asarray(v, np.float32).reshape(16, 128).T)

    for l in range(NL):
        cf[:, CF_GMIX + l * 16:CF_GMIX + (l + 1) * 16] = cols(inp["norm_mix_g"][l])
        cf[:, CF_GFFN + l * 16:CF_GFFN + (l + 1) * 16] = cols(inp["norm_ffn_g"][l])
        cf[:, CF_GSB + l] = np.asarray(inp["g_sb_out"][l], np.float32)
        cf[:, CF_GDF + l] = np.asarray(inp["g_diff_out"][l], np.float32)
        cf[:, CF_GDL + l] = np.asarray(inp["g_dil_out"][l], np.float32)
        cf[:, CF_LNA + l] = math.log(1.0 - lambda_init(l))
        for i, nm in enumerate(("lambda_q1", "lambda_k1", "lambda_q2", "lambda_k2")):
            o = CF_LAM + (l * 4 + i) * 64
            cf[:, o:o + 64] = np.asarray(inp[nm][l], np.float32)[None, :]
    cf[:, CF_GFIN:CF_GFIN + 16] = cols(inp["norm_final_g"])
    invd = np.zeros(128, np.float32)
    sgd = np.zeros(128, np.float32)
    fr = (np.float32(THETA) ** (-np.arange(16, dtype=np.float32) / np.float32(16))).astype(np.float32)
    for d in range(32):
        invd[d] = fr[d % 16]
        sgd[d] = -1.0 if d < 16 else 1.0
    invf = np.zeros(128, np.float32)
    sgf = np.zeros(128, np.float32)
    fr8 = (np.float32(THETA) ** (-np.arange(8, dtype=np.float32) / np.float32(8))).astype(np.float32)
    for base in (0, 64):
        for d in range(16):
            invf[base + d] = fr8[d % 8]
            sgf[base + d] = -1.0 if d < 8 else 1.0
    cf[:, CF_INVF] = invd
    cf[:, CF_INVF + 1] = invf
    cf[:, CF_SGN] = sgd
    cf[:, CF_SGN + 1] = sgf
    cf[:, CF_EPS] = EPS
    cf[:, CF_ONE] = 1.0
    return cf


def build_program(stop_after=None, dbg=None):
    nc = bass.Bass("TRN2", target_bir_lowering=False)
    P = Prog()
    dbg = dbg or []
    dbg_out = {}

    def dram(name, shape, dt, kind=None):
        if kind:
            return nc.dram_tensor(name, shape, dt, kind=kind)
        return nc.dram_tensor(name, shape, dt)

    x_in = dram("x", [T, D], F32, "ExternalInput")
    pos_in = dram("pos", [1, S], I32, "ExternalInput")
    cb_in = dram("cb", [128, NB], BF16, "ExternalInput")
    cf_in = dram("cf", [128, NF], F32, "ExternalInput")
    w_in = dram("w_in", [NL, D, 1536], F32, "ExternalInput")
    w_out = dram("w_out", [NL, D, D], F32, "ExternalInput")
    need_ffn = stop_after is None or stop_after.startswith("ffn") or stop_after.startswith("h1") or stop_after.startswith("att1") or stop_after.startswith("oproj1")
    if need_ffn:
        w_gate = dram("w_gate", [NL, D, DFF], F32, "ExternalInput")
        w_up = dram("w_up", [NL, D, DFF], F32, "ExternalInput")
        w_down = dram("w_down", [NL, DFF, D], F32, "ExternalInput")
    out_t = dram("out", [T, D], F32, "ExternalOutput")

    hbuf_in = [dram("hbuf_in%d" % l, [NC, 128, T], BF16) for l in range(NL)]
    hbuf_all = [dram("hbuf_all%d" % l, [NC, 512, T], BF16) for l in range(NL)]
    mbuf_in = [dram("mbuf_in%d" % l, [4, 4, 128, T], BF16) for l in range(NL)]
    mbuf_all = [dram("mbuf_all%d" % l, [4, 4, 512, T], BF16) for l in range(NL)]
    ropeC = [dram("ropeC%d" % i, [128, S], F32) for i in range(2)]
    ropeS = [dram("ropeS%d" % i, [128, S], F32) for i in range(2)]
    tk_hin = [[Tk("hin") for _ in range(NC)] for _ in range(NL)]
    tk_hall = [Tk("hall", multi=True) for _ in range(NL)]
    tk_min = [Tk("min", multi=True) for _ in range(NL)]
    tk_mall = [Tk("mall", multi=True) for _ in range(NL)]
    tk_rope = Tk("rope", multi=True)
    tk_out = Tk("out", multi=True)
    tk_dbg = Tk("dbg", multi=True)

    stack = ExitStack()
    ARENA_BYTES = int(os.environ.get("ARENA_KB", "200")) * 1024
    arena = stack.enter_context(nc.sbuf_tensor("arena", [128, ARENA_BYTES // 2], BF16))
    psum = stack.enter_context(nc.psum_tensor("psum", [128, 8 * 512], F32))
    banks = [psum[:, i * 512:(i + 1) * 512] for i in range(8)]
    bank_tk = [Tk("bank%d" % i) for i in range(8)]

    class Alloc:
        def __init__(self):
            self.base = 0
            self.top = 0

        def take(self, nbytes):
            nbytes = (nbytes + 63) // 64 * 64
            off = self.top
            self.top += nbytes
            assert self.top <= ARENA_BYTES, ("SBUF overflow", self.top)
            return off

        def persist(self):
            self.base = self.top

        def reset(self):
            self.top = self.base

    A = Alloc()

    def sb(shape, dt):
        n = int(np.prod(shape))
        esz = 2 if dt == BF16 else 4
        off = A.take(n * esz)
        v = arena[:, off // 2: off // 2 + n * esz // 2]
        if dt != BF16:
            v = v.bitcast(dt)
        if len(shape) == 2:
            v = v.rearrange("p (a b) -> p a b", a=shape[0])
        elif len(shape) == 3:
            v = v.rearrange("p (a b c) -> p a b c", a=shape[0], b=shape[1])
        return v

    bank_rr = [0]

    def nb(lo=0, hi=8):
        i = lo + bank_rr[0] % (hi - lo)
        bank_rr[0] += 1
        return i

    def phase_end():
        P.barrier()
        A.reset()

    def dump(name, ap_sb, shape, dt, reads):
        if name not in dbg:
            return
        t = dram("dbg_" + name, shape, dt, "ExternalOutput")
        dbg_out[name] = t
        P.op("sp", lambda e, t=t, a=ap_sb: e.dma_start(out=t.ap(), in_=a), reads=reads, writes=[tk_dbg], dma=True)

    cb = sb([NB], BF16)
    cf = sb([NF], F32)
    xT = sb([NC, T], F32)
    neglam = sb([NL], F32)
    tk_cb, tk_cf, tk_neglam = Tk("cb"), Tk("cf"), Tk("neglam")
    tk_x = [[Tk("xT") for _ in range(2)] for _ in range(NC)]
    A.persist()

    ident_f = cf[:, CF_ID:CF_ID + 128]
    ident_b = cb[:, CB_ID:CB_ID + 128]
    ones_b = cb[:, CB_ONES:CB_ONES + 128]

    def cfc(col):
        return cf[:, col:col + 1]

    try:
        P.op("sp", lambda e: e.dma_start(out=cb, in_=cb_in.ap()), writes=[tk_cb], dma=True)
        P.op("sp", lambda e: e.dma_start(out=cf, in_=cf_in.ap()), writes=[tk_cf], dma=True)

        lt = sb([4, 64], F32)
        ls = sb([8], F32)
        tk_lt, tk_ls = Tk(), Tk()
        for l in range(NL):
            for i in range(2):
                a0 = CF_LAM + (l * 4 + 2 * i) * 64
                P.op("dve", lambda e, i=i, a0=a0: e.tensor_tensor(out=lt[:, i, :], in0=cf[:, a0:a0 + 64],
                                                                   in1=cf[:, a0 + 64:a0 + 128], op=ALU.mult),
                     reads=[tk_cf], writes=[tk_lt])
                P.op("dve", lambda e, i=i: e.reduce_sum(out=ls[:, i:i + 1], in_=lt[:, i, :], axis=AX.X),
                     reads=[tk_lt], writes=[tk_ls])
                P.op("act", lambda e, i=i: e.activation(out=ls[:, 2 + i:3 + i], in_=ls[:, i:i + 1], func=AF.Exp),
                     reads=[tk_ls], writes=[tk_ls])
            P.op("dve", lambda e: e.tensor_tensor(out=ls[:, 4:5], in0=ls[:, 2:3], in1=ls[:, 3:4], op=ALU.subtract),
                 reads=[tk_ls], writes=[tk_ls])
            P.op("dve", lambda e, l=l: e.tensor_scalar(out=neglam[:, l:l + 1], in0=ls[:, 4:5], scalar1=-1.0,
                                                       scalar2=-lambda_init(l), op0=ALU.mult, op1=ALU.add),
                 reads=[tk_ls], writes=[tk_neglam])

        xs = [sb([D], F32) for _ in range(8)]
        tk_xs = [Tk() for _ in range(8)]
        for i in range(8):
            P.op("sp", lambda e, i=i: e.dma_start(out=xs[i], in_=x_in.ap()[i * 128:(i + 1) * 128, :]),
                 writes=[tk_xs[i]], dma=True)
        ev = 0
        for c in range(NC):
            for tg in range(2):
                b = nb()

                def tr(e, c=c, tg=tg, b=b):
                    ins = None
                    for i in range(4):
                        ins = e.transpose(out=banks[b][:, i * 128:(i + 1) * 128],
                                          in_=xs[tg * 4 + i][:, c * 128:(c + 1) * 128], identity=ident_f)
                    return ins
                P.op("pe", tr, reads=[tk_xs[tg * 4 + i] for i in range(4)] + [tk_cf], writes=[bank_tk[b]])
                dst = xT[:, c, tg * 512:(tg + 1) * 512]
                if ev % 2 == 0:
                    P.op("act", lambda e, b=b, dst=dst: e.activation(out=dst, in_=banks[b][:], func=AF.Copy),
                         reads=[bank_tk[b]], writes=[tk_x[c][tg]])
                else:
                    P.op("dve", lambda e, b=b, dst=dst: e.tensor_copy(out=dst, in_=banks[b][:]),
                         reads=[bank_tk[b]], writes=[tk_x[c][tg]])
                ev += 1
        phase_end()
        if stop_after == 'p0a':
            raise StopBuild()

        def rope_phase():
            posi = sb([S], I32)
            posf = sb([S], F32)
            tk_posi, tk_posf = Tk(), Tk()
            P.op("sp", lambda e: e.dma_start(out=posi, in_=pos_in.ap().partition_broadcast(128)), writes=[tk_posi], dma=True)
            P.op("dve", lambda e: e.tensor_copy(out=posf, in_=posi), reads=[tk_posi], writes=[tk_posf])
            RW = 1024
            rt = {k: [sb([RW], I32 if k == "ki" else F32) for _ in range(2)] for k in ("a", "y", "ki", "kf", "r", "m", "sn")}
            rtk = {k: [Tk() for _ in range(2)] for k in rt}
            it = 0
            for ty in range(2):
                for kind in range(2):
                    for ch in range(S // RW):
                        s_ = it % 2
                        it += 1
                        cs = slice(ch * RW, (ch + 1) * RW)
                        a, y, ki, kf, r, m, sn = (rt[k][s_] for k in ("a", "y", "ki", "kf", "r", "m", "sn"))
                        ta, ty_, tki, tkf, tr_, tm, tsn = (rtk[k][s_] for k in ("a", "y", "ki", "kf", "r", "m", "sn"))
                        off = PI / 2 if kind == 0 else 0.0
                        P.op("dve", lambda e, a=a, cs=cs, ty=ty, off=off: e.tensor_scalar(
                            out=a, in0=posf[:, cs], scalar1=cfc(CF_INVF + ty), scalar2=off, op0=ALU.mult, op1=ALU.add),
                            reads=[tk_posf, tk_cf], writes=[ta])
                        P.op("dve", lambda e, a=a, y=y: e.tensor_scalar(out=y, in0=a, scalar1=1.0 / (2 * PI), scalar2=None,
                                                                         op0=ALU.mult), reads=[ta], writes=[ty_])
                        P.op("dve", lambda e, y=y, ki=ki: e.tensor_copy(out=ki, in_=y), reads=[ty_], writes=[tki])
                        P.op("dve", lambda e, kf=kf, ki=ki: e.tensor_copy(out=kf, in_=ki), reads=[tki], writes=[tkf])
                        P.op("dve", lambda e, kf=kf, a=a, r=r: e.scalar_tensor_tensor(
                            out=r, in0=kf, scalar=-2 * PI, in1=a, op0=ALU.mult, op1=ALU.add), reads=[tkf, ta], writes=[tr_])
                        P.op("dve", lambda e, r=r, m=m: e.tensor_scalar(out=m, in0=r, scalar1=PI, scalar2=2 * PI,
                                                                         op0=ALU.is_gt, op1=ALU.mult), reads=[tr_], writes=[tm])
                        P.op("dve", lambda e, r=r, m=m: e.tensor_tensor(out=r, in0=r, in1=m, op=ALU.subtract),
                             reads=[tm, tr_], writes=[tr_])
                        P.op("dve", lambda e, r=r: e.tensor_scalar(out=r, in0=r, scalar1=PI, scalar2=-PI,
                                                                    op0=ALU.min, op1=ALU.max), reads=[tr_], writes=[tr_])
                        P.op("act", lambda e, r=r, sn=sn: e.activation(out=sn, in_=r, func=AF.Sin), reads=[tr_], writes=[tsn])
                        if kind == 1:
                            P.op("dve", lambda e, sn=sn, ty=ty: e.tensor_scalar(out=sn, in0=sn, scalar1=cfc(CF_SGN + ty),
                                                                                 scalar2=None, op0=ALU.mult),
                                 reads=[tsn, tk_cf], writes=[tsn])
                        dst = (ropeC if kind == 0 else ropeS)[ty]
                        P.op("sp", lambda e, dst=dst, cs=cs, sn=sn: e.dma_start(out=dst.ap()[:, cs], in_=sn),
                             reads=[tsn], writes=[tk_rope], dma=True)
            phase_end()

        def norm_phase(gcol, hT, tk_h, out_dt_is_bf16=True):
            sq = [sb([T], BF16) for _ in range(2)]
            tk_sq = [Tk() for _ in range(2)]
            lnv = sb([T], F32)
            rstd = sb([T], F32)
            tk_lnv, tk_rstd = Tk(), Tk()
            b0, b1 = nb(), nb()
            bs = (b0, b1)
            for c in range(NC):
                s_ = c % 2
                P.op("act", lambda e, c=c, s_=s_: e.activation(out=sq[s_], in_=xT[:, c, :], func=AF.Square),
                     reads=[tk_x[c][0], tk_x[c][1]], writes=[tk_sq[s_]])

                def mm(e, c=c, s_=s_):
                    ins = None
                    for th in range(2):
                        ins = e.matmul(banks[bs[th]][:], lhsT=ones_b, rhs=sq[s_][:, th * 512:(th + 1) * 512],
                                       start=(c == 0), stop=(c == NC - 1))
                    return ins
                P.op("pe", mm, reads=[tk_sq[s_], tk_cb], writes=[bank_tk[b0], bank_tk[b1]])
            for th in range(2):
                P.op("act", lambda e, th=th: e.activation(out=lnv[:, th * 512:(th + 1) * 512], in_=banks[bs[th]][:],
                                                         func=AF.Ln, bias=cfc(CF_EPS), scale=1.0 / D),
                     reads=[bank_tk[bs[th]], tk_cf], writes=[tk_lnv])
            P.op("act", lambda e: e.activation(out=rstd, in_=lnv, func=AF.Exp, scale=-0.5), reads=[tk_lnv], writes=[tk_rstd])
            for c in range(NC):
                P.op("dve", lambda e, c=c: e.scalar_tensor_tensor(out=hT[:, c, :], in0=xT[:, c, :], scalar=cfc(gcol + c),
                                                                  in1=rstd, op0=ALU.mult, op1=ALU.mult),
                     reads=[tk_x[c][0], tk_x[c][1], tk_rstd, tk_cf], writes=[tk_h[c]])

        def head_norm(o, tk_o, gcol, lnbias_col, dst_dram_ap, tk_dst, ring, msb=4):
            sq, tk_sq, ln_, tk_ln, rs, tk_rs, mo, tk_mo = ring
            P.op("act", lambda e: e.activation(out=sq, in_=o, func=AF.Square), reads=[tk_o], writes=[tk_sq])
            b = msb
            P.op("pe", lambda e, b=b: e.matmul(banks[b][:], lhsT=ones_b, rhs=sq, start=True, stop=True),
                 reads=[tk_sq, tk_cb], writes=[bank_tk[b]])
            P.op("act", lambda e, b=b: e.activation(out=ln_, in_=banks[b][:], func=AF.Ln, bias=cfc(CF_EPS), scale=1.0 / 128),
                 reads=[bank_tk[b], tk_cf], writes=[tk_ln])
            if lnbias_col is None:
                P.op("act", lambda e: e.activation(out=rs, in_=ln_, func=AF.Exp, scale=-0.5), reads=[tk_ln], writes=[tk_rs])
            else:
                P.op("act", lambda e: e.activation(out=rs, in_=ln_, func=AF.Exp, scale=-0.5, bias=cfc(lnbias_col)),
                     reads=[tk_ln, tk_cf], writes=[tk_rs])
            P.op("dve", lambda e: e.scalar_tensor_tensor(out=mo, in0=o, scalar=cfc(gcol), in1=rs, op0=ALU.mult, op1=ALU.mult),
                 reads=[tk_o, tk_rs, tk_cf], writes=[tk_mo])
            P.op("sp", lambda e: e.dma_start(out=dst_dram_ap, in_=mo), reads=[tk_mo], writes=[tk_dst], dma=True)

        for l in range(NL):
            saved_base = A.base
            wq_ring = [sb([3, NC, 128], BF16) for _ in range(2)]
            tk_w_ring = [[Tk() for _ in range(3)] for _ in range(2)]
            A.persist()

            def load_wq(s_, l=l, wq_ring=wq_ring, tk_w_ring=tk_w_ring):
                wcols_ = (2 * s_ * 128, (2 * s_ + 1) * 128, 1024 + s_ * 128)
                for i in range(3):
                    P.op("pool", lambda e, i=i, wcols_=wcols_, s_=s_: e.dma_start(
                        out=wq_ring[s_ % 2][:, i], in_=w_in.ap()[l, :, wcols_[i]:wcols_[i] + 128].rearrange("(c p) n -> p c n", p=128)),
                        writes=[tk_w_ring[s_ % 2][i]], dma=True)
            load_wq(0)
            hT = sb([NC, T], BF16)
            tk_h = [Tk() for _ in range(NC)]
            norm_phase(CF_GMIX + l * 16, hT, tk_h)
            for c in range(NC):
                P.op("sp", lambda e, l=l, hT=hT, c=c: e.dma_start(out=hbuf_in[l].ap()[c], in_=hT[:, c, :]),
                     reads=[tk_h[c]], writes=[tk_hin[l][c]], dma=True, semkey=("hin", c))
                P.op("pool", lambda e, l=l, c=c: e.collective_compute("AllGather", ALU.bypass, replica_groups=[[0, 1, 2, 3], [4, 5, 6, 7]],
                                                                     ins=[hbuf_in[l].ap()[c]], outs=[hbuf_all[l].ap()[c]]),
                     reads=[tk_hin[l][c]], writes=[tk_hall[l]], dma=True, inc=1, bar=False)
            if l == 0:
                dump("h0", hT, [128, NC, T], BF16, tk_h)
            phase_end()
            if l == 0:
                rope_phase()
            if stop_after == "h%d" % l:
                break

            for s in range(4):
                kind = ("sb", "diff", "dil", "dil")[s]
                rope_ty = {"sb": None, "diff": 1, "dil": 0}[kind]
                qscale = 0.125 if kind == "diff" else 128 ** -0.5
                perm = None if rope_ty is None else cb[:, (CB_PDIL if rope_ty == 0 else CB_PDIFF):][:, 0:128]

                wq = wq_ring[s % 2]
                tk_w = tk_w_ring[s % 2]
                qT = sb([S], BF16)
                kT = sb([S], BF16)
                vv = sb([32, 128], BF16)
                tk_q = [Tk() for _ in range(8)]
                tk_k = [Tk() for _ in range(8)]
                tk_v = [Tk() for _ in range(8)]
                mark_ = A.top
                ht = [sb([NC, 512], BF16) for _ in range(2)]
                tk_ht = [Tk() for _ in range(2)]
                rC = [sb([512], F32) for _ in range(2)]
                rS = [sb([512], F32) for _ in range(2)]
                tk_rC = [Tk() for _ in range(2)]
                tk_rS = [Tk() for _ in range(2)]
                raw = [sb([512], BF16) for _ in range(2)]
                tk_raw = [Tk() for _ in range(2)]
                t1 = [sb([512], F32) for _ in range(2)]
                t2 = [sb([512], F32) for _ in range(2)]
                tk_t1 = [Tk() for _ in range(2)]
                tk_t2 = [Tk() for _ in range(2)]

                ri = 0
                for tt in range(8):
                    hs = tt % 2
                    r_, off = tt // 2, (tt % 2) * 512
                    P.op("sp", lambda e, l=l, r_=r_, off=off, hs=hs, ht=ht: e.dma_start(
                        out=ht[hs], in_=hbuf_all[l].ap()[:, r_ * 128:(r_ + 1) * 128, off:off + 512].rearrange("c p t -> p c t")),
                        reads=[tk_hall[l]], writes=[tk_ht[hs]], dma=True)
                    if rope_ty is not None:
                        P.op("sp", lambda e, tt=tt, hs=hs, rC=rC, rope_ty=rope_ty: e.dma_start(
                            out=rC[hs], in_=ropeC[rope_ty].ap()[:, tt * 512:(tt + 1) * 512]),
                            reads=[tk_rope], writes=[tk_rC[hs]], dma=True)
                        P.op("sp", lambda e, tt=tt, hs=hs, rS=rS, rope_ty=rope_ty: e.dma_start(
                            out=rS[hs], in_=ropeS[rope_ty].ap()[:, tt * 512:(tt + 1) * 512]),
                            reads=[tk_rope], writes=[tk_rS[hs]], dma=True)
                    post = []
                    for qi, (dstT, tkd, sc) in enumerate(((qT, tk_q, qscale), (kT, tk_k, 1.0))):
                        b = nb()

                        def mm(e, b=b, qi=qi, hs=hs, wq=wq, ht=ht):
                            ins = None
                            for c in range(NC):
                                ins = e.matmul(banks[b][:], lhsT=wq[:, qi, c, :], rhs=ht[hs][:, c, :],
                                               start=(c == 0), stop=(c == NC - 1))
                            return ins
                        P.op("pe", mm, reads=[tk_w[qi], tk_ht[hs]], writes=[bank_tk[b]])
                        dst = dstT[:, tt * 512:(tt + 1) * 512]
                        if rope_ty is None:
                            P.op("act", lambda e, b=b, dst=dst, sc=sc: e.activation(out=dst, in_=banks[b][:], func=AF.Copy, scale=sc),
                                 reads=[bank_tk[b]], writes=[tkd[tt]])
                        else:
                            rs_ = ri % 2
                            ri += 1
                            P.op("act", lambda e, b=b, rs_=rs_, sc=sc, raw=raw: e.activation(out=raw[rs_], in_=banks[b][:], func=AF.Copy, scale=sc),
                                 reads=[bank_tk[b]], writes=[tk_raw[rs_]])
                            def post_fn(rs_=rs_, hs=hs, dst=dst, tkd=tkd, tt=tt, perm=perm, raw=raw, t1=t1, t2=t2, rS=rS, rC=rC,
                                        tk_raw=tk_raw, tk_t1=tk_t1, tk_t2=tk_t2, tk_rS=tk_rS, tk_rC=tk_rC):
                                b2 = nb()
                                P.op("pe", lambda e, b2=b2, rs_=rs_, perm=perm, raw=raw: e.matmul(banks[b2][:], lhsT=perm, rhs=raw[rs_], start=True, stop=True),
                                     reads=[tk_raw[rs_], tk_cb], writes=[bank_tk[b2]])
                                P.op("dve", lambda e, b2=b2, rs_=rs_, hs=hs, t1=t1, rS=rS: e.tensor_tensor(out=t1[rs_], in0=banks[b2][:], in1=rS[hs], op=ALU.mult),
                                     reads=[bank_tk[b2], tk_rS[hs]], writes=[tk_t1[rs_]])
                                P.op("pool", lambda e, rs_=rs_, hs=hs, t2=t2, raw=raw, rC=rC: e.tensor_tensor(out=t2[rs_], in0=raw[rs_], in1=rC[hs], op=ALU.mult),
                                     reads=[tk_raw[rs_], tk_rC[hs]], writes=[tk_t2[rs_]])
                                P.op("dve", lambda e, rs_=rs_, dst=dst, t1=t1, t2=t2: e.tensor_tensor(out=dst, in0=t1[rs_], in1=t2[rs_], op=ALU.add),
                                     reads=[tk_t1[rs_], tk_t2[rs_]], writes=[tkd[tt]])
                            post.append(post_fn)
                    b = nb()

                    def mmv(e, b=b, hs=hs, wq=wq, ht=ht):
                        ins = None
                        for sub in range(4):
                            for c in range(NC):
                                ins = e.matmul(banks[b][:, sub * 128:(sub + 1) * 128], lhsT=ht[hs][:, c, sub * 128:(sub + 1) * 128],
                                               rhs=wq[:, 2, c, :], start=(c == 0), stop=(c == NC - 1))
                        return ins
                    P.op("pe", mmv, reads=[tk_w[2], tk_ht[hs]], writes=[bank_tk[b]])
                    P.op("dve", lambda e, b=b, tt=tt, vv=vv: e.tensor_copy(out=vv[:, tt * 4:(tt + 1) * 4, :],
                                                                            in_=banks[b][:].rearrange("p (a b) -> p a b", a=4)),
                         reads=[bank_tk[b]], writes=[tk_v[tt]])
                    for pf_ in post:
                        pf_()
                if l == 0 and s in (0, 1, 2):
                    dump("q%d" % s, qT, [128, S], BF16, tk_q)
                    dump("k%d" % s, kT, [128, S], BF16, tk_k)
                    dump("v%d" % s, vv, [128, 32, 128], BF16, tk_v)

                P.barrier()
                A.top = mark_
                if s + 1 < 4:
                    load_wq(s + 1)
                NE = 4
                NE2 = 3
                if kind == "sb":
                    E = [sb([512], BF16) for _ in range(NE)]
                    Em = [sb([512], BF16) for _ in range(NE)]
                    tk_E = [Tk() for _ in range(NE)]
                    tk_Em = [Tk() for _ in range(NE)]
                else:
                    E2 = [sb([1024], BF16) for _ in range(NE2)]
                    Em2 = [sb([1024], BF16) for _ in range(NE2)]
                    tk_E2 = [Tk() for _ in range(NE2)]
                    tk_Em2 = [[Tk(), Tk()] for _ in range(NE2)]
                pcnt = [0]
                oacc = [sb([512], F32) for _ in range(3)]
                tk_oacc = [Tk() for _ in range(3)]
                rl = sb([512], F32)
                tk_rl = Tk()
                hn_ring = []
                for _ in range(2):
                    hn_ring.append((sb([512], BF16), Tk(), sb([512], F32), Tk(), sb([512], F32), Tk(), sb([512], BF16), Tk()))
                if kind == "sb":
                    ef = [sb([512], F32) for _ in range(2)]
                    tk_ef = [Tk() for _ in range(2)]
                    sp_ = [sb([512], BF16) for _ in range(3)]
                    tk_sp = [Tk() for _ in range(3)]
                    spm = [sb([512], BF16) for _ in range(3)]
                    tk_spm = [Tk() for _ in range(3)]
                    Ts = [sb([512], BF16) for _ in range(4)]
                    tk_Ts = [Tk() for _ in range(4)]
                OB, LB = 6, 7
                ecnt = [0]
                hcnt = [0]

                pending = []
                olcnt = [0]
                lnl = [sb([512], F32) for _ in range(2)]
                tk_lnl = [Tk() for _ in range(2)]
                oraw = [sb([512], F32) for _ in range(2)]
                tk_oraw = [Tk() for _ in range(2)]

                def flush_pending():
                    while pending:
                        pending.pop(0)()

                def softmax_pass(j, parts, ktiles, maskfn, o_dst, tk_odst, after=None):
                    p0, p1 = parts
                    qsl = qT[p0:p1, j * 512:(j + 1) * 512]
                    npair = len(ktiles) // 2
                    assert len(ktiles) % 2 == 0
                    sb_of = {}
                    OB, LB = 6, 7
                    ek = olcnt[0] % 2
                    olcnt[0] += 1

                    def issue_s(pi):
                        pb = (pcnt[0] % 2) * 2
                        pcnt[0] += 1
                        sb_of[pi] = pb
                        i0, i1 = ktiles[2 * pi], ktiles[2 * pi + 1]

                        def mm(e, pb=pb, i0=i0, i1=i1):
                            e.matmul(banks[pb][:], lhsT=kT[p0:p1, i0 * 128:(i0 + 1) * 128], rhs=qsl, start=True, stop=True)
                            return e.matmul(banks[pb + 1][:], lhsT=kT[p0:p1, i1 * 128:(i1 + 1) * 128], rhs=qsl, start=True, stop=True)
                        P.op("pe", mm, reads=[tk_k[i0 // 4], tk_k[i1 // 4], tk_q[j]], writes=[bank_tk[pb], bank_tk[pb + 1]])
                    issue_s(0)
                    for pi in range(npair):
                        if pi + 1 < npair:
                            issue_s(pi + 1)
                        if pi == 1:
                            flush_pending()
                        pb = sb_of.pop(pi)
                        es = ecnt[0] % NE2
                        ecnt[0] += 1
                        P.op("act", lambda e, pb=pb, es=es: e.activation(out=E2[es], in_=psum[:, pb * 512:(pb + 2) * 512], func=AF.Exp),
                             reads=[bank_tk[pb], bank_tk[pb + 1]], writes=[tk_E2[es]])
                        srcs = []
                        for h_ in range(2):
                            i = ktiles[2 * pi + h_]
                            hsl = slice(h_ * 512, (h_ + 1) * 512)
                            mk = maskfn(i)
                            if mk is not None:
                                P.op("dve", lambda e, es=es, mk=mk, hsl=hsl: e.tensor_tensor(out=Em2[es][:, hsl], in0=E2[es][:, hsl], in1=mk, op=ALU.mult),
                                     reads=[tk_E2[es], tk_cb], writes=[tk_Em2[es][h_]])
                                srcs.append((Em2[es][:, hsl], tk_Em2[es][h_], i))
                            else:
                                srcs.append((E2[es][:, hsl], tk_E2[es], i))

                        def mmo(e, srcs=srcs, pi=pi, OB=OB, LB=LB):
                            first, last = (pi == 0), (pi == npair - 1)
                            (s0, _, i0), (s1, _, i1) = srcs
                            e.matmul(banks[OB][:], lhsT=vv[:, i0, :], rhs=s0, start=first, stop=False)
                            e.matmul(banks[OB][:], lhsT=vv[:, i1, :], rhs=s1, start=False, stop=last)
                            e.matmul(banks[LB][:], lhsT=ones_b, rhs=s0, start=first, stop=False)
                            return e.matmul(banks[LB][:], lhsT=ones_b, rhs=s1, start=False, stop=last)
                        P.op("pe", mmo, reads=[srcs[0][1], srcs[1][1], tk_v[srcs[0][2] // 4], tk_v[srcs[1][2] // 4], tk_cb],
                             writes=[bank_tk[OB], bank_tk[LB]])
                    flush_pending()
                    P.op("act", lambda e, ek=ek: e.activation(out=lnl[ek], in_=banks[LB][:], func=AF.Ln), reads=[bank_tk[LB]], writes=[tk_lnl[ek]])
                    P.op("dve", lambda e, ek=ek: e.tensor_copy(out=oraw[ek], in_=banks[OB][:]), reads=[bank_tk[OB]], writes=[tk_oraw[ek]])

                    def epilogue(ek=ek):
                        P.op("act", lambda e: e.activation(out=rl, in_=lnl[ek], func=AF.Exp, scale=-1.0), reads=[tk_lnl[ek]], writes=[tk_rl])
                        P.op("dve", lambda e: e.tensor_tensor(out=o_dst, in0=oraw[ek], in1=rl, op=ALU.mult),
                             reads=[tk_oraw[ek], tk_rl], writes=[tk_odst])
                        if after is not None:
                            after()
                    pending.append(epilogue)

                def sb_pass(j, o_dst, tk_odst):
                    tiles = list(range(4 * j + 3, -1, -1))
                    n = len(tiles)
                    cstr = cb[:, CB_CSTR:CB_CSTR + 896]
                    st1 = {}
                    st2 = {}

                    def mask_of(i):
                        if i >= 4 * j:
                            Dq = 512 * j - 128 * i
                            return cstr[:, Dq + 384:Dq + 384 + 512]
                        return None

                    def stage1(idx):
                        i = tiles[idx]
                        qsl = qT[:, j * 512:(j + 1) * 512]
                        b = nb(0, 5)
                        P.op("pe", lambda e, b=b, i=i: e.matmul(banks[b][:], lhsT=kT[:, i * 128:(i + 1) * 128], rhs=qsl, start=True, stop=True),
                             reads=[tk_k[i // 4], tk_q[j]], writes=[bank_tk[b]])
                        fs = idx % 2
                        P.op("act", lambda e, b=b, fs=fs: e.activation(out=ef[fs], in_=banks[b][:], func=AF.Exp),
                             reads=[bank_tk[b]], writes=[tk_ef[fs]])
                        ss = idx % 3
                        P.op("act", lambda e, fs=fs, ss=ss: e.activation(out=sp_[ss], in_=ef[fs], func=AF.Ln, bias=cfc(CF_ONE)),
                             reads=[tk_ef[fs], tk_cf], writes=[tk_sp[ss]])
                        mk = mask_of(i)
                        if mk is not None:
                            P.op("dve", lambda e, ss=ss, mk=mk: e.tensor_tensor(out=spm[ss], in0=sp_[ss], in1=mk, op=ALU.mult),
                                 reads=[tk_sp[ss], tk_cb], writes=[tk_spm[ss]])
                            cur, tkc = spm[ss], tk_spm[ss]
                        else:
                            cur, tkc = sp_[ss], tk_sp[ss]
                        tprev = idx % 4
                        tnew = (idx + 1) % 4
                        if idx + 1 < n:
                            if idx == 0:
                                P.op("pool", lambda e, cur=cur, tnew=tnew: e.tensor_copy(out=Ts[tnew], in_=cur),
                                     reads=[tkc], writes=[tk_Ts[tnew]])
                            else:
                                P.op("pool", lambda e, cur=cur, tnew=tnew, tprev=tprev: e.tensor_tensor(out=Ts[tnew], in0=Ts[tprev], in1=cur, op=ALU.add),
                                     reads=[tkc, tk_Ts[tprev]], writes=[tk_Ts[tnew]])
                        st1[idx] = (cur, tkc, tprev)

                    def stage2(idx):
                        i = tiles[idx]
                        cur, tkc, tprev = st1.pop(idx)
                        qsl = qT[:, j * 512:(j + 1) * 512]
                        b = nb(0, 5)

                        def mml(e, b=b, i=i, cur=cur, tprev=tprev, idx=idx):
                            e.matmul(banks[b][:], lhsT=kT[:, i * 128:(i + 1) * 128], rhs=qsl, start=True, stop=False)
                            ins = e.matmul(banks[b][:], lhsT=cb[:, CB_NUINC:CB_NUINC + 128], rhs=cur, start=False, stop=(idx == 0))
                            if idx > 0:
                                ins = e.matmul(banks[b][:], lhsT=cb[:, CB_NONES:CB_NONES + 128], rhs=Ts[tprev], start=False, stop=True)
                            return ins
                        rd = [tk_k[i // 4], tk_q[j], tkc, tk_cb] + ([tk_Ts[tprev]] if idx > 0 else [])
                        P.op("pe", mml, reads=rd, writes=[bank_tk[b]])
                        es = ecnt[0] % NE
                        ecnt[0] += 1
                        P.op("act", lambda e, b=b, es=es: e.activation(out=E[es], in_=banks[b][:], func=AF.Exp),
                             reads=[bank_tk[b]], writes=[tk_E[es]])
                        mk = mask_of(i)
                        if mk is not None:
                            P.op("dve", lambda e, es=es, mk=mk: e.tensor_tensor(out=Em[es], in0=E[es], in1=mk, op=ALU.mult),
                                 reads=[tk_E[es], tk_cb], writes=[tk_Em[es]])
                            src, tks = Em[es], tk_Em[es]
                        else:
                            src, tks = E[es], tk_E[es]
                        st2[idx] = (src, tks)

                    def stage2b(idx):
                        i = tiles[idx]
                        src, tks = st2.pop(idx)
                        P.op("pe", lambda e, i=i, src=src, idx=idx: e.matmul(banks[OB][:], lhsT=vv[:, i, :], rhs=src, start=(idx == 0), stop=(idx == n - 1)),
                             reads=[tks, tk_v[i // 4]], writes=[bank_tk[OB]])

                    stage1(0)
                    if n > 1:
                        stage1(1)
                    for idx in range(n):
                        stage2(idx)
                        if idx + 2 < n:
                            stage1(idx + 2)
                        if idx >= 1:
                            stage2b(idx - 1)
                    stage2b(n - 1)
                    P.op("act", lambda e: e.activation(out=o_dst, in_=banks[OB][:], func=AF.Copy), reads=[bank_tk[OB]], writes=[tk_odst])

                cinc = cb[:, CB_CINC:CB_CINC + 896]
                toep = cb[:, CB_TOEP:CB_TOEP + TOEP_W]
                def quarter_collective(j, l=l, s=s):
                    if j % 2 == 1:
                        qt = j // 2
                        P.op("pool", lambda e, l=l, qt=qt, s=s: e.collective_compute("AllGather", ALU.bypass, replica_groups=[[0, 1, 2, 3], [4, 5, 6, 7]],
                                                                                    ins=[mbuf_in[l].ap()[qt, s]], outs=[mbuf_all[l].ap()[qt, s]]),
                             reads=[tk_min[l]], writes=[tk_mall[l]], dma=True, inc=1, bar=False)

                for j in range(8):
                    dst = mbuf_in[l].ap()[j // 2, s, :, (j % 2) * 512:(j % 2) * 512 + 512]
                    ring = hn_ring[hcnt[0] % 2]
                    hcnt[0] += 1
                    if kind == "sb":
                        sb_pass(j, oacc[0], tk_oacc[0])
                        head_norm(oacc[0], tk_oacc[0], CF_GSB + l, None, dst, tk_min[l], ring)
                        quarter_collective(j)
                    elif kind == "diff":
                        def mk_diff(i, j=j):
                            if i >= 4 * j:
                                Dq = 512 * j - 128 * i
                                return cinc[:, Dq + 384:Dq + 384 + 512]
                            return None
                        kt = list(range(0, 4 * j + 4))

                        def after_diff(j=j, dst=dst, ring=ring, l=l):
                            P.op("dve", lambda e, l=l: e.scalar_tensor_tensor(out=oacc[2], in0=oacc[1], scalar=neglam[:, l:l + 1], in1=oacc[0],
                                                                             op0=ALU.mult, op1=ALU.add),
                                 reads=[tk_oacc[0], tk_oacc[1], tk_neglam], writes=[tk_oacc[2]])
                            head_norm(oacc[2], tk_oacc[2], CF_GDF + l, CF_LNA + l, dst, tk_min[l], ring, msb=4)
                            quarter_collective(j)
                        softmax_pass(j, (0, 64), kt, mk_diff, oacc[0], tk_oacc[0])
                        softmax_pass(j, (64, 128), kt, mk_diff, oacc[1], tk_oacc[1], after=after_diff)
                    else:
                        def mk_dil(i, j=j):
                            Dq = 512 * j - 128 * i
                            return toep[:, Dq + 384:Dq + 384 + 512]
                        kt = list(range(max(0, 4 * j - 16), 4 * j + 4))
                        oslot = j % 2

                        def after_dil(j=j, dst=dst, ring=ring, l=l, oslot=oslot):
                            head_norm(oacc[oslot], tk_oacc[oslot], CF_GDL + l, None, dst, tk_min[l], ring, msb=4)
                            quarter_collective(j)
                        softmax_pass(j, (0, 128), kt, mk_dil, oacc[oslot], tk_oacc[oslot], after=after_dil)
                if kind != "sb":
                    flush_pending()
                phase_end()
                if stop_after == "att%d_%d" % (l, s):
                    break
            if stop_after is not None and stop_after.startswith("att%d" % l):
                break

            A.base = saved_base
            A.top = saved_base
            mx = sb([NC, T], BF16)
            tk_mx = Tk()

            def ld_mx(e, l=l, mx=mx):
                me = e.partition_id() % 4
                return e.dma_start(out=mx, in_=mbuf_all[l].ap()[bass.ds(me, 1)].rearrange("o s (r p) t -> p (o s r) t", p=128))
            if l == 0:
                dump("mx0", mx, [128, NC, T], BF16, [tk_mx])
            NWO = 3
            wo = [sb([NC, 128], BF16) for _ in range(NWO)]
            tk_wo = [Tk() for _ in range(NWO)]

            def mxchunk(n):
                if n < 4:
                    return n
                if n < 8:
                    return 4 + (n - 4)
                m = n - 8
                return (2 + m % 2) * 4 + m // 2

            def ld_wo(dc):
                ws = dc % NWO
                P.op("pool", lambda e, dc=dc, ws=ws, l=l: e.dma_start(
                    out=wo[ws], in_=w_out.ap()[l, :, dc * 128:(dc + 1) * 128].rearrange("(n p) c -> p n c", p=128)),
                    writes=[tk_wo[ws]], dma=True)
            ld_wo(0)
            ld_wo(1)
            P.op("pool", ld_mx, reads=[tk_mall[l]], writes=[tk_mx], dma=True)
            for dc in range(NC):
                if dc + 2 < NC:
                    ld_wo(dc + 2)
                ws = dc % NWO
                for th in range(2):
                    b = nb()

                    def mm(e, b=b, ws=ws, th=th, mx=mx):
                        ins = None
                        for n_ in range(NC):
                            ins = e.matmul(banks[b][:], lhsT=wo[ws][:, n_, :], rhs=mx[:, mxchunk(n_), th * 512:(th + 1) * 512],
                                           start=(n_ == 0), stop=(n_ == NC - 1))
                        return ins
                    P.op("pe", mm, reads=[tk_wo[ws], tk_mx], writes=[bank_tk[b]])
                    xsl = xT[:, dc, th * 512:(th + 1) * 512]
                    P.op("dve", lambda e, b=b, xsl=xsl: e.tensor_tensor(out=xsl, in0=xsl, in1=banks[b][:], op=ALU.add),
                         reads=[bank_tk[b], tk_x[dc][th]], writes=[tk_x[dc][th]])
            if l == 0:
                dump("x1", xT, [128, NC, T], F32, [t for c in tk_x for t in c])
            phase_end()
            if stop_after == "oproj%d" % l:
                break

            h2 = sb([NC, T], BF16)
            tk_h2 = [Tk() for _ in range(NC)]
            norm_phase(CF_GFFN + l * 16, h2, tk_h2)
            NWG = 3
            wg = [sb([NC, 128], BF16) for _ in range(NWG)]
            wu = [sb([NC, 128], BF16) for _ in range(NWG)]
            tk_wg = [Tk() for _ in range(NWG)]
            tk_wu = [Tk() for _ in range(NWG)]
            wd = [sb([GFF, D], BF16) for _ in range(2)]
            tk_wd = [Tk() for _ in range(2)]
            actT = [sb([GFF, T], BF16) for _ in range(2)]
            tk_act = [[Tk() for _ in range(GFF)] for _ in range(2)]
            sg = [sb([512], F32) for _ in range(2)]
            tk_sg = [Tk() for _ in range(2)]

            def ld_gu(jc):
                ws = jc % NWG
                P.op("pool", lambda e, jc=jc, ws=ws, l=l: e.dma_start(
                    out=wg[ws], in_=w_gate.ap()[l, :, jc * 128:(jc + 1) * 128].rearrange("(n p) c -> p n c", p=128)),
                    writes=[tk_wg[ws]], dma=True)
                P.op("pool", lambda e, jc=jc, ws=ws, l=l: e.dma_start(
                    out=wu[ws], in_=w_up.ap()[l, :, jc * 128:(jc + 1) * 128].rearrange("(n p) c -> p n c", p=128)),
                    writes=[tk_wu[ws]], dma=True)

            def ld_wd(gi):
                ws = gi % 2
                P.op("pool", lambda e, gi=gi, ws=ws, l=l: e.dma_start(
                    out=wd[ws], in_=w_down.ap()[l, gi * GFF * 128:(gi + 1) * GFF * 128, :].rearrange("(a p) c -> p a c", p=128)),
                    writes=[tk_wd[ws]], dma=True)
            ld_gu(0)
            ld_gu(1)
            ld_wd(0)
            sgc = 0
            NG = NFF // GFF
            for gi in range(NG):
                as_ = gi % 2
                if gi + 1 < NG:
                    ld_wd(gi + 1)
                for jj in range(GFF):
                    jc = gi * GFF + jj
                    if jc + 2 < NFF:
                        ld_gu(jc + 2)
                    ws = jc % NWG
                    for th in range(2):
                        bg, bu = nb(), nb()

                        def mm(e, bg=bg, bu=bu, ws=ws, th=th, h2=h2):
                            ins = None
                            for c in range(NC):
                                ins = e.matmul(banks[bg][:], lhsT=wg[ws][:, c, :], rhs=h2[:, c, th * 512:(th + 1) * 512],
                                               start=(c == 0), stop=(c == NC - 1))
                            for c in range(NC):
                                ins = e.matmul(banks[bu][:], lhsT=wu[ws][:, c, :], rhs=h2[:, c, th * 512:(th + 1) * 512],
                                               start=(c == 0), stop=(c == NC - 1))
                            return ins
                        P.op("pe", mm, reads=[tk_wg[ws], tk_wu[ws]] + tk_h2, writes=[bank_tk[bg], bank_tk[bu]])
                        ss = sgc % 2
                        sgc += 1
                        P.op("act", lambda e, bg=bg, ss=ss: e.activation(out=sg[ss], in_=banks[bg][:], func=AF.Silu),
                             reads=[bank_tk[bg]], writes=[tk_sg[ss]])
                        adst = actT[as_][:, jj, th * 512:(th + 1) * 512]
                        P.op("dve", lambda e, bu=bu, ss=ss, adst=adst: e.tensor_tensor(out=adst, in0=sg[ss], in1=banks[bu][:], op=ALU.mult),
                             reads=[tk_sg[ss], bank_tk[bu]], writes=[tk_act[as_][jj]])
                wsd = gi % 2
                for dc in range(NC):
                    for th in range(2):
                        b = nb()

                        def mmd(e, b=b, dc=dc, th=th, as_=as_, wsd=wsd):
                            ins = None
                            for jj in range(GFF):
                                ins = e.matmul(banks[b][:], lhsT=wd[wsd][:, jj, dc * 128:(dc + 1) * 128],
                                               rhs=actT[as_][:, jj, th * 512:(th + 1) * 512], start=(jj == 0), stop=(jj == GFF - 1))
                            return ins
                        P.op("pe", mmd, reads=[tk_wd[wsd]] + tk_act[as_], writes=[bank_tk[b]])
                        xsl = xT[:, dc, th * 512:(th + 1) * 512]
                        P.op("dve", lambda e, b=b, xsl=xsl: e.tensor_tensor(out=xsl, in0=xsl, in1=banks[b][:], op=ALU.add),
                             reads=[bank_tk[b], tk_x[dc][th]], writes=[tk_x[dc][th]])
            if l == 0:
                dump("x2", xT, [128, NC, T], F32, [t for c in tk_x for t in c])
            phase_end()
            if stop_after == "ffn%d" % l:
                break

    except StopBuild:
        pass

    if stop_after is None:
        yT = sb([NC, T], F32)
        tk_y = [Tk() for _ in range(NC)]
        norm_phase(CF_GFIN, yT, tk_y)
        ot = [sb([D], F32) for _ in range(2)]
        tk_ot = [Tk() for _ in range(2)]
        ev = 0
        for ti in range(8):
            os_ = ti % 2
            for cg in range(4):
                b = nb()

                def tr(e, b=b, cg=cg, ti=ti):
                    ins = None
                    for i in range(4):
                        c = cg * 4 + i
                        ins = e.transpose(out=banks[b][:, i * 128:(i + 1) * 128], in_=yT[:, c, ti * 128:(ti + 1) * 128], identity=ident_f)
                    return ins
                P.op("pe", tr, reads=[tk_y[cg * 4 + i] for i in range(4)] + [tk_cf], writes=[bank_tk[b]])
                dsl = ot[os_][:, cg * 512:(cg + 1) * 512]
                if ev % 2 == 0:
                    P.op("act", lambda e, b=b, dsl=dsl: e.activation(out=dsl, in_=banks[b][:], func=AF.Copy),
                         reads=[bank_tk[b]], writes=[tk_ot[os_]])
                else:
                    P.op("dve", lambda e, b=b, dsl=dsl: e.tensor_copy(out=dsl, in_=banks[b][:]),
                         reads=[bank_tk[b]], writes=[tk_ot[os_]])
                ev += 1
            P.op("sp", lambda e, ti=ti, os_=os_: e.dma_start(out=out_t.ap()[ti * 128:(ti + 1) * 128, :], in_=ot[os_]),
                 reads=[tk_ot[os_]], writes=[tk_out], dma=True)
    else:
        zt = sb([D], F32)
        tkz = Tk()
        P.op("dve", lambda e: e.memset(zt, 0.0), writes=[tkz])
        for ti in range(8):
            P.op("sp", lambda e, ti=ti: e.dma_start(out=out_t.ap()[ti * 128:(ti + 1) * 128, :], in_=zt),
                 reads=[tkz], writes=[tk_out], dma=True)
    P.barrier()

    P.emit(nc, stack)
    stack.close()
    return nc, dbg_out


def make_in_maps(inp, need_ffn=True):
    x = np.asarray(inp["x"], np.float32)
    positions = np.asarray(inp["positions"], np.int32)
    w_in = np.asarray(inp["w_in"], np.float32)
    cbv = build_cb()
    cfv = build_cf(inp)
    w_out = np.ascontiguousarray(np.asarray(inp["w_out"], np.float32))
    w_gate = np.ascontiguousarray(np.asarray(inp["w_gate"], np.float32))
    w_up = np.ascontiguousarray(np.asarray(inp["w_up"], np.float32))
    w_down = np.ascontiguousarray(np.asarray(inp["w_down"], np.float32))
    maps = []
    for c in range(8):
        b, g = c // 4, c % 4
        qk_cols, v_cols = [], []
        bases = [(0, g), (1536, g), (3072, 2 * g), (3072, 2 * g + 1)]
        widths = [512, 512, 1024, 1024]
        for (base, h), wdt in zip(bases, widths):
            qk_cols.append(np.arange(base + h * 128, base + (h + 1) * 128))
            qk_cols.append(np.arange(base + wdt + h * 128, base + wdt + (h + 1) * 128))
            v_cols.append(np.arange(base + 2 * wdt + h * 128, base + 2 * wdt + (h + 1) * 128))
        cols = np.concatenate(qk_cols + v_cols)
        maps.append({
            "x": np.ascontiguousarray(x[b, g * T:(g + 1) * T, :]),
            "pos": np.ascontiguousarray(positions[b][None, :]),
            "cb": cbv,
            "cf": cfv,
            "w_in": np.ascontiguousarray(w_in[:, :, cols]),
            "w_out": w_out,
        })
        if need_ffn:
            maps[-1].update({"w_gate": w_gate, "w_up": w_up, "w_down": w_down})
    return maps


_CACHE = {}


def kernel(**inputs):
    if "nc" not in _CACHE:
        _CACHE["nc"] = build_program()[0]
    nc = _CACHE["nc"]
    maps = make_in_maps(inputs)
    res = run_bass_kernel_spmd(nc, maps, core_ids=list(range(8)))
    out = np.zeros((2, S, D), np.float32)
    for c in range(8):
        b, g = c // 4, c % 4
        out[b, g * T:(g + 1) * T, :] = np.asarray(res.results[c]["out"], np.float32)
    return out
```

```python
import math
import os
from contextlib import ExitStack

import numpy as np
import ml_dtypes

import concourse.bass as bass
import concourse.mybir as mybir
from concourse.bass_utils import run_bass_kernel_spmd

F32 = mybir.dt.float32
BF16 = mybir.dt.bfloat16
I32 = mybir.dt.int32
AF = mybir.ActivationFunctionType
ALU = mybir.AluOpType
AX = mybir.AxisListType

D = 2048
S = 4096
T = 1024
NL = 2
DFF = 5632
NFF = DFF // 128
NC = 16
EPS = 1e-6
THETA = 500000.0
PI = math.pi
GFF = 4

CB_ID, CB_ONES, CB_NUINC, CB_NONES, CB_PDIL, CB_PDIFF = 0, 128, 256, 384, 512, 640
CB_TOEP = 768
TOEP_W = 2944
CB_CINC = CB_TOEP + TOEP_W
CB_CSTR = CB_CINC + 896
NB = CB_CSTR + 896
CF_ID = 0
CF_GMIX = 128
CF_GFFN = 160
CF_GFIN = 192
CF_GSB = 208
CF_GDF = 210
CF_GDL = 212
CF_INVF = 214
CF_SGN = 216
CF_EPS = 218
CF_ONE = 219
CF_LNA = 220
CF_LAM = 224
NF = CF_LAM + NL * 4 * 64


def lambda_init(l):
    return 0.8 - 0.6 * math.exp(-0.3 * l)


class StopBuild(Exception):
    pass


class Tk:
    __slots__ = ("name", "w", "r", "multi")

    def __init__(self, name="", multi=False):
        self.name = name
        self.w = [] if multi else None
        self.r = []
        self.multi = multi


class Op:
    __slots__ = ("eng", "fn", "deps", "dma", "semkey", "val", "inc", "target", "sem")

    def __init__(self, eng, fn, dma, inc):
        self.eng = eng
        self.fn = fn
        self.deps = set()
        self.dma = dma
        self.semkey = None
        self.val = 0
        self.inc = inc
        self.target = False
        self.sem = None


class Prog:
    ENGS = ["pe", "act", "dve", "pool", "sp"]

    def __init__(self):
        self.stream = {e: [] for e in self.ENGS}
        self.semcount = {}
        self.since_barrier = []

    def op(self, eng, fn, reads=(), writes=(), dma=False, inc=16, semkey=None, bar=True):
        o = Op(eng, fn, dma, inc)
        deps = set()
        for t in list(reads) + list(writes):
            if t.multi:
                if t not in writes:
                    deps.update(t.w)
            elif t.w is not None:
                deps.add(t.w)
        for t in writes:
            deps.update(t.r)
        o.deps = deps
        for t in writes:
            if t.multi:
                t.w.append(o)
            else:
                t.w = o
                t.r = []
        for t in reads:
            t.r.append(o)
        if dma:
            if semkey is None:
                semkey = id(writes[0])
            o.semkey = semkey
            self.semcount[semkey] = self.semcount.get(semkey, 0) + inc
            o.val = self.semcount[semkey]
        self.stream[eng].append(o)
        if bar:
            self.since_barrier.append(o)
        return o

    def barrier(self):
        lasts = []
        for e in self.ENGS:
            for o in reversed(self.stream[e]):
                if not o.dma and o.fn is not None:
                    lasts.append(o)
                    break
        dmas = [o for o in self.since_barrier if o.dma]
        self.since_barrier = []
        for e in self.ENGS:
            o = Op(e, None, False, 0)
            o.deps = set(lasts) | set(dmas)
            self.stream[e].append(o)

    def emit(self, nc, stack):
        for e in self.ENGS:
            for o in self.stream[e]:
                for d in o.deps:
                    if not d.dma:
                        d.target = True
        engsem = {e: stack.enter_context(nc.semaphore("es_" + e)) for e in self.ENGS}
        keysem = {}
        for e in self.ENGS:
            cnt = 0
            for o in self.stream[e]:
                if o.dma:
                    if o.semkey not in keysem:
                        keysem[o.semkey] = stack.enter_context(nc.semaphore("ds%d" % len(keysem)))
                    o.sem = keysem[o.semkey]
                else:
                    if o.target:
                        cnt += 1
                    o.val = cnt
                    o.sem = engsem[e]
        self.nsem = len(keysem) + len(engsem)
        print('semaphores used:', self.nsem)
        block = stack.enter_context(nc.Block())
        prog = self

        def run(e):
            def body(eng):
                seen = {}
                for o in prog.stream[e]:
                    need = {}
                    for d in o.deps:
                        if e == "pe" and d.eng == "pe" and not d.dma:
                            continue
                        k = id(d.sem)
                        if k not in need or need[k][1] < d.val:
                            need[k] = (d.sem, d.val)
                    for k, (sem, val) in need.items():
                        if seen.get(k, 0) < val:
                            eng.wait_ge(sem, val)
                            seen[k] = val
                    if o.fn is None:
                        continue
                    inst = o.fn(eng)
                    if o.dma:
                        if o.inc == 1:
                            inst.then_inc(o.sem)
                        else:
                            inst.then_inc(o.sem, o.inc)
                    elif o.target:
                        inst.then_inc(o.sem, 1)
            return body

        block.tensor(run("pe"))
        block.scalar(run("act"))
        block.vector(run("dve"))
        block.gpsimd(run("pool"))
        block.sync(run("sp"))


def _dil_count(delta):
    c = np.zeros_like(delta, dtype=np.float32)
    c += ((delta >= 0) & (delta <= 128)).astype(np.float32)
    c += ((delta >= 0) & (delta <= 512) & (delta % 4 == 0)).astype(np.float32)
    c += ((delta >= 0) & (delta <= 2048) & (delta % 16 == 0)).astype(np.float32)
    return c


def build_cb():
    cb = np.zeros((128, NB), np.float32)
    j = np.arange(128)[:, None]
    k = np.arange(128)[None, :]
    cb[:, CB_ID:CB_ID + 128] = np.eye(128)
    cb[:, CB_ONES:CB_ONES + 128] = 1.0
    cb[:, CB_NUINC:CB_NUINC + 128] = -(j >= k).astype(np.float32)
    cb[:, CB_NONES:CB_NONES + 128] = -1.0
    pd = np.zeros((128, 128), np.float32)
    for d in range(32):
        partner = d + 16 if d < 16 else d - 16
        pd[partner, d] = 1.0
    cb[:, CB_PDIL:CB_PDIL + 128] = pd
    pf = np.zeros((128, 128), np.float32)
    for base in (0, 64):
        for d in range(16):
            partner = d + 8 if d < 8 else d - 8
            pf[base + partner, base + d] = 1.0
    cb[:, CB_PDIFF:CB_PDIFF + 128] = pf
    ki = np.arange(128)[:, None]
    xx = np.arange(TOEP_W)[None, :]
    cb[:, CB_TOEP:CB_TOEP + TOEP_W] = _dil_count(xx - 384 - ki)
    xx = np.arange(896)[None, :]
    cb[:, CB_CINC:CB_CINC + 896] = ((xx - 384 - ki) >= 0).astype(np.float32)
    cb[:, CB_CSTR:CB_CSTR + 896] = ((xx - 384 - ki) >= 1).astype(np.float32)
    return cb.astype(ml_dtypes.bfloat16)


def build_cf(inp):
    cf = np.zeros((128, NF), np.float32)
    cf[:, CF_ID:CF_ID + 128] = np.eye(128)

    def cols(v):
        return np.ascontiguousarray(np.asarray(v, np.float32).reshape(16, 128).T)

    for l in range(NL):
        cf[:, CF_GMIX + l * 16:CF_GMIX + (l + 1) * 16] = cols(inp["norm_mix_g"][l])
        cf[:, CF_GFFN + l * 16:CF_GFFN + (l + 1) * 16] = cols(inp["norm_ffn_g"][l])
        cf[:, CF_GSB + l] = np.asarray(inp["g_sb_out"][l], np.float32)
        cf[:, CF_GDF + l] = np.asarray(inp["g_diff_out"][l], np.float32)
        cf[:, CF_GDL + l] = np.asarray(inp["g_dil_out"][l], np.float32)
        cf[:, CF_LNA + l] = math.log(1.0 - lambda_init(l))
        for i, nm in enumerate(("lambda_q1", "lambda_k1", "lambda_q2", "lambda_k2")):
            o = CF_LAM + (l * 4 + i) * 64
            cf[:, o:o + 64] = np.asarray(inp[nm][l], np.float32)[None, :]
    cf[:, CF_GFIN:CF_GFIN + 16] = cols(inp["norm_final_g"])
    invd = np.zeros(128, np.float32)
    sgd = np.zeros(128, np.float32)
    fr = (np.float32(THETA) ** (-np.arange(16, dtype=np.float32) / np.float32(16))).astype(np.float32)
    for d in range(32):
        invd[d] = fr[d % 16]
        sgd[d] = -1.0 if d < 16 else 1.0
    invf = np.zeros(128, np.float32)
    sgf = np.zeros(128, np.float32)
    fr8 = (np.float32(THETA) ** (-np.arange(8, dtype=np.float32) / np.float32(8))).astype(np.float32)
    for base in (0, 64):
        for d in range(16):
            invf[base + d] = fr8[d % 8]
            sgf[base + d] = -1.0 if d < 8 else 1.0
    cf[:, CF_INVF] = invd
    cf[:, CF_INVF + 1] = invf
    cf[:, CF_SGN] = sgd
    cf[:, CF_SGN + 1] = sgf
    cf[:, CF_EPS] = EPS
    cf[:, CF_ONE] = 1.0
    return cf


def build_program(stop_after=None, dbg=None):
    nc = bass.Bass("TRN2", target_bir_lowering=False)
    P = Prog()
    dbg = dbg or []
    dbg_out = {}

    def dram(name, shape, dt, kind=None):
        if kind:
            return nc.dram_tensor(name, shape, dt, kind=kind)
        return nc.dram_tensor(name, shape, dt)

    x_in = dram("x", [T, D], F32, "ExternalInput")
    pos_in = dram("pos", [1, S], I32, "ExternalInput")
    cb_in = dram("cb", [128, NB], BF16, "ExternalInput")
    cf_in = dram("cf", [128, NF], F32, "ExternalInput")
    w_in = dram("w_in", [NL, D, 1536], F32, "ExternalInput")
    w_out = dram("w_out", [NL, D, D], F32, "ExternalInput")
    need_ffn = stop_after is None or stop_after.startswith("ffn") or stop_after.startswith("h1") or stop_after.startswith("att1") or stop_after.startswith("oproj1")
    if need_ffn:
        w_gate = dram("w_gate", [NL, D, DFF], F32, "ExternalInput")
        w_up = dram("w_up", [NL, D, DFF], F32, "ExternalInput")
        w_down = dram("w_down", [NL, DFF, D], F32, "ExternalInput")
    out_t = dram("out", [T, D], F32, "ExternalOutput")

    hbuf_in = [dram("hbuf_in%d" % l, [NC, 128, T], BF16) for l in range(NL)]
    hbuf_all = [dram("hbuf_all%d" % l, [NC, 512, T], BF16) for l in range(NL)]
    mbuf_in = [dram("mbuf_in%d" % l, [4, 4, 128, T], BF16) for l in range(NL)]
    mbuf_all = [dram("mbuf_all%d" % l, [4, 4, 512, T], BF16) for l in range(NL)]
    ropeC = [dram("ropeC%d" % i, [128, S], F32) for i in range(2)]
    ropeS = [dram("ropeS%d" % i, [128, S], F32) for i in range(2)]
    tk_hin = [[Tk("hin") for _ in range(NC)] for _ in range(NL)]
    tk_hall = [Tk("hall", multi=True) for _ in range(NL)]
    tk_min = [Tk("min", multi=True) for _ in range(NL)]
    tk_mall = [Tk("mall", multi=True) for _ in range(NL)]
    tk_rope = Tk("rope", multi=True)
    tk_out = Tk("out", multi=True)
    tk_dbg = Tk("dbg", multi=True)

    stack = ExitStack()
    ARENA_BYTES = int(os.environ.get("ARENA_KB", "200")) * 1024
    arena = stack.enter_context(nc.sbuf_tensor("arena", [128, ARENA_BYTES // 2], BF16))
    psum = stack.enter_context(nc.psum_tensor("psum", [128, 8 * 512], F32))
    banks = [psum[:, i * 512:(i + 1) * 512] for i in range(8)]
    bank_tk = [Tk("bank%d" % i) for i in range(8)]

    class Alloc:
        def __init__(self):
            self.base = 0
            self.top = 0

        def take(self, nbytes):
            nbytes = (nbytes + 63) // 64 * 64
            off = self.top
            self.top += nbytes
            assert self.top <= ARENA_BYTES, ("SBUF overflow", self.top)
            return off

        def persist(self):
            self.base = self.top

        def reset(self):
            self.top = self.base

    A = Alloc()

    def sb(shape, dt):
        n = int(np.prod(shape))
        esz = 2 if dt == BF16 else 4
        off = A.take(n * esz)
        v = arena[:, off // 2: off // 2 + n * esz // 2]
        if dt != BF16:
            v = v.bitcast(dt)
        if len(shape) == 2:
            v = v.rearrange("p (a b) -> p a b", a=shape[0])
        elif len(shape) == 3:
            v = v.rearrange("p (a b c) -> p a b c", a=shape[0], b=shape[1])
        return v

    bank_rr = [0]

    def nb(lo=0, hi=8):
        i = lo + bank_rr[0] % (hi - lo)
        bank_rr[0] += 1
        return i

    def phase_end():
        P.barrier()
        A.reset()

    def dump(name, ap_sb, shape, dt, reads):
        if name not in dbg:
            return
        t = dram("dbg_" + name, shape, dt, "ExternalOutput")
        dbg_out[name] = t
        P.op("sp", lambda e, t=t, a=ap_sb: e.dma_start(out=t.ap(), in_=a), reads=reads, writes=[tk_dbg], dma=True)

    cb = sb([NB], BF16)
    cf = sb([NF], F32)
    xT = sb([NC, T], F32)
    neglam = sb([NL], F32)
    tk_cb, tk_cf, tk_neglam = Tk("cb"), Tk("cf"), Tk("neglam")
    tk_x = [[Tk("xT") for _ in range(2)] for _ in range(NC)]
    A.persist()

    ident_f = cf[:, CF_ID:CF_ID + 128]
    ident_b = cb[:, CB_ID:CB_ID + 128]
    ones_b = cb[:, CB_ONES:CB_ONES + 128]

    def cfc(col):
        return cf[:, col:col + 1]

    try:
        P.op("sp", lambda e: e.dma_start(out=cb, in_=cb_in.ap()), writes=[tk_cb], dma=True)
        P.op("sp", lambda e: e.dma_start(out=cf, in_=cf_in.ap()), writes=[tk_cf], dma=True)

        lt = sb([4, 64], F32)
        ls = sb([8], F32)
        tk_lt, tk_ls = Tk(), Tk()
        for l in range(NL):
            for i in range(2):
                a0 = CF_LAM + (l * 4 + 2 * i) * 64
                P.op("dve", lambda e, i=i, a0=a0: e.tensor_tensor(out=lt[:, i, :], in0=cf[:, a0:a0 + 64],
                                                                   in1=cf[:, a0 + 64:a0 + 128], op=ALU.mult),
                     reads=[tk_cf], writes=[tk_lt])
                P.op("dve", lambda e, i=i: e.reduce_sum(out=ls[:, i:i + 1], in_=lt[:, i, :], axis=AX.X),
                     reads=[tk_lt], writes=[tk_ls])
                P.op("act", lambda e, i=i: e.activation(out=ls[:, 2 + i:3 + i], in_=ls[:, i:i + 1], func=AF.Exp),
                     reads=[tk_ls], writes=[tk_ls])
            P.op("dve", lambda e: e.tensor_tensor(out=ls[:, 4:5], in0=ls[:, 2:3], in1=ls[:, 3:4], op=ALU.subtract),
                 reads=[tk_ls], writes=[tk_ls])
            P.op("dve", lambda e, l=l: e.tensor_scalar(out=neglam[:, l:l + 1], in0=ls[:, 4:5], scalar1=-1.0,
                                                       scalar2=-lambda_init(l), op0=ALU.mult, op1=ALU.add),
                 reads=[tk_ls], writes=[tk_neglam])

        xs = [sb([D], F32) for _ in range(8)]
        tk_xs = [Tk() for _ in range(8)]
        for i in range(8):
            P.op("sp", lambda e, i=i: e.dma_start(out=xs[i], in_=x_in.ap()[i * 128:(i + 1) * 128, :]),
                 writes=[tk_xs[i]], dma=True)
        ev = 0
        for c in range(NC):
            for tg in range(2):
                b = nb()

                def tr(e, c=c, tg=tg, b=b):
                    ins = None
                    for i in range(4):
                        ins = e.transpose(out=banks[b][:, i * 128:(i + 1) * 128],
                                          in_=xs[tg * 4 + i][:, c * 128:(c + 1) * 128], identity=ident_f)
                    return ins
                P.op("pe", tr, reads=[tk_xs[tg * 4 + i] for i in range(4)] + [tk_cf], writes=[bank_tk[b]])
                dst = xT[:, c, tg * 512:(tg + 1) * 512]
                if ev % 2 == 0:
                    P.op("act", lambda e, b=b, dst=dst: e.activation(out=dst, in_=banks[b][:], func=AF.Copy),
                         reads=[bank_tk[b]], writes=[tk_x[c][tg]])
                else:
                    P.op("dve", lambda e, b=b, dst=dst: e.tensor_copy(out=dst, in_=banks[b][:]),
                         reads=[bank_tk[b]], writes=[tk_x[c][tg]])
                ev += 1
        phase_end()
        if stop_after == 'p0a':
            raise StopBuild()

        def rope_phase():
            posi = sb([S], I32)
            posf = sb([S], F32)
            tk_posi, tk_posf = Tk(), Tk()
            P.op("sp", lambda e: e.dma_start(out=posi, in_=pos_in.ap().partition_broadcast(128)), writes=[tk_posi], dma=True)
            P.op("dve", lambda e: e.tensor_copy(out=posf, in_=posi), reads=[tk_posi], writes=[tk_posf])
            RW = 1024
            rt = {k: [sb([RW], I32 if k == "ki" else F32) for _ in range(2)] for k in ("a", "y", "ki", "kf", "r", "m", "sn")}
            rtk = {k: [Tk() for _ in range(2)] for k in rt}
            it = 0
            for ty in range(2):
                for kind in range(2):
                    for ch in range(S // RW):
                        s_ = it % 2
                        it += 1
                        cs = slice(ch * RW, (ch + 1) * RW)
                        a, y, ki, kf, r, m, sn = (rt[k][s_] for k in ("a", "y", "ki", "kf", "r", "m", "sn"))
                        ta, ty_, tki, tkf, tr_, tm, tsn = (rtk[k][s_] for k in ("a", "y", "ki", "kf", "r", "m", "sn"))
                        off = PI / 2 if kind == 0 else 0.0
                        P.op("dve", lambda e, a=a, cs=cs, ty=ty, off=off: e.tensor_scalar(
                            out=a, in0=posf[:, cs], scalar1=cfc(CF_INVF + ty), scalar2=off, op0=ALU.mult, op1=ALU.add),
                            reads=[tk_posf, tk_cf], writes=[ta])
                        P.op("dve", lambda e, a=a, y=y: e.tensor_scalar(out=y, in0=a, scalar1=1.0 / (2 * PI), scalar2=None,
                                                                         op0=ALU.mult), reads=[ta], writes=[ty_])
                        P.op("dve", lambda e, y=y, ki=ki: e.tensor_copy(out=ki, in_=y), reads=[ty_], writes=[tki])
                        P.op("dve", lambda e, kf=kf, ki=ki: e.tensor_copy(out=kf, in_=ki), reads=[tki], writes=[tkf])
                        P.op("dve", lambda e, kf=kf, a=a, r=r: e.scalar_tensor_tensor(
                            out=r, in0=kf, scalar=-2 * PI, in1=a, op0=ALU.mult, op1=ALU.add), reads=[tkf, ta], writes=[tr_])
                        P.op("dve", lambda e, r=r, m=m: e.tensor_scalar(out=m, in0=r, scalar1=PI, scalar2=2 * PI,
                                                                         op0=ALU.is_gt, op1=ALU.mult), reads=[tr_], writes=[tm])
                        P.op("dve", lambda e, r=r, m=m: e.tensor_tensor(out=r, in0=r, in1=m, op=ALU.subtract),
                             reads=[tm, tr_], writes=[tr_])
                        P.op("dve", lambda e, r=r: e.tensor_scalar(out=r, in0=r, scalar1=PI, scalar2=-PI,
                                                                    op0=ALU.min, op1=ALU.max), reads=[tr_], writes=[tr_])
                        P.op("act", lambda e, r=r, sn=sn: e.activation(out=sn, in_=r, func=AF.Sin), reads=[tr_], writes=[tsn])
                        if kind == 1:
                            P.op("dve", lambda e, sn=sn, ty=ty: e.tensor_scalar(out=sn, in0=sn, scalar1=cfc(CF_SGN + ty),
                                                                                 scalar2=None, op0=ALU.mult),
                                 reads=[tsn, tk_cf], writes=[tsn])
                        dst = (ropeC if kind == 0 else ropeS)[ty]
                        P.op("sp", lambda e, dst=dst, cs=cs, sn=sn: e.dma_start(out=dst.ap()[:, cs], in_=sn),
                             reads=[tsn], writes=[tk_rope], dma=True, semkey=("sn", s_))
            phase_end()

        def norm_phase(gcol, hT, tk_h, out_dt_is_bf16=True):
            sq = [sb([T], BF16) for _ in range(2)]
            tk_sq = [Tk() for _ in range(2)]
            lnv = sb([T], F32)
            rstd = sb([T], F32)
            tk_lnv, tk_rstd = Tk(), Tk()
            b0, b1 = nb(), nb()
            bs = (b0, b1)
            for c in range(NC):
                s_ = c % 2
                P.op("act", lambda e, c=c, s_=s_: e.activation(out=sq[s_], in_=xT[:, c, :], func=AF.Square),
                     reads=[tk_x[c][0], tk_x[c][1]], writes=[tk_sq[s_]])

                def mm(e, c=c, s_=s_):
                    ins = None
                    for th in range(2):
                        ins = e.matmul(banks[bs[th]][:], lhsT=ones_b, rhs=sq[s_][:, th * 512:(th + 1) * 512],
                                       start=(c == 0), stop=(c == NC - 1))
                    return ins
                P.op("pe", mm, reads=[tk_sq[s_], tk_cb], writes=[bank_tk[b0], bank_tk[b1]])
            for th in range(2):
                P.op("act", lambda e, th=th: e.activation(out=lnv[:, th * 512:(th + 1) * 512], in_=banks[bs[th]][:],
                                                         func=AF.Ln, bias=cfc(CF_EPS), scale=1.0 / D),
                     reads=[bank_tk[bs[th]], tk_cf], writes=[tk_lnv])
            P.op("act", lambda e: e.activation(out=rstd, in_=lnv, func=AF.Exp, scale=-0.5), reads=[tk_lnv], writes=[tk_rstd])
            for c in range(NC):
                P.op("dve", lambda e, c=c: e.scalar_tensor_tensor(out=hT[:, c, :], in0=xT[:, c, :], scalar=cfc(gcol + c),
                                                                  in1=rstd, op0=ALU.mult, op1=ALU.mult),
                     reads=[tk_x[c][0], tk_x[c][1], tk_rstd, tk_cf], writes=[tk_h[c]])

        hn_calls = [0]

        def head_norm_steps(o, tk_o, gcol, lnbias_col, dst_dram_ap, tk_dst, ring, msb=4):
            sq, tk_sq, ln_, tk_ln, rs, tk_rs, mo, tk_mo = ring
            b = msb
            hn_calls[0] += 1
            skey = ("mo", hn_calls[0] % 2)

            def step_a():
                P.op("act", lambda e: e.activation(out=sq, in_=o, func=AF.Square), reads=[tk_o], writes=[tk_sq])
                P.op("pe", lambda e, b=b: e.matmul(banks[b][:], lhsT=ones_b, rhs=sq, start=True, stop=True),
                     reads=[tk_sq, tk_cb], writes=[bank_tk[b]])

            def step_b():
                P.op("act", lambda e, b=b: e.activation(out=ln_, in_=banks[b][:], func=AF.Ln, bias=cfc(CF_EPS), scale=1.0 / 128),
                     reads=[bank_tk[b], tk_cf], writes=[tk_ln])
                if lnbias_col is None:
                    P.op("act", lambda e: e.activation(out=rs, in_=ln_, func=AF.Exp, scale=-0.5), reads=[tk_ln], writes=[tk_rs])
                else:
                    P.op("act", lambda e: e.activation(out=rs, in_=ln_, func=AF.Exp, scale=-0.5, bias=cfc(lnbias_col)),
                         reads=[tk_ln, tk_cf], writes=[tk_rs])
                P.op("dve", lambda e: e.scalar_tensor_tensor(out=mo, in0=o, scalar=cfc(gcol), in1=rs, op0=ALU.mult, op1=ALU.mult),
                     reads=[tk_o, tk_rs, tk_cf], writes=[tk_mo])
                P.op("sp", lambda e: e.dma_start(out=dst_dram_ap, in_=mo), reads=[tk_mo], writes=[tk_dst], dma=True, semkey=skey)
            return [step_a, step_b]

        def head_norm(*args, **kw):
            for st in head_norm_steps(*args, **kw):
                st()

        for l in range(NL):
            saved_base = A.base
            wq_ring = [sb([3, NC, 128], BF16) for _ in range(2)]
            tk_w_ring = [[Tk() for _ in range(3)] for _ in range(2)]
            A.persist()

            def load_wq(s_, l=l, wq_ring=wq_ring, tk_w_ring=tk_w_ring):
                wcols_ = (2 * s_ * 128, (2 * s_ + 1) * 128, 1024 + s_ * 128)
                for i in range(3):
                    P.op("pool", lambda e, i=i, wcols_=wcols_, s_=s_: e.dma_start(
                        out=wq_ring[s_ % 2][:, i], in_=w_in.ap()[l, :, wcols_[i]:wcols_[i] + 128].rearrange("(c p) n -> p c n", p=128)),
                        writes=[tk_w_ring[s_ % 2][i]], dma=True, semkey=("wq", s_ % 2, i))
            load_wq(0)
            hT = sb([NC, T], BF16)
            tk_h = [Tk() for _ in range(NC)]
            norm_phase(CF_GMIX + l * 16, hT, tk_h)
            for c in range(NC):
                P.op("sp", lambda e, l=l, hT=hT, c=c: e.dma_start(out=hbuf_in[l].ap()[c], in_=hT[:, c, :]),
                     reads=[tk_h[c]], writes=[tk_hin[l][c]], dma=True, semkey=("hin", c))
                P.op("pool", lambda e, l=l, c=c: e.collective_compute("AllGather", ALU.bypass, replica_groups=[[0, 1, 2, 3], [4, 5, 6, 7]],
                                                                     ins=[hbuf_in[l].ap()[c]], outs=[hbuf_all[l].ap()[c]]),
                     reads=[tk_hin[l][c]], writes=[tk_hall[l]], dma=True, inc=1, bar=False)
            if l == 0:
                dump("h0", hT, [128, NC, T], BF16, tk_h)
            phase_end()
            if l == 0:
                rope_phase()
            if stop_after == "h%d" % l:
                break

            for s in range(4):
                kind = ("sb", "diff", "dil", "dil")[s]
                rope_ty = {"sb": None, "diff": 1, "dil": 0}[kind]
                qscale = 0.125 if kind == "diff" else 128 ** -0.5
                perm = None if rope_ty is None else cb[:, (CB_PDIL if rope_ty == 0 else CB_PDIFF):][:, 0:128]

                wq = wq_ring[s % 2]
                tk_w = tk_w_ring[s % 2]
                qT = sb([S], BF16)
                kT = sb([S], BF16)
                vv = sb([32, 128], BF16)
                tk_q = [Tk() for _ in range(8)]
                tk_k = [Tk() for _ in range(8)]
                tk_v = [Tk() for _ in range(8)]
                mark_ = A.top
                ht = [sb([NC, 512], BF16) for _ in range(2)]
                tk_ht = [Tk() for _ in range(2)]
                rC = [sb([512], F32) for _ in range(2)]
                rS = [sb([512], F32) for _ in range(2)]
                tk_rC = [Tk() for _ in range(2)]
                tk_rS = [Tk() for _ in range(2)]
                raw = [sb([512], BF16) for _ in range(2)]
                tk_raw = [Tk() for _ in range(2)]
                t1 = [sb([512], F32) for _ in range(2)]
                t2 = [sb([512], F32) for _ in range(2)]
                tk_t1 = [Tk() for _ in range(2)]
                tk_t2 = [Tk() for _ in range(2)]

                ri = 0
                for tt in range(8):
                    hs = tt % 2
                    r_, off = tt // 2, (tt % 2) * 512
                    P.op("sp", lambda e, l=l, r_=r_, off=off, hs=hs, ht=ht: e.dma_start(
                        out=ht[hs], in_=hbuf_all[l].ap()[:, r_ * 128:(r_ + 1) * 128, off:off + 512].rearrange("c p t -> p c t")),
                        reads=[tk_hall[l]], writes=[tk_ht[hs]], dma=True, semkey=("ht", hs))
                    if rope_ty is not None:
                        P.op("sp", lambda e, tt=tt, hs=hs, rC=rC, rope_ty=rope_ty: e.dma_start(
                            out=rC[hs], in_=ropeC[rope_ty].ap()[:, tt * 512:(tt + 1) * 512]),
                            reads=[tk_rope], writes=[tk_rC[hs]], dma=True, semkey=("rC", hs))
                        P.op("sp", lambda e, tt=tt, hs=hs, rS=rS, rope_ty=rope_ty: e.dma_start(
                            out=rS[hs], in_=ropeS[rope_ty].ap()[:, tt * 512:(tt + 1) * 512]),
                            reads=[tk_rope], writes=[tk_rS[hs]], dma=True, semkey=("rS", hs))
                    post = []
                    for qi, (dstT, tkd, sc) in enumerate(((qT, tk_q, qscale), (kT, tk_k, 1.0))):
                        b = nb()

                        def mm(e, b=b, qi=qi, hs=hs, wq=wq, ht=ht):
                            ins = None
                            for c in range(NC):
                                ins = e.matmul(banks[b][:], lhsT=wq[:, qi, c, :], rhs=ht[hs][:, c, :],
                                               start=(c == 0), stop=(c == NC - 1))
                            return ins
                        P.op("pe", mm, reads=[tk_w[qi], tk_ht[hs]], writes=[bank_tk[b]])
                        dst = dstT[:, tt * 512:(tt + 1) * 512]
                        if rope_ty is None:
                            P.op("act", lambda e, b=b, dst=dst, sc=sc: e.activation(out=dst, in_=banks[b][:], func=AF.Copy, scale=sc),
                                 reads=[bank_tk[b]], writes=[tkd[tt]])
                        else:
                            rs_ = ri % 2
                            ri += 1
                            P.op("act", lambda e, b=b, rs_=rs_, sc=sc, raw=raw: e.activation(out=raw[rs_], in_=banks[b][:], func=AF.Copy, scale=sc),
                                 reads=[bank_tk[b]], writes=[tk_raw[rs_]])
                            def post_fn(rs_=rs_, hs=hs, dst=dst, tkd=tkd, tt=tt, perm=perm, raw=raw, t1=t1, t2=t2, rS=rS, rC=rC,
                                        tk_raw=tk_raw, tk_t1=tk_t1, tk_t2=tk_t2, tk_rS=tk_rS, tk_rC=tk_rC):
                                b2 = nb()
                                P.op("pe", lambda e, b2=b2, rs_=rs_, perm=perm, raw=raw: e.matmul(banks[b2][:], lhsT=perm, rhs=raw[rs_], start=True, stop=True),
                                     reads=[tk_raw[rs_], tk_cb], writes=[bank_tk[b2]])
                                P.op("dve", lambda e, b2=b2, rs_=rs_, hs=hs, t1=t1, rS=rS: e.tensor_tensor(out=t1[rs_], in0=banks[b2][:], in1=rS[hs], op=ALU.mult),
                                     reads=[bank_tk[b2], tk_rS[hs]], writes=[tk_t1[rs_]])
                                P.op("pool", lambda e, rs_=rs_, hs=hs, t2=t2, raw=raw, rC=rC: e.tensor_tensor(out=t2[rs_], in0=raw[rs_], in1=rC[hs], op=ALU.mult),
                                     reads=[tk_raw[rs_], tk_rC[hs]], writes=[tk_t2[rs_]])
                                P.op("dve", lambda e, rs_=rs_, dst=dst, t1=t1, t2=t2: e.tensor_tensor(out=dst, in0=t1[rs_], in1=t2[rs_], op=ALU.add),
                                     reads=[tk_t1[rs_], tk_t2[rs_]], writes=[tkd[tt]])
                            post.append(post_fn)
                    b = nb()

                    def mmv(e, b=b, hs=hs, wq=wq, ht=ht):
                        ins = None
                        for sub in range(4):
                            for c in range(NC):
                                ins = e.matmul(banks[b][:, sub * 128:(sub + 1) * 128], lhsT=ht[hs][:, c, sub * 128:(sub + 1) * 128],
                                               rhs=wq[:, 2, c, :], start=(c == 0), stop=(c == NC - 1))
                        return ins
                    P.op("pe", mmv, reads=[tk_w[2], tk_ht[hs]], writes=[bank_tk[b]])
                    P.op("dve", lambda e, b=b, tt=tt, vv=vv: e.tensor_copy(out=vv[:, tt * 4:(tt + 1) * 4, :],
                                                                            in_=banks[b][:].rearrange("p (a b) -> p a b", a=4)),
                         reads=[bank_tk[b]], writes=[tk_v[tt]])
                    for pf_ in post:
                        pf_()
                if l == 0 and s in (0, 1, 2):
                    dump("q%d" % s, qT, [128, S], BF16, tk_q)
                    dump("k%d" % s, kT, [128, S], BF16, tk_k)
                    dump("v%d" % s, vv, [128, 32, 128], BF16, tk_v)

                P.barrier()
                A.top = mark_
                if s + 1 < 4:
                    load_wq(s + 1)
                NE = 4
                NE2 = 3
                if kind == "sb":
                    E = [sb([512], BF16) for _ in range(NE)]
                    Em = [sb([512], BF16) for _ in range(NE)]
                    tk_E = [Tk() for _ in range(NE)]
                    tk_Em = [Tk() for _ in range(NE)]
                else:
                    E2 = [sb([1024], BF16) for _ in range(NE2)]
                    Em2 = [sb([1024], BF16) for _ in range(NE2)]
                    tk_E2 = [Tk() for _ in range(NE2)]
                    tk_Em2 = [[Tk(), Tk()] for _ in range(NE2)]
                pcnt = [0]
                oacc = [sb([512], F32) for _ in range(3)]
                tk_oacc = [Tk() for _ in range(3)]
                rl = sb([512], F32)
                tk_rl = Tk()
                hn_ring = []
                for _ in range(2):
                    hn_ring.append((sb([512], BF16), Tk(), sb([512], F32), Tk(), sb([512], F32), Tk(), sb([512], BF16), Tk()))
                if kind == "sb":
                    ef = [sb([512], F32) for _ in range(2)]
                    tk_ef = [Tk() for _ in range(2)]
                    sp_ = [sb([512], BF16) for _ in range(3)]
                    tk_sp = [Tk() for _ in range(3)]
                    spm = [sb([512], BF16) for _ in range(3)]
                    tk_spm = [Tk() for _ in range(3)]
                    Ts = [sb([512], BF16) for _ in range(4)]
                    tk_Ts = [Tk() for _ in range(4)]
                OB, LB = 6, 7
                ecnt = [0]
                hcnt = [0]

                pending = []
                olcnt = [0]
                lnl = [sb([512], F32) for _ in range(2)]
                tk_lnl = [Tk() for _ in range(2)]
                oraw = [sb([512], F32) for _ in range(2)]
                tk_oraw = [Tk() for _ in range(2)]

                def flush_pending():
                    while pending:
                        pending.pop(0)()

                def run_pending_step():
                    if pending:
                        pending.pop(0)()

                def softmax_pass(j, parts, ktiles, maskfn, o_dst, tk_odst, after=None):
                    p0, p1 = parts
                    qsl = qT[p0:p1, j * 512:(j + 1) * 512]
                    npair = len(ktiles) // 2
                    assert len(ktiles) % 2 == 0
                    sb_of = {}
                    OB, LB = 6, 7
                    ek = olcnt[0] % 2
                    olcnt[0] += 1

                    def issue_s(pi):
                        pb = (pcnt[0] % 2) * 2
                        pcnt[0] += 1
                        sb_of[pi] = pb
                        i0, i1 = ktiles[2 * pi], ktiles[2 * pi + 1]

                        def mm(e, pb=pb, i0=i0, i1=i1):
                            e.matmul(banks[pb][:], lhsT=kT[p0:p1, i0 * 128:(i0 + 1) * 128], rhs=qsl, start=True, stop=True)
                            return e.matmul(banks[pb + 1][:], lhsT=kT[p0:p1, i1 * 128:(i1 + 1) * 128], rhs=qsl, start=True, stop=True)
                        P.op("pe", mm, reads=[tk_k[i0 // 4], tk_k[i1 // 4], tk_q[j]], writes=[bank_tk[pb], bank_tk[pb + 1]])
                    issue_s(0)
                    if npair > 1:
                        issue_s(1)
                    for pi in range(npair):
                        if pi >= 1:
                            run_pending_step()
                        pb = sb_of.pop(pi)
                        es = ecnt[0] % NE2
                        ecnt[0] += 1
                        P.op("act", lambda e, pb=pb, es=es: e.activation(out=E2[es], in_=psum[:, pb * 512:(pb + 2) * 512], func=AF.Exp),
                             reads=[bank_tk[pb], bank_tk[pb + 1]], writes=[tk_E2[es]])
                        srcs = []
                        for h_ in range(2):
                            i = ktiles[2 * pi + h_]
                            hsl = slice(h_ * 512, (h_ + 1) * 512)
                            mk = maskfn(i)
                            if mk is not None:
                                P.op("dve", lambda e, es=es, mk=mk, hsl=hsl: e.tensor_tensor(out=Em2[es][:, hsl], in0=E2[es][:, hsl], in1=mk, op=ALU.mult),
                                     reads=[tk_E2[es], tk_cb], writes=[tk_Em2[es][h_]])
                                srcs.append((Em2[es][:, hsl], tk_Em2[es][h_], i))
                            else:
                                srcs.append((E2[es][:, hsl], tk_E2[es], i))

                        if pi + 2 < npair:
                            issue_s(pi + 2)

                        def mmo(e, srcs=srcs, pi=pi, OB=OB, LB=LB):
                            first, last = (pi == 0), (pi == npair - 1)
                            (s0, _, i0), (s1, _, i1) = srcs
                            e.matmul(banks[OB][:], lhsT=vv[:, i0, :], rhs=s0, start=first, stop=False)
                            e.matmul(banks[OB][:], lhsT=vv[:, i1, :], rhs=s1, start=False, stop=last)
                            e.matmul(banks[LB][:], lhsT=ones_b, rhs=s0, start=first, stop=False)
                            return e.matmul(banks[LB][:], lhsT=ones_b, rhs=s1, start=False, stop=last)
                        P.op("pe", mmo, reads=[srcs[0][1], srcs[1][1], tk_v[srcs[0][2] // 4], tk_v[srcs[1][2] // 4], tk_cb],
                             writes=[bank_tk[OB], bank_tk[LB]])
                    flush_pending()
                    P.op("act", lambda e, ek=ek: e.activation(out=lnl[ek], in_=banks[LB][:], func=AF.Ln), reads=[bank_tk[LB]], writes=[tk_lnl[ek]])
                    P.op("dve", lambda e, ek=ek: e.tensor_copy(out=oraw[ek], in_=banks[OB][:]), reads=[bank_tk[OB]], writes=[tk_oraw[ek]])

                    def epilogue(ek=ek):
                        P.op("act", lambda e: e.activation(out=rl, in_=lnl[ek], func=AF.Exp, scale=-1.0), reads=[tk_lnl[ek]], writes=[tk_rl])
                        P.op("dve", lambda e: e.tensor_tensor(out=o_dst, in0=oraw[ek], in1=rl, op=ALU.mult),
                             reads=[tk_oraw[ek], tk_rl], writes=[tk_odst])
                    pending.append(epilogue)
                    if after is not None:
                        pending.extend(after())

                def sb_pass(j, o_dst, tk_odst):
                    tiles = list(range(4 * j + 3, -1, -1))
                    n = len(tiles)
                    cstr = cb[:, CB_CSTR:CB_CSTR + 896]
                    st1 = {}
                    st2 = {}

                    def mask_of(i):
                        if i >= 4 * j:
                            Dq = 512 * j - 128 * i
                            return cstr[:, Dq + 384:Dq + 384 + 512]
                        return None

                    def stage1(idx):
                        i = tiles[idx]
                        qsl = qT[:, j * 512:(j + 1) * 512]
                        b = nb(0, 5)
                        P.op("pe", lambda e, b=b, i=i: e.matmul(banks[b][:], lhsT=kT[:, i * 128:(i + 1) * 128], rhs=qsl, start=True, stop=True),
                             reads=[tk_k[i // 4], tk_q[j]], writes=[bank_tk[b]])
                        fs = idx % 2
                        P.op("act", lambda e, b=b, fs=fs: e.activation(out=ef[fs], in_=banks[b][:], func=AF.Exp),
                             reads=[bank_tk[b]], writes=[tk_ef[fs]])
                        ss = idx % 3
                        P.op("act", lambda e, fs=fs, ss=ss: e.activation(out=sp_[ss], in_=ef[fs], func=AF.Ln, bias=cfc(CF_ONE)),
                             reads=[tk_ef[fs], tk_cf], writes=[tk_sp[ss]])
                        mk = mask_of(i)
                        if mk is not None:
                            P.op("dve", lambda e, ss=ss, mk=mk: e.tensor_tensor(out=spm[ss], in0=sp_[ss], in1=mk, op=ALU.mult),
                                 reads=[tk_sp[ss], tk_cb], writes=[tk_spm[ss]])
                            cur, tkc = spm[ss], tk_spm[ss]
                        else:
                            cur, tkc = sp_[ss], tk_sp[ss]
                        tprev = idx % 4
                        tnew = (idx + 1) % 4
                        if idx + 1 < n:
                            if idx == 0:
                                P.op("pool", lambda e, cur=cur, tnew=tnew: e.tensor_copy(out=Ts[tnew], in_=cur),
                                     reads=[tkc], writes=[tk_Ts[tnew]])
                            else:
                                P.op("pool", lambda e, cur=cur, tnew=tnew, tprev=tprev: e.tensor_tensor(out=Ts[tnew], in0=Ts[tprev], in1=cur, op=ALU.add),
                                     reads=[tkc, tk_Ts[tprev]], writes=[tk_Ts[tnew]])
                        st1[idx] = (cur, tkc, tprev)

                    def stage2(idx):
                        i = tiles[idx]
                        cur, tkc, tprev = st1.pop(idx)
                        qsl = qT[:, j * 512:(j + 1) * 512]
                        b = nb(0, 5)

                        def mml(e, b=b, i=i, cur=cur, tprev=tprev, idx=idx):
                            e.matmul(banks[b][:], lhsT=kT[:, i * 128:(i + 1) * 128], rhs=qsl, start=True, stop=False)
                            ins = e.matmul(banks[b][:], lhsT=cb[:, CB_NUINC:CB_NUINC + 128], rhs=cur, start=False, stop=(idx == 0))
                            if idx > 0:
                                ins = e.matmul(banks[b][:], lhsT=cb[:, CB_NONES:CB_NONES + 128], rhs=Ts[tprev], start=False, stop=True)
                            return ins
                        rd = [tk_k[i // 4], tk_q[j], tkc, tk_cb] + ([tk_Ts[tprev]] if idx > 0 else [])
                        P.op("pe", mml, reads=rd, writes=[bank_tk[b]])
                        es = ecnt[0] % NE
                        ecnt[0] += 1
                        P.op("act", lambda e, b=b, es=es: e.activation(out=E[es], in_=banks[b][:], func=AF.Exp),
                             reads=[bank_tk[b]], writes=[tk_E[es]])
                        mk = mask_of(i)
                        if mk is not None:
                            P.op("dve", lambda e, es=es, mk=mk: e.tensor_tensor(out=Em[es], in0=E[es], in1=mk, op=ALU.mult),
                                 reads=[tk_E[es], tk_cb], writes=[tk_Em[es]])
                            src, tks = Em[es], tk_Em[es]
                        else:
                            src, tks = E[es], tk_E[es]
                        st2[idx] = (src, tks)

                    def stage2b(idx):
                        i = tiles[idx]
                        src, tks = st2.pop(idx)
                        P.op("pe", lambda e, i=i, src=src, idx=idx: e.matmul(banks[OB][:], lhsT=vv[:, i, :], rhs=src, start=(idx == 0), stop=(idx == n - 1)),
                             reads=[tks, tk_v[i // 4]], writes=[bank_tk[OB]])

                    stage1(0)
                    if n > 1:
                        stage1(1)
                    for idx in range(n):
                        stage2(idx)
                        if idx + 2 < n:
                            stage1(idx + 2)
                        if idx >= 1:
                            stage2b(idx - 1)
                    stage2b(n - 1)
                    P.op("act", lambda e: e.activation(out=o_dst, in_=banks[OB][:], func=AF.Copy), reads=[bank_tk[OB]], writes=[tk_odst])

                cinc = cb[:, CB_CINC:CB_CINC + 896]
                toep = cb[:, CB_TOEP:CB_TOEP + TOEP_W]
                def quarter_collective(j, l=l, s=s):
                    if j % 2 == 1:
                        qt = j // 2
                        P.op("pool", lambda e, l=l, qt=qt, s=s: e.collective_compute("AllGather", ALU.bypass, replica_groups=[[0, 1, 2, 3], [4, 5, 6, 7]],
                                                                                    ins=[mbuf_in[l].ap()[qt, s]], outs=[mbuf_all[l].ap()[qt, s]]),
                             reads=[tk_min[l]], writes=[tk_mall[l]], dma=True, inc=1, bar=False)

                for j in range(8):
                    dst = mbuf_in[l].ap()[j // 2, s, :, (j % 2) * 512:(j % 2) * 512 + 512]
                    ring = hn_ring[hcnt[0] % 2]
                    hcnt[0] += 1
                    if kind == "sb":
                        sb_pass(j, oacc[0], tk_oacc[0])
                        head_norm(oacc[0], tk_oacc[0], CF_GSB + l, None, dst, tk_min[l], ring)
                        quarter_collective(j)
                    elif kind == "diff":
                        def mk_diff(i, j=j):
                            if i >= 4 * j:
                                Dq = 512 * j - 128 * i
                                return cinc[:, Dq + 384:Dq + 384 + 512]
                            return None
                        kt = list(range(0, 4 * j + 4))

                        def after_diff(j=j, dst=dst, ring=ring, l=l):
                            sa, sb_ = head_norm_steps(oacc[2], tk_oacc[2], CF_GDF + l, CF_LNA + l, dst, tk_min[l], ring, msb=4)

                            def s2():
                                P.op("dve", lambda e, l=l: e.scalar_tensor_tensor(out=oacc[2], in0=oacc[1], scalar=neglam[:, l:l + 1], in1=oacc[0],
                                                                                 op0=ALU.mult, op1=ALU.add),
                                     reads=[tk_oacc[0], tk_oacc[1], tk_neglam], writes=[tk_oacc[2]])
                                sa()

                            def s3():
                                sb_()
                                quarter_collective(j)
                            return [s2, s3]
                        softmax_pass(j, (0, 64), kt, mk_diff, oacc[0], tk_oacc[0])
                        softmax_pass(j, (64, 128), kt, mk_diff, oacc[1], tk_oacc[1], after=after_diff)
                    else:
                        def mk_dil(i, j=j):
                            Dq = 512 * j - 128 * i
                            return toep[:, Dq + 384:Dq + 384 + 512]
                        kt = list(range(max(0, 4 * j - 16), 4 * j + 4))
                        oslot = j % 2

                        def after_dil(j=j, dst=dst, ring=ring, l=l, oslot=oslot):
                            sa, sb_ = head_norm_steps(oacc[oslot], tk_oacc[oslot], CF_GDL + l, None, dst, tk_min[l], ring, msb=4)

                            def s3():
                                sb_()
                                quarter_collective(j)
                            return [sa, s3]
                        softmax_pass(j, (0, 128), kt, mk_dil, oacc[oslot], tk_oacc[oslot], after=after_dil)
                if kind != "sb":
                    flush_pending()
                phase_end()
                if stop_after == "att%d_%d" % (l, s):
                    break
            if stop_after is not None and stop_after.startswith("att%d" % l):
                break

            A.base = saved_base
            A.top = saved_base
            mx = sb([NC, T], BF16)
            tk_mx = Tk()

            def ld_mx(e, l=l, mx=mx):
                me = e.partition_id() % 4
                return e.dma_start(out=mx, in_=mbuf_all[l].ap()[bass.ds(me, 1)].rearrange("o s (r p) t -> p (o s r) t", p=128))
            if l == 0:
                dump("mx0", mx, [128, NC, T], BF16, [tk_mx])
            NWO = 3
            wo = [sb([NC, 128], BF16) for _ in range(NWO)]
            tk_wo = [Tk() for _ in range(NWO)]

            def mxchunk(n):
                if n < 4:
                    return n
                if n < 8:
                    return 4 + (n - 4)
                m = n - 8
                return (2 + m % 2) * 4 + m // 2

            def ld_wo(dc):
                ws = dc % NWO
                P.op("pool", lambda e, dc=dc, ws=ws, l=l: e.dma_start(
                    out=wo[ws], in_=w_out.ap()[l, :, dc * 128:(dc + 1) * 128].rearrange("(n p) c -> p n c", p=128)),
                    writes=[tk_wo[ws]], dma=True, semkey=("wo", ws))
            ld_wo(0)
            ld_wo(1)
            P.op("pool", ld_mx, reads=[tk_mall[l]], writes=[tk_mx], dma=True, semkey=("mx",))
            for dc in range(NC):
                if dc + 2 < NC:
                    ld_wo(dc + 2)
                ws = dc % NWO
                for th in range(2):
                    b = nb()

                    def mm(e, b=b, ws=ws, th=th, mx=mx):
                        ins = None
                        for n_ in range(NC):
                            ins = e.matmul(banks[b][:], lhsT=wo[ws][:, n_, :], rhs=mx[:, mxchunk(n_), th * 512:(th + 1) * 512],
                                           start=(n_ == 0), stop=(n_ == NC - 1))
                        return ins
                    P.op("pe", mm, reads=[tk_wo[ws], tk_mx], writes=[bank_tk[b]])
                    xsl = xT[:, dc, th * 512:(th + 1) * 512]
                    P.op("dve", lambda e, b=b, xsl=xsl: e.tensor_tensor(out=xsl, in0=xsl, in1=banks[b][:], op=ALU.add),
                         reads=[bank_tk[b], tk_x[dc][th]], writes=[tk_x[dc][th]])
            if l == 0:
                dump("x1", xT, [128, NC, T], F32, [t for c in tk_x for t in c])
            phase_end()
            if stop_after == "oproj%d" % l:
                break

            h2 = sb([NC, T], BF16)
            tk_h2 = [Tk() for _ in range(NC)]
            norm_phase(CF_GFFN + l * 16, h2, tk_h2)
            NWG = 3
            wg = [sb([NC, 128], BF16) for _ in range(NWG)]
            wu = [sb([NC, 128], BF16) for _ in range(NWG)]
            tk_wg = [Tk() for _ in range(NWG)]
            tk_wu = [Tk() for _ in range(NWG)]
            wd = [sb([GFF, D], BF16) for _ in range(2)]
            tk_wd = [Tk() for _ in range(2)]
            actT = [sb([GFF, T], BF16) for _ in range(2)]
            tk_act = [[Tk() for _ in range(GFF)] for _ in range(2)]
            sg = [sb([512], F32) for _ in range(2)]
            tk_sg = [Tk() for _ in range(2)]

            def ld_gu(jc):
                ws = jc % NWG
                P.op("pool", lambda e, jc=jc, ws=ws, l=l: e.dma_start(
                    out=wg[ws], in_=w_gate.ap()[l, :, jc * 128:(jc + 1) * 128].rearrange("(n p) c -> p n c", p=128)),
                    writes=[tk_wg[ws]], dma=True, semkey=("wg", ws))
                P.op("pool", lambda e, jc=jc, ws=ws, l=l: e.dma_start(
                    out=wu[ws], in_=w_up.ap()[l, :, jc * 128:(jc + 1) * 128].rearrange("(n p) c -> p n c", p=128)),
                    writes=[tk_wu[ws]], dma=True, semkey=("wu", ws))

            def ld_wd(gi):
                ws = gi % 2
                P.op("pool", lambda e, gi=gi, ws=ws, l=l: e.dma_start(
                    out=wd[ws], in_=w_down.ap()[l, gi * GFF * 128:(gi + 1) * GFF * 128, :].rearrange("(a p) c -> p a c", p=128)),
                    writes=[tk_wd[ws]], dma=True, semkey=("wd", ws))
            ld_gu(0)
            ld_gu(1)
            ld_wd(0)
            sgc = 0
            NG = NFF // GFF
            for gi in range(NG):
                as_ = gi % 2
                if gi + 1 < NG:
                    ld_wd(gi + 1)
                for jj in range(GFF):
                    jc = gi * GFF + jj
                    if jc + 2 < NFF:
                        ld_gu(jc + 2)
                    ws = jc % NWG
                    for th in range(2):
                        bg, bu = nb(), nb()

                        def mm(e, bg=bg, bu=bu, ws=ws, th=th, h2=h2):
                            ins = None
                            for c in range(NC):
                                ins = e.matmul(banks[bg][:], lhsT=wg[ws][:, c, :], rhs=h2[:, c, th * 512:(th + 1) * 512],
                                               start=(c == 0), stop=(c == NC - 1))
                            for c in range(NC):
                                ins = e.matmul(banks[bu][:], lhsT=wu[ws][:, c, :], rhs=h2[:, c, th * 512:(th + 1) * 512],
                                               start=(c == 0), stop=(c == NC - 1))
                            return ins
                        P.op("pe", mm, reads=[tk_wg[ws], tk_wu[ws]] + tk_h2, writes=[bank_tk[bg], bank_tk[bu]])
                        ss = sgc % 2
                        sgc += 1
                        P.op("act", lambda e, bg=bg, ss=ss: e.activation(out=sg[ss], in_=banks[bg][:], func=AF.Silu),
                             reads=[bank_tk[bg]], writes=[tk_sg[ss]])
                        adst = actT[as_][:, jj, th * 512:(th + 1) * 512]
                        P.op("dve", lambda e, bu=bu, ss=ss, adst=adst: e.tensor_tensor(out=adst, in0=sg[ss], in1=banks[bu][:], op=ALU.mult),
                             reads=[tk_sg[ss], bank_tk[bu]], writes=[tk_act[as_][jj]])
                wsd = gi % 2
                for dc in range(NC):
                    for th in range(2):
                        b = nb()

                        def mmd(e, b=b, dc=dc, th=th, as_=as_, wsd=wsd):
                            ins = None
                            for jj in range(GFF):
                                ins = e.matmul(banks[b][:], lhsT=wd[wsd][:, jj, dc * 128:(dc + 1) * 128],
                                               rhs=actT[as_][:, jj, th * 512:(th + 1) * 512], start=(jj == 0), stop=(jj == GFF - 1))
                            return ins
                        P.op("pe", mmd, reads=[tk_wd[wsd]] + tk_act[as_], writes=[bank_tk[b]])
                        xsl = xT[:, dc, th * 512:(th + 1) * 512]
                        P.op("dve", lambda e, b=b, xsl=xsl: e.tensor_tensor(out=xsl, in0=xsl, in1=banks[b][:], op=ALU.add),
                             reads=[bank_tk[b], tk_x[dc][th]], writes=[tk_x[dc][th]])
            if l == 0:
                dump("x2", xT, [128, NC, T], F32, [t for c in tk_x for t in c])
            phase_end()
            if stop_after == "ffn%d" % l:
                break

    except StopBuild:
        pass

    if stop_after is None:
        yT = sb([NC, T], F32)
        tk_y = [Tk() for _ in range(NC)]
        norm_phase(CF_GFIN, yT, tk_y)
        ot = [sb([D], F32) for _ in range(2)]
        tk_ot = [Tk() for _ in range(2)]
        ev = 0
        for ti in range(8):
            os_ = ti % 2
            for cg in range(4):
                b = nb()

                def tr(e, b=b, cg=cg, ti=ti):
                    ins = None
                    for i in range(4):
                        c = cg * 4 + i
                        ins = e.transpose(out=banks[b][:, i * 128:(i + 1) * 128], in_=yT[:, c, ti * 128:(ti + 1) * 128], identity=ident_f)
                    return ins
                P.op("pe", tr, reads=[tk_y[cg * 4 + i] for i in range(4)] + [tk_cf], writes=[bank_tk[b]])
                dsl = ot[os_][:, cg * 512:(cg + 1) * 512]
                if ev % 2 == 0:
                    P.op("act", lambda e, b=b, dsl=dsl: e.activation(out=dsl, in_=banks[b][:], func=AF.Copy),
                         reads=[bank_tk[b]], writes=[tk_ot[os_]])
                else:
                    P.op("dve", lambda e, b=b, dsl=dsl: e.tensor_copy(out=dsl, in_=banks[b][:]),
                         reads=[bank_tk[b]], writes=[tk_ot[os_]])
                ev += 1
            P.op("sp", lambda e, ti=ti, os_=os_: e.dma_start(out=out_t.ap()[ti * 128:(ti + 1) * 128, :], in_=ot[os_]),
                 reads=[tk_ot[os_]], writes=[tk_out], dma=True, semkey=("ot", os_))
    else:
        zt = sb([D], F32)
        tkz = Tk()
        P.op("dve", lambda e: e.memset(zt, 0.0), writes=[tkz])
        for ti in range(8):
            P.op("sp", lambda e, ti=ti: e.dma_start(out=out_t.ap()[ti * 128:(ti + 1) * 128, :], in_=zt),
                 reads=[tkz], writes=[tk_out], dma=True)
    P.barrier()

    P.emit(nc, stack)
    stack.close()
    return nc, dbg_out


def make_in_maps(inp, need_ffn=True):
    x = np.asarray(inp["x"], np.float32)
    positions = np.asarray(inp["positions"], np.int32)
    w_in = np.asarray(inp["w_in"], np.float32)
    cbv = build_cb()
    cfv = build_cf(inp)
    w_out = np.ascontiguousarray(np.asarray(inp["w_out"], np.float32))
    w_gate = np.ascontiguousarray(np.asarray(inp["w_gate"], np.float32))
    w_up = np.ascontiguousarray(np.asarray(inp["w_up"], np.float32))
    w_down = np.ascontiguousarray(np.asarray(inp["w_down"], np.float32))
    maps = []
    for c in range(8):
        b, g = c // 4, c % 4
        qk_cols, v_cols = [], []
        bases = [(0, g), (1536, g), (3072, 2 * g), (3072, 2 * g + 1)]
        widths = [512, 512, 1024, 1024]
        for (base, h), wdt in zip(bases, widths):
            qk_cols.append(np.arange(base + h * 128, base + (h + 1) * 128))
            qk_cols.append(np.arange(base + wdt + h * 128, base + wdt + (h + 1) * 128))
            v_cols.append(np.arange(base + 2 * wdt + h * 128, base + 2 * wdt + (h + 1) * 128))
        cols = np.concatenate(qk_cols + v_cols)
        maps.append({
            "x": np.ascontiguousarray(x[b, g * T:(g + 1) * T, :]),
            "pos": np.ascontiguousarray(positions[b][None, :]),
            "cb": cbv,
            "cf": cfv,
            "w_in": np.ascontiguousarray(w_in[:, :, cols]),
            "w_out": w_out,
        })
        if need_ffn:
            maps[-1].update({"w_gate": w_gate, "w_up": w_up, "w_down": w_down})
    return maps


_CACHE = {}


def kernel(**inputs):
    if "nc" not in _CACHE:
        _CACHE["nc"] = build_program()[0]
    nc = _CACHE["nc"]
    maps = make_in_maps(inputs)
    res = run_bass_kernel_spmd(nc, maps, core_ids=list(range(8)))
    out = np.zeros((2, S, D), np.float32)
    for c in range(8):
        b, g = c // 4, c % 4
        out[b, g * T:(g + 1) * T, :] = np.asarray(res.results[c]["out"], np.float32)
    return out
```

```python
import math
import os
from contextlib import ExitStack

import numpy as np
import ml_dtypes

import concourse.bass as bass
import concourse.mybir as mybir
from concourse.bass_utils import run_bass_kernel_spmd

F32 = mybir.dt.float32
BF16 = mybir.dt.bfloat16
I32 = mybir.dt.int32
AF = mybir.ActivationFunctionType
ALU = mybir.AluOpType
AX = mybir.AxisListType

D = 2048
S = 4096
T = 1024
NL = 2
DFF = 5632
NFF = DFF // 128
NC = 16
EPS = 1e-6
THETA = 500000.0
PI = math.pi
GFF = 4

CB_ID, CB_ONES, CB_NUINC, CB_NONES, CB_PDIL, CB_PDIFF = 0, 128, 256, 384, 512, 640
CB_TOEP = 768
TOEP_W = 2944
CB_CINC = CB_TOEP + TOEP_W
CB_CSTR = CB_CINC + 896
NB = CB_CSTR + 896
CF_ID = 0
CF_GMIX = 128
CF_GFFN = 160
CF_GFIN = 192
CF_GSB = 208
CF_GDF = 210
CF_GDL = 212
CF_INVF = 214
CF_SGN = 216
CF_EPS = 218
CF_ONE = 219
CF_LNA = 220
CF_LAM = 224
NF = CF_LAM + NL * 4 * 64


def lambda_init(l):
    return 0.8 - 0.6 * math.exp(-0.3 * l)


class StopBuild(Exception):
    pass


class Tk:
    __slots__ = ("name", "w", "r", "multi")

    def __init__(self, name="", multi=False):
        self.name = name
        self.w = [] if multi else None
        self.r = []
        self.multi = multi


class Op:
    __slots__ = ("eng", "fn", "deps", "dma", "semkey", "val", "inc", "target", "sem")

    def __init__(self, eng, fn, dma, inc):
        self.eng = eng
        self.fn = fn
        self.deps = set()
        self.dma = dma
        self.semkey = None
        self.val = 0
        self.inc = inc
        self.target = False
        self.sem = None


class Prog:
    ENGS = ["pe", "act", "dve", "pool", "sp"]

    def __init__(self):
        self.stream = {e: [] for e in self.ENGS}
        self.semcount = {}
        self.since_barrier = []

    def op(self, eng, fn, reads=(), writes=(), dma=False, inc=16, semkey=None, bar=True):
        o = Op(eng, fn, dma, inc)
        deps = set()
        for t in list(reads) + list(writes):
            if t.multi:
                if t not in writes:
                    deps.update(t.w)
            elif t.w is not None:
                deps.add(t.w)
        for t in writes:
            deps.update(t.r)
        o.deps = deps
        for t in writes:
            if t.multi:
                t.w.append(o)
            else:
                t.w = o
                t.r = []
        for t in reads:
            t.r.append(o)
        if dma:
            if semkey is None:
                semkey = id(writes[0])
            o.semkey = semkey
            self.semcount[semkey] = self.semcount.get(semkey, 0) + inc
            o.val = self.semcount[semkey]
        self.stream[eng].append(o)
        if bar:
            self.since_barrier.append(o)
        return o

    def barrier(self):
        lasts = []
        for e in self.ENGS:
            for o in reversed(self.stream[e]):
                if not o.dma and o.fn is not None:
                    lasts.append(o)
                    break
        dmas = [o for o in self.since_barrier if o.dma]
        self.since_barrier = []
        for e in self.ENGS:
            o = Op(e, None, False, 0)
            o.deps = set(lasts) | set(dmas)
            self.stream[e].append(o)

    def emit(self, nc, stack):
        for e in self.ENGS:
            for o in self.stream[e]:
                for d in o.deps:
                    if not d.dma:
                        d.target = True
        engsem = {e: stack.enter_context(nc.semaphore("es_" + e)) for e in self.ENGS}
        keysem = {}
        for e in self.ENGS:
            cnt = 0
            for o in self.stream[e]:
                if o.dma:
                    if o.semkey not in keysem:
                        keysem[o.semkey] = stack.enter_context(nc.semaphore("ds%d" % len(keysem)))
                    o.sem = keysem[o.semkey]
                else:
                    if o.target:
                        cnt += 1
                    o.val = cnt
                    o.sem = engsem[e]
        self.nsem = len(keysem) + len(engsem)
        print('semaphores used:', self.nsem)
        block = stack.enter_context(nc.Block())
        prog = self

        def run(e):
            def body(eng):
                seen = {}
                for o in prog.stream[e]:
                    need = {}
                    for d in o.deps:
                        if e == "pe" and d.eng == "pe" and not d.dma:
                            continue
                        k = id(d.sem)
                        if k not in need or need[k][1] < d.val:
                            need[k] = (d.sem, d.val)
                    for k, (sem, val) in need.items():
                        if seen.get(k, 0) < val:
                            eng.wait_ge(sem, val)
                            seen[k] = val
                    if o.fn is None:
                        continue
                    inst = o.fn(eng)
                    if o.dma:
                        if o.inc == 1:
                            inst.then_inc(o.sem)
                        else:
                            inst.then_inc(o.sem, o.inc)
                    elif o.target:
                        inst.then_inc(o.sem, 1)
            return body

        block.tensor(run("pe"))
        block.scalar(run("act"))
        block.vector(run("dve"))
        block.gpsimd(run("pool"))
        block.sync(run("sp"))


def _dil_count(delta):
    c = np.zeros_like(delta, dtype=np.float32)
    c += ((delta >= 0) & (delta <= 128)).astype(np.float32)
    c += ((delta >= 0) & (delta <= 512) & (delta % 4 == 0)).astype(np.float32)
    c += ((delta >= 0) & (delta <= 2048) & (delta % 16 == 0)).astype(np.float32)
    return c


def build_cb():
    cb = np.zeros((128, NB), np.float32)
    j = np.arange(128)[:, None]
    k = np.arange(128)[None, :]
    cb[:, CB_ID:CB_ID + 128] = np.eye(128)
    cb[:, CB_ONES:CB_ONES + 128] = 1.0
    cb[:, CB_NUINC:CB_NUINC + 128] = -(j >= k).astype(np.float32)
    cb[:, CB_NONES:CB_NONES + 128] = -1.0
    pd = np.zeros((128, 128), np.float32)
    for d in range(32):
        partner = d + 16 if d < 16 else d - 16
        pd[partner, d] = 1.0
    cb[:, CB_PDIL:CB_PDIL + 128] = pd
    pf = np.zeros((128, 128), np.float32)
    for base in (0, 64):
        for d in range(16):
            partner = d + 8 if d < 8 else d - 8
            pf[base + partner, base + d] = 1.0
    cb[:, CB_PDIFF:CB_PDIFF + 128] = pf
    ki = np.arange(128)[:, None]
    xx = np.arange(TOEP_W)[None, :]
    cb[:, CB_TOEP:CB_TOEP + TOEP_W] = _dil_count(xx - 384 - ki)
    xx = np.arange(896)[None, :]
    cb[:, CB_CINC:CB_CINC + 896] = ((xx - 384 - ki) >= 0).astype(np.float32)
    cb[:, CB_CSTR:CB_CSTR + 896] = ((xx - 384 - ki) >= 1).astype(np.float32)
    return cb.astype(ml_dtypes.bfloat16)


def build_cf(inp):
    cf = np.zeros((128, NF), np.float32)
    cf[:, CF_ID:CF_ID + 128] = np.eye(128)

    def cols(v):
        return np.ascontiguousarray(np.asarray(v, np.float32).reshape(16, 128).T)

    for l in range(NL):
        cf[:, CF_GMIX + l * 16:CF_GMIX + (l + 1) * 16] = cols(inp["norm_mix_g"][l])
        cf[:, CF_GFFN + l * 16:CF_GFFN + (l + 1) * 16] = cols(inp["norm_ffn_g"][l])
        cf[:, CF_GSB + l] = np.asarray(inp["g_sb_out"][l], np.float32)
        cf[:, CF_GDF + l] = np.asarray(inp["g_diff_out"][l], np.float32)
        cf[:, CF_GDL + l] = np.asarray(inp["g_dil_out"][l], np.float32)
        cf[:, CF_LNA + l] = math.log(1.0 - lambda_init(l))
        for i, nm in enumerate(("lambda_q1", "lambda_k1", "lambda_q2", "lambda_k2")):
            o = CF_LAM + (l * 4 + i) * 64
            cf[:, o:o + 64] = np.asarray(inp[nm][l], np.float32)[None, :]
    cf[:, CF_GFIN:CF_GFIN + 16] = cols(inp["norm_final_g"])
    invd = np.zeros(128, np.float32)
    sgd = np.zeros(128, np.float32)
    fr = (np.float32(THETA) ** (-np.arange(16, dtype=np.float32) / np.float32(16))).astype(np.float32)
    for d in range(32):
        invd[d] = fr[d % 16]
        sgd[d] = -1.0 if d < 16 else 1.0
    invf = np.zeros(128, np.float32)
    sgf = np.zeros(128, np.float32)
    fr8 = (np.float32(THETA) ** (-np.arange(8, dtype=np.float32) / np.float32(8))).astype(np.float32)
    for base in (0, 64):
        for d in range(16):
            invf[base + d] = fr8[d % 8]
            sgf[base + d] = -1.0 if d < 8 else 1.0
    cf[:, CF_INVF] = invd
    cf[:, CF_INVF + 1] = invf
    cf[:, CF_SGN] = sgd
    cf[:, CF_SGN + 1] = sgf
    cf[:, CF_EPS] = EPS
    cf[:, CF_ONE] = 1.0
    return cf


def build_program(stop_after=None, dbg=None):
    nc = bass.Bass("TRN2", target_bir_lowering=False)
    P = Prog()
    dbg = dbg or []
    dbg_out = {}

    def dram(name, shape, dt, kind=None):
        if kind:
            return nc.dram_tensor(name, shape, dt, kind=kind)
        return nc.dram_tensor(name, shape, dt)

    x_in = dram("x", [T, D], F32, "ExternalInput")
    pos_in = dram("pos", [1, S], I32, "ExternalInput")
    cb_in = dram("cb", [128, NB], BF16, "ExternalInput")
    cf_in = dram("cf", [128, NF], F32, "ExternalInput")
    w_in = dram("w_in", [NL, D, 1536], F32, "ExternalInput")
    w_out = dram("w_out", [NL, D, D], F32, "ExternalInput")
    need_ffn = stop_after is None or stop_after.startswith("ffn") or stop_after.startswith("h1") or stop_after.startswith("att1") or stop_after.startswith("oproj1")
    if need_ffn:
        w_gate = dram("w_gate", [NL, D, DFF], F32, "ExternalInput")
        w_up = dram("w_up", [NL, D, DFF], F32, "ExternalInput")
        w_down = dram("w_down", [NL, DFF, D], F32, "ExternalInput")
    out_t = dram("out", [T, D], F32, "ExternalOutput")

    hbuf_in = [dram("hbuf_in%d" % l, [NC, 128, T], BF16) for l in range(NL)]
    hbuf_all = [dram("hbuf_all%d" % l, [NC, 512, T], BF16) for l in range(NL)]
    mbuf_in = [dram("mbuf_in%d" % l, [4, 4, 128, T], BF16) for l in range(NL)]
    mbuf_all = [dram("mbuf_all%d" % l, [4, 4, 512, T], BF16) for l in range(NL)]
    ropeC = [dram("ropeC%d" % i, [128, S], F32) for i in range(2)]
    ropeS = [dram("ropeS%d" % i, [128, S], F32) for i in range(2)]
    tk_hin = [[Tk("hin") for _ in range(NC)] for _ in range(NL)]
    tk_hall = [Tk("hall", multi=True) for _ in range(NL)]
    tk_min = [Tk("min", multi=True) for _ in range(NL)]
    tk_mall = [Tk("mall", multi=True) for _ in range(NL)]
    tk_rope = Tk("rope", multi=True)
    tk_out = Tk("out", multi=True)
    tk_dbg = Tk("dbg", multi=True)

    stack = ExitStack()
    ARENA_BYTES = int(os.environ.get("ARENA_KB", "200")) * 1024
    arena = stack.enter_context(nc.sbuf_tensor("arena", [128, ARENA_BYTES // 2], BF16))
    psum = stack.enter_context(nc.psum_tensor("psum", [128, 8 * 512], F32))
    banks = [psum[:, i * 512:(i + 1) * 512] for i in range(8)]
    bank_tk = [Tk("bank%d" % i) for i in range(8)]

    class Alloc:
        def __init__(self):
            self.base = 0
            self.top = 0

        def take(self, nbytes):
            nbytes = (nbytes + 63) // 64 * 64
            off = self.top
            self.top += nbytes
            assert self.top <= ARENA_BYTES, ("SBUF overflow", self.top)
            return off

        def persist(self):
            self.base = self.top

        def reset(self):
            self.top = self.base

    A = Alloc()

    def sb(shape, dt):
        n = int(np.prod(shape))
        esz = 2 if dt == BF16 else 4
        off = A.take(n * esz)
        v = arena[:, off // 2: off // 2 + n * esz // 2]
        if dt != BF16:
            v = v.bitcast(dt)
        if len(shape) == 2:
            v = v.rearrange("p (a b) -> p a b", a=shape[0])
        elif len(shape) == 3:
            v = v.rearrange("p (a b c) -> p a b c", a=shape[0], b=shape[1])
        return v

    bank_rr = [0]

    def nb(lo=0, hi=8):
        i = lo + bank_rr[0] % (hi - lo)
        bank_rr[0] += 1
        return i

    def phase_end():
        P.barrier()
        A.reset()

    def dump(name, ap_sb, shape, dt, reads):
        if name not in dbg:
            return
        t = dram("dbg_" + name, shape, dt, "ExternalOutput")
        dbg_out[name] = t
        P.op("sp", lambda e, t=t, a=ap_sb: e.dma_start(out=t.ap(), in_=a), reads=reads, writes=[tk_dbg], dma=True)

    cb = sb([NB], BF16)
    cf = sb([NF], F32)
    xT = sb([NC, T], F32)
    neglam = sb([NL], F32)
    tk_cb, tk_cf, tk_neglam = Tk("cb"), Tk("cf"), Tk("neglam")
    tk_x = [[Tk("xT") for _ in range(2)] for _ in range(NC)]
    A.persist()

    ident_f = cf[:, CF_ID:CF_ID + 128]
    ident_b = cb[:, CB_ID:CB_ID + 128]
    ones_b = cb[:, CB_ONES:CB_ONES + 128]

    def cfc(col):
        return cf[:, col:col + 1]

    try:
        P.op("sp", lambda e: e.dma_start(out=cb, in_=cb_in.ap()), writes=[tk_cb], dma=True)
        P.op("sp", lambda e: e.dma_start(out=cf, in_=cf_in.ap()), writes=[tk_cf], dma=True)

        lt = sb([4, 64], F32)
        ls = sb([8], F32)
        tk_lt, tk_ls = Tk(), Tk()
        for l in range(NL):
            for i in range(2):
                a0 = CF_LAM + (l * 4 + 2 * i) * 64
                P.op("dve", lambda e, i=i, a0=a0: e.tensor_tensor(out=lt[:, i, :], in0=cf[:, a0:a0 + 64],
                                                                   in1=cf[:, a0 + 64:a0 + 128], op=ALU.mult),
                     reads=[tk_cf], writes=[tk_lt])
                P.op("dve", lambda e, i=i: e.reduce_sum(out=ls[:, i:i + 1], in_=lt[:, i, :], axis=AX.X),
                     reads=[tk_lt], writes=[tk_ls])
                P.op("act", lambda e, i=i: e.activation(out=ls[:, 2 + i:3 + i], in_=ls[:, i:i + 1], func=AF.Exp),
                     reads=[tk_ls], writes=[tk_ls])
            P.op("dve", lambda e: e.tensor_tensor(out=ls[:, 4:5], in0=ls[:, 2:3], in1=ls[:, 3:4], op=ALU.subtract),
                 reads=[tk_ls], writes=[tk_ls])
            P.op("dve", lambda e, l=l: e.tensor_scalar(out=neglam[:, l:l + 1], in0=ls[:, 4:5], scalar1=-1.0,
                                                       scalar2=-lambda_init(l), op0=ALU.mult, op1=ALU.add),
                 reads=[tk_ls], writes=[tk_neglam])

        xs = [sb([D], F32) for _ in range(8)]
        tk_xs = [Tk() for _ in range(8)]
        for i in range(8):
            P.op("sp", lambda e, i=i: e.dma_start(out=xs[i], in_=x_in.ap()[i * 128:(i + 1) * 128, :]),
                 writes=[tk_xs[i]], dma=True)
        ev = 0
        for c in range(NC):
            for tg in range(2):
                b = nb()

                def tr(e, c=c, tg=tg, b=b):
                    ins = None
                    for i in range(4):
                        ins = e.transpose(out=banks[b][:, i * 128:(i + 1) * 128],
                                          in_=xs[tg * 4 + i][:, c * 128:(c + 1) * 128], identity=ident_f)
                    return ins
                P.op("pe", tr, reads=[tk_xs[tg * 4 + i] for i in range(4)] + [tk_cf], writes=[bank_tk[b]])
                dst = xT[:, c, tg * 512:(tg + 1) * 512]
                if ev % 2 == 0:
                    P.op("act", lambda e, b=b, dst=dst: e.activation(out=dst, in_=banks[b][:], func=AF.Copy),
                         reads=[bank_tk[b]], writes=[tk_x[c][tg]])
                else:
                    P.op("dve", lambda e, b=b, dst=dst: e.tensor_copy(out=dst, in_=banks[b][:]),
                         reads=[bank_tk[b]], writes=[tk_x[c][tg]])
                ev += 1
        phase_end()
        if stop_after == 'p0a':
            raise StopBuild()

        def rope_phase():
            posi = sb([S], I32)
            posf = sb([S], F32)
            tk_posi, tk_posf = Tk(), Tk()
            P.op("sp", lambda e: e.dma_start(out=posi, in_=pos_in.ap().partition_broadcast(128)), writes=[tk_posi], dma=True)
            P.op("dve", lambda e: e.tensor_copy(out=posf, in_=posi), reads=[tk_posi], writes=[tk_posf])
            RW = 1024
            rt = {k: [sb([RW], I32 if k == "ki" else F32) for _ in range(2)] for k in ("a", "y", "ki", "kf", "r", "m", "sn")}
            rtk = {k: [Tk() for _ in range(2)] for k in rt}
            it = 0
            for ty in range(2):
                for kind in range(2):
                    for ch in range(S // RW):
                        s_ = it % 2
                        it += 1
                        cs = slice(ch * RW, (ch + 1) * RW)
                        a, y, ki, kf, r, m, sn = (rt[k][s_] for k in ("a", "y", "ki", "kf", "r", "m", "sn"))
                        ta, ty_, tki, tkf, tr_, tm, tsn = (rtk[k][s_] for k in ("a", "y", "ki", "kf", "r", "m", "sn"))
                        off = PI / 2 if kind == 0 else 0.0
                        P.op("dve", lambda e, a=a, cs=cs, ty=ty, off=off: e.tensor_scalar(
                            out=a, in0=posf[:, cs], scalar1=cfc(CF_INVF + ty), scalar2=off, op0=ALU.mult, op1=ALU.add),
                            reads=[tk_posf, tk_cf], writes=[ta])
                        P.op("dve", lambda e, a=a, y=y: e.tensor_scalar(out=y, in0=a, scalar1=1.0 / (2 * PI), scalar2=None,
                                                                         op0=ALU.mult), reads=[ta], writes=[ty_])
                        P.op("dve", lambda e, y=y, ki=ki: e.tensor_copy(out=ki, in_=y), reads=[ty_], writes=[tki])
                        P.op("dve", lambda e, kf=kf, ki=ki: e.tensor_copy(out=kf, in_=ki), reads=[tki], writes=[tkf])
                        P.op("dve", lambda e, kf=kf, a=a, r=r: e.scalar_tensor_tensor(
                            out=r, in0=kf, scalar=-2 * PI, in1=a, op0=ALU.mult, op1=ALU.add), reads=[tkf, ta], writes=[tr_])
                        P.op("dve", lambda e, r=r, m=m: e.tensor_scalar(out=m, in0=r, scalar1=PI, scalar2=2 * PI,
                                                                         op0=ALU.is_gt, op1=ALU.mult), reads=[tr_], writes=[tm])
                        P.op("dve", lambda e, r=r, m=m: e.tensor_tensor(out=r, in0=r, in1=m, op=ALU.subtract),
                             reads=[tm, tr_], writes=[tr_])
                        P.op("dve", lambda e, r=r: e.tensor_scalar(out=r, in0=r, scalar1=PI, scalar2=-PI,
                                                                    op0=ALU.min, op1=ALU.max), reads=[tr_], writes=[tr_])
                        P.op("act", lambda e, r=r, sn=sn: e.activation(out=sn, in_=r, func=AF.Sin), reads=[tr_], writes=[tsn])
                        if kind == 1:
                            P.op("dve", lambda e, sn=sn, ty=ty: e.tensor_scalar(out=sn, in0=sn, scalar1=cfc(CF_SGN + ty),
                                                                                 scalar2=None, op0=ALU.mult),
                                 reads=[tsn, tk_cf], writes=[tsn])
                        dst = (ropeC if kind == 0 else ropeS)[ty]
                        P.op("sp", lambda e, dst=dst, cs=cs, sn=sn: e.dma_start(out=dst.ap()[:, cs], in_=sn),
                             reads=[tsn], writes=[tk_rope], dma=True, semkey=("sn", s_))
            phase_end()

        def norm_phase(gcol, hT, tk_h, out_dt_is_bf16=True):
            sq = [sb([T], BF16) for _ in range(2)]
            tk_sq = [Tk() for _ in range(2)]
            lnv = sb([T], F32)
            rstd = sb([T], F32)
            tk_lnv, tk_rstd = Tk(), Tk()
            b0, b1 = nb(), nb()
            bs = (b0, b1)
            for c in range(NC):
                s_ = c % 2
                P.op("act", lambda e, c=c, s_=s_: e.activation(out=sq[s_], in_=xT[:, c, :], func=AF.Square),
                     reads=[tk_x[c][0], tk_x[c][1]], writes=[tk_sq[s_]])

                def mm(e, c=c, s_=s_):
                    ins = None
                    for th in range(2):
                        ins = e.matmul(banks[bs[th]][:], lhsT=ones_b, rhs=sq[s_][:, th * 512:(th + 1) * 512],
                                       start=(c == 0), stop=(c == NC - 1))
                    return ins
                P.op("pe", mm, reads=[tk_sq[s_], tk_cb], writes=[bank_tk[b0], bank_tk[b1]])
            for th in range(2):
                P.op("act", lambda e, th=th: e.activation(out=lnv[:, th * 512:(th + 1) * 512], in_=banks[bs[th]][:],
                                                         func=AF.Ln, bias=cfc(CF_EPS), scale=1.0 / D),
                     reads=[bank_tk[bs[th]], tk_cf], writes=[tk_lnv])
            P.op("act", lambda e: e.activation(out=rstd, in_=lnv, func=AF.Exp, scale=-0.5), reads=[tk_lnv], writes=[tk_rstd])
            for c in range(NC):
                P.op("dve", lambda e, c=c: e.scalar_tensor_tensor(out=hT[:, c, :], in0=xT[:, c, :], scalar=cfc(gcol + c),
                                                                  in1=rstd, op0=ALU.mult, op1=ALU.mult),
                     reads=[tk_x[c][0], tk_x[c][1], tk_rstd, tk_cf], writes=[tk_h[c]])

        hn_calls = [0]

        def head_norm_steps(o, tk_o, gcol, lnbias_col, dst_dram_ap, tk_dst, ring, msb=4):
            sq, tk_sq, ln_, tk_ln, rs, tk_rs, mo, tk_mo = ring
            b = msb
            hn_calls[0] += 1
            skey = ("mo", hn_calls[0] % 2)

            def step_a():
                P.op("act", lambda e: e.activation(out=sq, in_=o, func=AF.Square), reads=[tk_o], writes=[tk_sq])
                P.op("pe", lambda e, b=b: e.matmul(banks[b][:], lhsT=ones_b, rhs=sq, start=True, stop=True),
                     reads=[tk_sq, tk_cb], writes=[bank_tk[b]])

            def step_b():
                P.op("act", lambda e, b=b: e.activation(out=ln_, in_=banks[b][:], func=AF.Ln, bias=cfc(CF_EPS), scale=1.0 / 128),
                     reads=[bank_tk[b], tk_cf], writes=[tk_ln])
                if lnbias_col is None:
                    P.op("act", lambda e: e.activation(out=rs, in_=ln_, func=AF.Exp, scale=-0.5), reads=[tk_ln], writes=[tk_rs])
                else:
                    P.op("act", lambda e: e.activation(out=rs, in_=ln_, func=AF.Exp, scale=-0.5, bias=cfc(lnbias_col)),
                         reads=[tk_ln, tk_cf], writes=[tk_rs])
                P.op("dve", lambda e: e.scalar_tensor_tensor(out=mo, in0=o, scalar=cfc(gcol), in1=rs, op0=ALU.mult, op1=ALU.mult),
                     reads=[tk_o, tk_rs, tk_cf], writes=[tk_mo])
                P.op("sp", lambda e: e.dma_start(out=dst_dram_ap, in_=mo), reads=[tk_mo], writes=[tk_dst], dma=True, semkey=skey)
            return [step_a, step_b]

        def head_norm(*args, **kw):
            for st in head_norm_steps(*args, **kw):
                st()

        for l in range(NL):
            saved_base = A.base
            wq_ring = [sb([3, NC, 128], BF16) for _ in range(2)]
            tk_w_ring = [[Tk() for _ in range(3)] for _ in range(2)]
            A.persist()

            def load_wq(s_, l=l, wq_ring=wq_ring, tk_w_ring=tk_w_ring):
                wcols_ = (2 * s_ * 128, (2 * s_ + 1) * 128, 1024 + s_ * 128)
                for i in range(3):
                    P.op("pool", lambda e, i=i, wcols_=wcols_, s_=s_: e.dma_start(
                        out=wq_ring[s_ % 2][:, i], in_=w_in.ap()[l, :, wcols_[i]:wcols_[i] + 128].rearrange("(c p) n -> p c n", p=128)),
                        writes=[tk_w_ring[s_ % 2][i]], dma=True, semkey=("wq", s_ % 2, i))
            load_wq(0)
            hT = sb([NC, T], BF16)
            tk_h = [Tk() for _ in range(NC)]
            norm_phase(CF_GMIX + l * 16, hT, tk_h)
            for c in range(NC):
                P.op("sp", lambda e, l=l, hT=hT, c=c: e.dma_start(out=hbuf_in[l].ap()[c], in_=hT[:, c, :]),
                     reads=[tk_h[c]], writes=[tk_hin[l][c]], dma=True, semkey=("hin", c))
                P.op("pool", lambda e, l=l, c=c: e.collective_compute("AllGather", ALU.bypass, replica_groups=[[0, 1, 2, 3], [4, 5, 6, 7]],
                                                                     ins=[hbuf_in[l].ap()[c]], outs=[hbuf_all[l].ap()[c]]),
                     reads=[tk_hin[l][c]], writes=[tk_hall[l]], dma=True, inc=1, bar=False)
            if l == 0:
                dump("h0", hT, [128, NC, T], BF16, tk_h)
            phase_end()
            if l == 0:
                rope_phase()
            if stop_after == "h%d" % l:
                break

            for s in range(4):
                kind = ("sb", "diff", "dil", "dil")[s]
                rope_ty = {"sb": None, "diff": 1, "dil": 0}[kind]
                qscale = 0.125 if kind == "diff" else 128 ** -0.5
                perm = None if rope_ty is None else cb[:, (CB_PDIL if rope_ty == 0 else CB_PDIFF):][:, 0:128]

                wq = wq_ring[s % 2]
                tk_w = tk_w_ring[s % 2]
                qT = sb([S], BF16)
                kT = sb([S], BF16)
                vv = sb([32, 128], BF16)
                tk_q = [Tk() for _ in range(8)]
                tk_k = [Tk() for _ in range(8)]
                tk_v = [Tk() for _ in range(8)]
                mark_ = A.top
                ht = [sb([NC, 512], BF16) for _ in range(2)]
                tk_ht = [Tk() for _ in range(2)]
                rC = [sb([512], F32) for _ in range(2)]
                rS = [sb([512], F32) for _ in range(2)]
                tk_rC = [Tk() for _ in range(2)]
                tk_rS = [Tk() for _ in range(2)]
                raw = [sb([512], BF16) for _ in range(2)]
                tk_raw = [Tk() for _ in range(2)]
                t1 = [sb([512], F32) for _ in range(2)]
                t2 = [sb([512], F32) for _ in range(2)]
                tk_t1 = [Tk() for _ in range(2)]
                tk_t2 = [Tk() for _ in range(2)]

                ri = 0
                for tt in range(8):
                    hs = tt % 2
                    r_, off = tt // 2, (tt % 2) * 512
                    P.op("sp", lambda e, l=l, r_=r_, off=off, hs=hs, ht=ht: e.dma_start(
                        out=ht[hs], in_=hbuf_all[l].ap()[:, r_ * 128:(r_ + 1) * 128, off:off + 512].rearrange("c p t -> p c t")),
                        reads=[tk_hall[l]], writes=[tk_ht[hs]], dma=True, semkey=("ht", hs))
                    if rope_ty is not None:
                        P.op("sp", lambda e, tt=tt, hs=hs, rC=rC, rope_ty=rope_ty: e.dma_start(
                            out=rC[hs], in_=ropeC[rope_ty].ap()[:, tt * 512:(tt + 1) * 512]),
                            reads=[tk_rope], writes=[tk_rC[hs]], dma=True, semkey=("rC", hs))
                        P.op("sp", lambda e, tt=tt, hs=hs, rS=rS, rope_ty=rope_ty: e.dma_start(
                            out=rS[hs], in_=ropeS[rope_ty].ap()[:, tt * 512:(tt + 1) * 512]),
                            reads=[tk_rope], writes=[tk_rS[hs]], dma=True, semkey=("rS", hs))
                    post = []
                    for qi, (dstT, tkd, sc) in enumerate(((qT, tk_q, qscale), (kT, tk_k, 1.0))):
                        b = nb()

                        def mm(e, b=b, qi=qi, hs=hs, wq=wq, ht=ht):
                            ins = None
                            for c in range(NC):
                                ins = e.matmul(banks[b][:], lhsT=wq[:, qi, c, :], rhs=ht[hs][:, c, :],
                                               start=(c == 0), stop=(c == NC - 1))
                            return ins
                        P.op("pe", mm, reads=[tk_w[qi], tk_ht[hs]], writes=[bank_tk[b]])
                        dst = dstT[:, tt * 512:(tt + 1) * 512]
                        if rope_ty is None:
                            P.op("act", lambda e, b=b, dst=dst, sc=sc: e.activation(out=dst, in_=banks[b][:], func=AF.Copy, scale=sc),
                                 reads=[bank_tk[b]], writes=[tkd[tt]])
                        else:
                            rs_ = ri % 2
                            ri += 1
                            P.op("act", lambda e, b=b, rs_=rs_, sc=sc, raw=raw: e.activation(out=raw[rs_], in_=banks[b][:], func=AF.Copy, scale=sc),
                                 reads=[bank_tk[b]], writes=[tk_raw[rs_]])
                            def post_fn(rs_=rs_, hs=hs, dst=dst, tkd=tkd, tt=tt, perm=perm, raw=raw, t1=t1, t2=t2, rS=rS, rC=rC,
                                        tk_raw=tk_raw, tk_t1=tk_t1, tk_t2=tk_t2, tk_rS=tk_rS, tk_rC=tk_rC):
                                b2 = nb()
                                P.op("pe", lambda e, b2=b2, rs_=rs_, perm=perm, raw=raw: e.matmul(banks[b2][:], lhsT=perm, rhs=raw[rs_], start=True, stop=True),
                                     reads=[tk_raw[rs_], tk_cb], writes=[bank_tk[b2]])
                                P.op("dve", lambda e, b2=b2, rs_=rs_, hs=hs, t1=t1, rS=rS: e.tensor_tensor(out=t1[rs_], in0=banks[b2][:], in1=rS[hs], op=ALU.mult),
                                     reads=[bank_tk[b2], tk_rS[hs]], writes=[tk_t1[rs_]])
                                P.op("pool", lambda e, rs_=rs_, hs=hs, t2=t2, raw=raw, rC=rC: e.tensor_tensor(out=t2[rs_], in0=raw[rs_], in1=rC[hs], op=ALU.mult),
                                     reads=[tk_raw[rs_], tk_rC[hs]], writes=[tk_t2[rs_]])
                                P.op("dve", lambda e, rs_=rs_, dst=dst, t1=t1, t2=t2: e.tensor_tensor(out=dst, in0=t1[rs_], in1=t2[rs_], op=ALU.add),
                                     reads=[tk_t1[rs_], tk_t2[rs_]], writes=[tkd[tt]])
                            post.append(post_fn)
                    b = nb()

                    def mmv(e, b=b, hs=hs, wq=wq, ht=ht):
                        ins = None
                        for sub in range(4):
                            for c in range(NC):
                                ins = e.matmul(banks[b][:, sub * 128:(sub + 1) * 128], lhsT=ht[hs][:, c, sub * 128:(sub + 1) * 128],
                                               rhs=wq[:, 2, c, :], start=(c == 0), stop=(c == NC - 1))
                        return ins
                    P.op("pe", mmv, reads=[tk_w[2], tk_ht[hs]], writes=[bank_tk[b]])
                    P.op("dve", lambda e, b=b, tt=tt, vv=vv: e.tensor_copy(out=vv[:, tt * 4:(tt + 1) * 4, :],
                                                                            in_=banks[b][:].rearrange("p (a b) -> p a b", a=4)),
                         reads=[bank_tk[b]], writes=[tk_v[tt]])
                    for pf_ in post:
                        pf_()
                if l == 0 and s in (0, 1, 2):
                    dump("q%d" % s, qT, [128, S], BF16, tk_q)
                    dump("k%d" % s, kT, [128, S], BF16, tk_k)
                    dump("v%d" % s, vv, [128, 32, 128], BF16, tk_v)

                P.barrier()
                A.top = mark_
                if s + 1 < 4:
                    load_wq(s + 1)
                NE2 = 3
                E2 = [sb([1024], BF16) for _ in range(NE2)]
                Em2 = [sb([1024], BF16) for _ in range(NE2)]
                tk_E2 = [Tk() for _ in range(NE2)]
                tk_Em2 = [[Tk(), Tk()] for _ in range(NE2)]
                pcnt = [0]
                oacc = [sb([512], F32) for _ in range(3)]
                tk_oacc = [Tk() for _ in range(3)]
                rl = sb([512], F32)
                tk_rl = Tk()
                hn_ring = []
                for _ in range(2):
                    hn_ring.append((sb([512], BF16), Tk(), sb([512], F32), Tk(), sb([512], F32), Tk(), sb([512], BF16), Tk()))
                if kind == "sb":
                    ef2 = [sb([1024], F32) for _ in range(2)]
                    tk_ef2 = [Tk() for _ in range(2)]
                    sp2 = [sb([1024], BF16) for _ in range(3)]
                    tk_sp2 = [Tk() for _ in range(3)]
                    spm2 = [sb([1024], BF16) for _ in range(3)]
                    tk_spm2 = [[Tk(), Tk()] for _ in range(3)]
                    Ts = [sb([512], BF16) for _ in range(4)]
                    tk_Ts = [Tk() for _ in range(4)]
                    tmpT = [sb([512], BF16) for _ in range(2)]
                    tk_tmpT = [Tk() for _ in range(2)]
                OB, LB = 6, 7
                ecnt = [0]
                hcnt = [0]

                pending = []
                olcnt = [0]
                if kind != "sb":
                    lnl = [sb([512], F32) for _ in range(2)]
                    tk_lnl = [Tk() for _ in range(2)]
                    oraw = [sb([512], F32) for _ in range(2)]
                    tk_oraw = [Tk() for _ in range(2)]

                def flush_pending():
                    while pending:
                        pending.pop(0)()

                def run_pending_step():
                    if pending:
                        pending.pop(0)()

                def softmax_pass(j, parts, ktiles, maskfn, o_dst, tk_odst, after=None):
                    p0, p1 = parts
                    qsl = qT[p0:p1, j * 512:(j + 1) * 512]
                    npair = len(ktiles) // 2
                    assert len(ktiles) % 2 == 0
                    sb_of = {}
                    OB, LB = 6, 7
                    ek = olcnt[0] % 2
                    olcnt[0] += 1

                    def issue_s(pi):
                        pb = (pcnt[0] % 2) * 2
                        pcnt[0] += 1
                        sb_of[pi] = pb
                        i0, i1 = ktiles[2 * pi], ktiles[2 * pi + 1]

                        def mm(e, pb=pb, i0=i0, i1=i1):
                            e.matmul(banks[pb][:], lhsT=kT[p0:p1, i0 * 128:(i0 + 1) * 128], rhs=qsl, start=True, stop=True)
                            return e.matmul(banks[pb + 1][:], lhsT=kT[p0:p1, i1 * 128:(i1 + 1) * 128], rhs=qsl, start=True, stop=True)
                        P.op("pe", mm, reads=[tk_k[i0 // 4], tk_k[i1 // 4], tk_q[j]], writes=[bank_tk[pb], bank_tk[pb + 1]])
                    issue_s(0)
                    if npair > 1:
                        issue_s(1)
                    for pi in range(npair):
                        if pi >= 1:
                            run_pending_step()
                        pb = sb_of.pop(pi)
                        es = ecnt[0] % NE2
                        ecnt[0] += 1
                        P.op("act", lambda e, pb=pb, es=es: e.activation(out=E2[es], in_=psum[:, pb * 512:(pb + 2) * 512], func=AF.Exp),
                             reads=[bank_tk[pb], bank_tk[pb + 1]], writes=[tk_E2[es]])
                        srcs = []
                        for h_ in range(2):
                            i = ktiles[2 * pi + h_]
                            hsl = slice(h_ * 512, (h_ + 1) * 512)
                            mk = maskfn(i)
                            if mk is not None:
                                P.op("dve", lambda e, es=es, mk=mk, hsl=hsl: e.tensor_tensor(out=Em2[es][:, hsl], in0=E2[es][:, hsl], in1=mk, op=ALU.mult),
                                     reads=[tk_E2[es], tk_cb], writes=[tk_Em2[es][h_]])
                                srcs.append((Em2[es][:, hsl], tk_Em2[es][h_], i))
                            else:
                                srcs.append((E2[es][:, hsl], tk_E2[es], i))

                        if pi + 2 < npair:
                            issue_s(pi + 2)

                        def mmo(e, srcs=srcs, pi=pi, OB=OB, LB=LB):
                            first, last = (pi == 0), (pi == npair - 1)
                            (s0, _, i0), (s1, _, i1) = srcs
                            e.matmul(banks[OB][:], lhsT=vv[:, i0, :], rhs=s0, start=first, stop=False)
                            e.matmul(banks[OB][:], lhsT=vv[:, i1, :], rhs=s1, start=False, stop=last)
                            e.matmul(banks[LB][:], lhsT=ones_b, rhs=s0, start=first, stop=False)
                            return e.matmul(banks[LB][:], lhsT=ones_b, rhs=s1, start=False, stop=last)
                        P.op("pe", mmo, reads=[srcs[0][1], srcs[1][1], tk_v[srcs[0][2] // 4], tk_v[srcs[1][2] // 4], tk_cb],
                             writes=[bank_tk[OB], bank_tk[LB]])
                    flush_pending()
                    P.op("act", lambda e, ek=ek: e.activation(out=lnl[ek], in_=banks[LB][:], func=AF.Ln), reads=[bank_tk[LB]], writes=[tk_lnl[ek]])
                    P.op("dve", lambda e, ek=ek: e.tensor_copy(out=oraw[ek], in_=banks[OB][:]), reads=[bank_tk[OB]], writes=[tk_oraw[ek]])

                    def epilogue(ek=ek):
                        P.op("act", lambda e: e.activation(out=rl, in_=lnl[ek], func=AF.Exp, scale=-1.0), reads=[tk_lnl[ek]], writes=[tk_rl])
                        P.op("dve", lambda e: e.tensor_tensor(out=o_dst, in0=oraw[ek], in1=rl, op=ALU.mult),
                             reads=[tk_oraw[ek], tk_rl], writes=[tk_odst])
                    pending.append(epilogue)
                    if after is not None:
                        pending.extend(after())

                def sb_pass(j, o_dst, tk_odst):
                    tiles = list(range(4 * j + 3, -1, -1))
                    npair = len(tiles) // 2
                    cstr = cb[:, CB_CSTR:CB_CSTR + 896]
                    nuinc = cb[:, CB_NUINC:CB_NUINC + 128]
                    nones = cb[:, CB_NONES:CB_NONES + 128]
                    qsl = qT[:, j * 512:(j + 1) * 512]
                    SOB = 7
                    st1 = {}
                    st2 = {}

                    def mask_of(i):
                        if i >= 4 * j:
                            Dq = 512 * j - 128 * i
                            return cstr[:, Dq + 384:Dq + 384 + 512]
                        return None

                    def stage1(p):
                        iA, iB = tiles[2 * p], tiles[2 * p + 1]

                        def mmz(e, iA=iA, iB=iB):
                            e.matmul(banks[0][:], lhsT=kT[:, iA * 128:(iA + 1) * 128], rhs=qsl, start=True, stop=True)
                            return e.matmul(banks[1][:], lhsT=kT[:, iB * 128:(iB + 1) * 128], rhs=qsl, start=True, stop=True)
                        P.op("pe", mmz, reads=[tk_k[iA // 4], tk_k[iB // 4], tk_q[j]], writes=[bank_tk[0], bank_tk[1]])
                        fs = p % 2
                        P.op("act", lambda e, fs=fs: e.activation(out=ef2[fs], in_=psum[:, 0:1024], func=AF.Exp),
                             reads=[bank_tk[0], bank_tk[1]], writes=[tk_ef2[fs]])
                        ss = p % 3
                        P.op("act", lambda e, fs=fs, ss=ss: e.activation(out=sp2[ss], in_=ef2[fs], func=AF.Ln, bias=cfc(CF_ONE)),
                             reads=[tk_ef2[fs], tk_cf], writes=[tk_sp2[ss]])
                        curs = []
                        for h_, i in enumerate((iA, iB)):
                            hsl = slice(h_ * 512, (h_ + 1) * 512)
                            mk = mask_of(i)
                            if mk is not None:
                                P.op("dve", lambda e, ss=ss, mk=mk, hsl=hsl: e.tensor_tensor(out=spm2[ss][:, hsl], in0=sp2[ss][:, hsl], in1=mk, op=ALU.mult),
                                     reads=[tk_sp2[ss], tk_cb], writes=[tk_spm2[ss][h_]])
                                curs.append((spm2[ss][:, hsl], tk_spm2[ss][h_]))
                            else:
                                curs.append((sp2[ss][:, hsl], tk_sp2[ss]))
                        tprev, tnew = p % 4, (p + 1) % 4
                        if p + 1 < npair:
                            (cA, tA), (cB, tB) = curs
                            if p == 0:
                                P.op("dve", lambda e, cA=cA, cB=cB, tnew=tnew: e.tensor_tensor(out=Ts[tnew], in0=cA, in1=cB, op=ALU.add),
                                     reads=[tA, tB], writes=[tk_Ts[tnew]])
                            else:
                                ts_ = p % 2
                                P.op("dve", lambda e, cA=cA, cB=cB, ts_=ts_: e.tensor_tensor(out=tmpT[ts_], in0=cA, in1=cB, op=ALU.add),
                                     reads=[tA, tB], writes=[tk_tmpT[ts_]])
                                P.op("dve", lambda e, ts_=ts_, tprev=tprev, tnew=tnew: e.tensor_tensor(out=Ts[tnew], in0=Ts[tprev], in1=tmpT[ts_], op=ALU.add),
                                     reads=[tk_tmpT[ts_], tk_Ts[tprev]], writes=[tk_Ts[tnew]])
                        st1[p] = (curs, tprev, iA, iB)

                    def stage2(p):
                        curs, tprev, iA, iB = st1.pop(p)
                        (cA, tA), (cB, tB) = curs
                        lb = 2 + 2 * (p % 2)

                        def mml(e, lb=lb, iA=iA, iB=iB, cA=cA, cB=cB, tprev=tprev, p=p):
                            e.matmul(banks[lb][:], lhsT=kT[:, iA * 128:(iA + 1) * 128], rhs=qsl, start=True, stop=False)
                            ins = e.matmul(banks[lb][:], lhsT=nuinc, rhs=cA, start=False, stop=(p == 0))
                            if p > 0:
                                ins = e.matmul(banks[lb][:], lhsT=nones, rhs=Ts[tprev], start=False, stop=True)
                            e.matmul(banks[lb + 1][:], lhsT=kT[:, iB * 128:(iB + 1) * 128], rhs=qsl, start=True, stop=False)
                            e.matmul(banks[lb + 1][:], lhsT=nuinc, rhs=cB, start=False, stop=False)
                            ins = e.matmul(banks[lb + 1][:], lhsT=nones, rhs=cA, start=False, stop=(p == 0))
                            if p > 0:
                                ins = e.matmul(banks[lb + 1][:], lhsT=nones, rhs=Ts[tprev], start=False, stop=True)
                            return ins
                        rd = [tk_k[iA // 4], tk_k[iB // 4], tk_q[j], tA, tB, tk_cb] + ([tk_Ts[tprev]] if p > 0 else [])
                        P.op("pe", mml, reads=rd, writes=[bank_tk[lb], bank_tk[lb + 1]])
                        es = ecnt[0] % NE2
                        ecnt[0] += 1
                        P.op("act", lambda e, lb=lb, es=es: e.activation(out=E2[es], in_=psum[:, lb * 512:(lb + 2) * 512], func=AF.Exp),
                             reads=[bank_tk[lb], bank_tk[lb + 1]], writes=[tk_E2[es]])
                        srcs = []
                        for h_, i in enumerate((iA, iB)):
                            hsl = slice(h_ * 512, (h_ + 1) * 512)
                            mk = mask_of(i)
                            if mk is not None:
                                P.op("dve", lambda e, es=es, mk=mk, hsl=hsl: e.tensor_tensor(out=Em2[es][:, hsl], in0=E2[es][:, hsl], in1=mk, op=ALU.mult),
                                     reads=[tk_E2[es], tk_cb], writes=[tk_Em2[es][h_]])
                                srcs.append((Em2[es][:, hsl], tk_Em2[es][h_], i))
                            else:
                                srcs.append((E2[es][:, hsl], tk_E2[es], i))
                        st2[p] = srcs

                    def stage2b(p):
                        (s0, t0_, i0), (s1, t1_, i1) = st2.pop(p)

                        def mmav(e, s0=s0, s1=s1, i0=i0, i1=i1, p=p):
                            e.matmul(banks[SOB][:], lhsT=vv[:, i0, :], rhs=s0, start=(p == 0), stop=False)
                            return e.matmul(banks[SOB][:], lhsT=vv[:, i1, :], rhs=s1, start=False, stop=(p == npair - 1))
                        P.op("pe", mmav, reads=[t0_, t1_, tk_v[i0 // 4], tk_v[i1 // 4]], writes=[bank_tk[SOB]])

                    stage1(0)
                    if npair > 1:
                        stage1(1)
                    for p in range(npair):
                        stage2(p)
                        if p + 2 < npair:
                            stage1(p + 2)
                        if p >= 1:
                            stage2b(p - 1)
                            run_pending_step()
                    stage2b(npair - 1)
                    flush_pending()
                    P.op("act", lambda e: e.activation(out=o_dst, in_=banks[SOB][:], func=AF.Copy), reads=[bank_tk[SOB]], writes=[tk_odst])

                cinc = cb[:, CB_CINC:CB_CINC + 896]
                toep = cb[:, CB_TOEP:CB_TOEP + TOEP_W]
                def quarter_collective(j, l=l, s=s):
                    if j % 2 == 1:
                        qt = j // 2
                        P.op("pool", lambda e, l=l, qt=qt, s=s: e.collective_compute("AllGather", ALU.bypass, replica_groups=[[0, 1, 2, 3], [4, 5, 6, 7]],
                                                                                    ins=[mbuf_in[l].ap()[qt, s]], outs=[mbuf_all[l].ap()[qt, s]]),
                             reads=[tk_min[l]], writes=[tk_mall[l]], dma=True, inc=1, bar=False)

                for j in range(8):
                    dst = mbuf_in[l].ap()[j // 2, s, :, (j % 2) * 512:(j % 2) * 512 + 512]
                    ring = hn_ring[hcnt[0] % 2]
                    hcnt[0] += 1
                    if kind == "sb":
                        oslot = j % 2
                        sb_pass(j, oacc[oslot], tk_oacc[oslot])
                        sa, sb_ = head_norm_steps(oacc[oslot], tk_oacc[oslot], CF_GSB + l, None, dst, tk_min[l], ring, msb=6)

                        def s3(sb_=sb_, j=j):
                            sb_()
                            quarter_collective(j)
                        pending.extend([sa, s3])
                    elif kind == "diff":
                        def mk_diff(i, j=j):
                            if i >= 4 * j:
                                Dq = 512 * j - 128 * i
                                return cinc[:, Dq + 384:Dq + 384 + 512]
                            return None
                        kt = list(range(0, 4 * j + 4))

                        def after_diff(j=j, dst=dst, ring=ring, l=l):
                            sa, sb_ = head_norm_steps(oacc[2], tk_oacc[2], CF_GDF + l, CF_LNA + l, dst, tk_min[l], ring, msb=4)

                            def s2():
                                P.op("dve", lambda e, l=l: e.scalar_tensor_tensor(out=oacc[2], in0=oacc[1], scalar=neglam[:, l:l + 1], in1=oacc[0],
                                                                                 op0=ALU.mult, op1=ALU.add),
                                     reads=[tk_oacc[0], tk_oacc[1], tk_neglam], writes=[tk_oacc[2]])
                                sa()

                            def s3():
                                sb_()
                                quarter_collective(j)
                            return [s2, s3]
                        softmax_pass(j, (0, 64), kt, mk_diff, oacc[0], tk_oacc[0])
                        softmax_pass(j, (64, 128), kt, mk_diff, oacc[1], tk_oacc[1], after=after_diff)
                    else:
                        def mk_dil(i, j=j):
                            Dq = 512 * j - 128 * i
                            return toep[:, Dq + 384:Dq + 384 + 512]
                        kt = list(range(max(0, 4 * j - 16), 4 * j + 4))
                        oslot = j % 2

                        def after_dil(j=j, dst=dst, ring=ring, l=l, oslot=oslot):
                            sa, sb_ = head_norm_steps(oacc[oslot], tk_oacc[oslot], CF_GDL + l, None, dst, tk_min[l], ring, msb=4)

                            def s3():
                                sb_()
                                quarter_collective(j)
                            return [sa, s3]
                        softmax_pass(j, (0, 128), kt, mk_dil, oacc[oslot], tk_oacc[oslot], after=after_dil)
                flush_pending()
                phase_end()
                if stop_after == "att%d_%d" % (l, s):
                    break
            if stop_after is not None and stop_after.startswith("att%d" % l):
                break

            A.base = saved_base
            A.top = saved_base
            mx = sb([NC, T], BF16)
            tk_mx = Tk()

            def ld_mx(e, l=l, mx=mx):
                me = e.partition_id() % 4
                return e.dma_start(out=mx, in_=mbuf_all[l].ap()[bass.ds(me, 1)].rearrange("o s (r p) t -> p (o s r) t", p=128))
            if l == 0:
                dump("mx0", mx, [128, NC, T], BF16, [tk_mx])
            NWO = 3
            wo = [sb([NC, 128], BF16) for _ in range(NWO)]
            tk_wo = [Tk() for _ in range(NWO)]

            def mxchunk(n):
                if n < 4:
                    return n
                if n < 8:
                    return 4 + (n - 4)
                m = n - 8
                return (2 + m % 2) * 4 + m // 2

            def ld_wo(dc):
                ws = dc % NWO
                P.op("pool", lambda e, dc=dc, ws=ws, l=l: e.dma_start(
                    out=wo[ws], in_=w_out.ap()[l, :, dc * 128:(dc + 1) * 128].rearrange("(n p) c -> p n c", p=128)),
                    writes=[tk_wo[ws]], dma=True, semkey=("wo", ws))
            ld_wo(0)
            ld_wo(1)
            P.op("pool", ld_mx, reads=[tk_mall[l]], writes=[tk_mx], dma=True, semkey=("mx",))
            for dc in range(NC):
                if dc + 2 < NC:
                    ld_wo(dc + 2)
                ws = dc % NWO
                for th in range(2):
                    b = nb()

                    def mm(e, b=b, ws=ws, th=th, mx=mx):
                        ins = None
                        for n_ in range(NC):
                            ins = e.matmul(banks[b][:], lhsT=wo[ws][:, n_, :], rhs=mx[:, mxchunk(n_), th * 512:(th + 1) * 512],
                                           start=(n_ == 0), stop=(n_ == NC - 1))
                        return ins
                    P.op("pe", mm, reads=[tk_wo[ws], tk_mx], writes=[bank_tk[b]])
                    xsl = xT[:, dc, th * 512:(th + 1) * 512]
                    P.op("dve", lambda e, b=b, xsl=xsl: e.tensor_tensor(out=xsl, in0=xsl, in1=banks[b][:], op=ALU.add),
                         reads=[bank_tk[b], tk_x[dc][th]], writes=[tk_x[dc][th]])
            if l == 0:
                dump("x1", xT, [128, NC, T], F32, [t for c in tk_x for t in c])
            phase_end()
            if stop_after == "oproj%d" % l:
                break

            h2 = sb([NC, T], BF16)
            tk_h2 = [Tk() for _ in range(NC)]
            norm_phase(CF_GFFN + l * 16, h2, tk_h2)
            NWG = 3
            wg = [sb([NC, 128], BF16) for _ in range(NWG)]
            wu = [sb([NC, 128], BF16) for _ in range(NWG)]
            tk_wg = [Tk() for _ in range(NWG)]
            tk_wu = [Tk() for _ in range(NWG)]
            wd = [sb([GFF, D], BF16) for _ in range(2)]
            tk_wd = [Tk() for _ in range(2)]
            actT = [sb([GFF, T], BF16) for _ in range(2)]
            tk_act = [[Tk() for _ in range(GFF)] for _ in range(2)]
            sg = [sb([512], F32) for _ in range(2)]
            tk_sg = [Tk() for _ in range(2)]

            def ld_gu(jc):
                ws = jc % NWG
                P.op("pool", lambda e, jc=jc, ws=ws, l=l: e.dma_start(
                    out=wg[ws], in_=w_gate.ap()[l, :, jc * 128:(jc + 1) * 128].rearrange("(n p) c -> p n c", p=128)),
                    writes=[tk_wg[ws]], dma=True, semkey=("wg", ws))
                P.op("pool", lambda e, jc=jc, ws=ws, l=l: e.dma_start(
                    out=wu[ws], in_=w_up.ap()[l, :, jc * 128:(jc + 1) * 128].rearrange("(n p) c -> p n c", p=128)),
                    writes=[tk_wu[ws]], dma=True, semkey=("wu", ws))

            def ld_wd(gi):
                ws = gi % 2
                P.op("pool", lambda e, gi=gi, ws=ws, l=l: e.dma_start(
                    out=wd[ws], in_=w_down.ap()[l, gi * GFF * 128:(gi + 1) * GFF * 128, :].rearrange("(a p) c -> p a c", p=128)),
                    writes=[tk_wd[ws]], dma=True, semkey=("wd", ws))
            ld_gu(0)
            ld_gu(1)
            ld_wd(0)
            sgc = 0
            NG = NFF // GFF
            for gi in range(NG):
                as_ = gi % 2
                if gi + 1 < NG:
                    ld_wd(gi + 1)
                for jj in range(GFF):
                    jc = gi * GFF + jj
                    if jc + 2 < NFF:
                        ld_gu(jc + 2)
                    ws = jc % NWG
                    for th in range(2):
                        bg, bu = nb(), nb()

                        def mm(e, bg=bg, bu=bu, ws=ws, th=th, h2=h2):
                            ins = None
                            for c in range(NC):
                                ins = e.matmul(banks[bg][:], lhsT=wg[ws][:, c, :], rhs=h2[:, c, th * 512:(th + 1) * 512],
                                               start=(c == 0), stop=(c == NC - 1))
                            for c in range(NC):
                                ins = e.matmul(banks[bu][:], lhsT=wu[ws][:, c, :], rhs=h2[:, c, th * 512:(th + 1) * 512],
                                               start=(c == 0), stop=(c == NC - 1))
                            return ins
                        P.op("pe", mm, reads=[tk_wg[ws], tk_wu[ws]] + tk_h2, writes=[bank_tk[bg], bank_tk[bu]])
                        ss = sgc % 2
                        sgc += 1
                        P.op("act", lambda e, bg=bg, ss=ss: e.activation(out=sg[ss], in_=banks[bg][:], func=AF.Silu),
                             reads=[bank_tk[bg]], writes=[tk_sg[ss]])
                        adst = actT[as_][:, jj, th * 512:(th + 1) * 512]
                        P.op("dve", lambda e, bu=bu, ss=ss, adst=adst: e.tensor_tensor(out=adst, in0=sg[ss], in1=banks[bu][:], op=ALU.mult),
                             reads=[tk_sg[ss], bank_tk[bu]], writes=[tk_act[as_][jj]])
                wsd = gi % 2
                for dc in range(NC):
                    for th in range(2):
                        b = nb()

                        def mmd(e, b=b, dc=dc, th=th, as_=as_, wsd=wsd):
                            ins = None
                            for jj in range(GFF):
                                ins = e.matmul(banks[b][:], lhsT=wd[wsd][:, jj, dc * 128:(dc + 1) * 128],
                                               rhs=actT[as_][:, jj, th * 512:(th + 1) * 512], start=(jj == 0), stop=(jj == GFF - 1))
                            return ins
                        P.op("pe", mmd, reads=[tk_wd[wsd]] + tk_act[as_], writes=[bank_tk[b]])
                        xsl = xT[:, dc, th * 512:(th + 1) * 512]
                        P.op("dve", lambda e, b=b, xsl=xsl: e.tensor_tensor(out=xsl, in0=xsl, in1=banks[b][:], op=ALU.add),
                             reads=[bank_tk[b], tk_x[dc][th]], writes=[tk_x[dc][th]])
            if l == 0:
                dump("x2", xT, [128, NC, T], F32, [t for c in tk_x for t in c])
            phase_end()
            if stop_after == "ffn%d" % l:
                break

    except StopBuild:
        pass

    if stop_after is None:
        yT = sb([NC, T], F32)
        tk_y = [Tk() for _ in range(NC)]
        norm_phase(CF_GFIN, yT, tk_y)
        ot = [sb([D], F32) for _ in range(2)]
        tk_ot = [Tk() for _ in range(2)]
        ev = 0
        for ti in range(8):
            os_ = ti % 2
            for cg in range(4):
                b = nb()

                def tr(e, b=b, cg=cg, ti=ti):
                    ins = None
                    for i in range(4):
                        c = cg * 4 + i
                        ins = e.transpose(out=banks[b][:, i * 128:(i + 1) * 128], in_=yT[:, c, ti * 128:(ti + 1) * 128], identity=ident_f)
                    return ins
                P.op("pe", tr, reads=[tk_y[cg * 4 + i] for i in range(4)] + [tk_cf], writes=[bank_tk[b]])
                dsl = ot[os_][:, cg * 512:(cg + 1) * 512]
                if ev % 2 == 0:
                    P.op("act", lambda e, b=b, dsl=dsl: e.activation(out=dsl, in_=banks[b][:], func=AF.Copy),
                         reads=[bank_tk[b]], writes=[tk_ot[os_]])
                else:
                    P.op("dve", lambda e, b=b, dsl=dsl: e.tensor_copy(out=dsl, in_=banks[b][:]),
                         reads=[bank_tk[b]], writes=[tk_ot[os_]])
                ev += 1
            P.op("sp", lambda e, ti=ti, os_=os_: e.dma_start(out=out_t.ap()[ti * 128:(ti + 1) * 128, :], in_=ot[os_]),
                 reads=[tk_ot[os_]], writes=[tk_out], dma=True, semkey=("ot", os_))
    else:
        zt = sb([D], F32)
        tkz = Tk()
        P.op("dve", lambda e: e.memset(zt, 0.0), writes=[tkz])
        for ti in range(8):
            P.op("sp", lambda e, ti=ti: e.dma_start(out=out_t.ap()[ti * 128:(ti + 1) * 128, :], in_=zt),
                 reads=[tkz], writes=[tk_out], dma=True)
    P.barrier()

    P.emit(nc, stack)
    stack.close()
    return nc, dbg_out


def make_in_maps(inp, need_ffn=True):
    x = np.asarray(inp["x"], np.float32)
    positions = np.asarray(inp["positions"], np.int32)
    w_in = np.asarray(inp["w_in"], np.float32)
    cbv = build_cb()
    cfv = build_cf(inp)
    w_out = np.ascontiguousarray(np.asarray(inp["w_out"], np.float32))
    w_gate = np.ascontiguousarray(np.asarray(inp["w_gate"], np.float32))
    w_up = np.ascontiguousarray(np.asarray(inp["w_up"], np.float32))
    w_down = np.ascontiguousarray(np.asarray(inp["w_down"], np.float32))
    maps = []
    for c in range(8):
        b, g = c // 4, c % 4
        qk_cols, v_cols = [], []
        bases = [(0, g), (1536, g), (3072, 2 * g), (3072, 2 * g + 1)]
        widths = [512, 512, 1024, 1024]
        for (base, h), wdt in zip(bases, widths):
            qk_cols.append(np.arange(base + h * 128, base + (h + 1) * 128))
            qk_cols.append(np.arange(base + wdt + h * 128, base + wdt + (h + 1) * 128))
            v_cols.append(np.arange(base + 2 * wdt + h * 128, base + 2 * wdt + (h + 1) * 128))
        cols = np.concatenate(qk_cols + v_cols)
        maps.append({
            "x": np.ascontiguousarray(x[b, g * T:(g + 1) * T, :]),
            "pos": np.ascontiguousarray(positions[b][None, :]),
            "cb": cbv,
            "cf": cfv,
            "w_in": np.ascontiguousarray(w_in[:, :, cols]),
            "w_out": w_out,
        })
        if need_ffn:
            maps[-1].update({"w_gate": w_gate, "w_up": w_up, "w_down": w_down})
    return maps


_CACHE = {}


def kernel(**inputs):
    if "nc" not in _CACHE:
        _CACHE["nc"] = build_program()[0]
    nc = _CACHE["nc"]
    maps = make_in_maps(inputs)
    res = run_bass_kernel_spmd(nc, maps, core_ids=list(range(8)))
    out = np.zeros((2, S, D), np.float32)
    for c in range(8):
        b, g = c // 4, c % 4
        out[b, g * T:(g + 1) * T, :] = np.asarray(res.results[c]["out"], np.float32)
    return out
```

```python
import math
import os
from contextlib import ExitStack

import numpy as np
import ml_dtypes

import concourse.bass as bass
import concourse.mybir as mybir
from concourse.bass_utils import run_bass_kernel_spmd

F32 = mybir.dt.float32
BF16 = mybir.dt.bfloat16
I32 = mybir.dt.int32
AF = mybir.ActivationFunctionType
ALU = mybir.AluOpType
AX = mybir.AxisListType

D = 2048
S = 4096
T = 1024
NL = 2
DFF = 5632
NFF = DFF // 128
NC = 16
EPS = 1e-6
THETA = 500000.0
PI = math.pi
GFF = 4

CB_ID, CB_ONES, CB_NUINC, CB_NONES, CB_PDIL, CB_PDIFF = 0, 128, 256, 384, 512, 640
CB_TOEP = 768
TOEP_W = 2944
CB_CINC = CB_TOEP + TOEP_W
CB_CSTR = CB_CINC + 896
NB = CB_CSTR + 896
CF_ID = 0
CF_GMIX = 128
CF_GFFN = 160
CF_GFIN = 192
CF_GSB = 208
CF_GDF = 210
CF_GDL = 212
CF_INVF = 214
CF_SGN = 216
CF_EPS = 218
CF_ONE = 219
CF_LNA = 220
CF_LAM = 224
NF = CF_LAM + NL * 4 * 64


def lambda_init(l):
    return 0.8 - 0.6 * math.exp(-0.3 * l)


class StopBuild(Exception):
    pass


class Tk:
    __slots__ = ("name", "w", "r", "multi")

    def __init__(self, name="", multi=False):
        self.name = name
        self.w = [] if multi else None
        self.r = []
        self.multi = multi


class Op:
    __slots__ = ("eng", "fn", "deps", "dma", "semkey", "val", "inc", "target", "sem")

    def __init__(self, eng, fn, dma, inc):
        self.eng = eng
        self.fn = fn
        self.deps = set()
        self.dma = dma
        self.semkey = None
        self.val = 0
        self.inc = inc
        self.target = False
        self.sem = None


class Prog:
    ENGS = ["pe", "act", "dve", "pool", "sp"]

    def __init__(self):
        self.stream = {e: [] for e in self.ENGS}
        self.semcount = {}
        self.since_barrier = []

    def op(self, eng, fn, reads=(), writes=(), dma=False, inc=16, semkey=None, bar=True):
        o = Op(eng, fn, dma, inc)
        deps = set()
        for t in list(reads) + list(writes):
            if t.multi:
                if t not in writes:
                    deps.update(t.w)
            elif t.w is not None:
                deps.add(t.w)
        for t in writes:
            deps.update(t.r)
        o.deps = deps
        for t in writes:
            if t.multi:
                t.w.append(o)
            else:
                t.w = o
                t.r = []
        for t in reads:
            t.r.append(o)
        if dma:
            if semkey is None:
                semkey = id(writes[0])
            o.semkey = semkey
            self.semcount[semkey] = self.semcount.get(semkey, 0) + inc
            o.val = self.semcount[semkey]
        self.stream[eng].append(o)
        if bar:
            self.since_barrier.append(o)
        return o

    def barrier(self):
        lasts = []
        for e in self.ENGS:
            for o in reversed(self.stream[e]):
                if not o.dma and o.fn is not None:
                    lasts.append(o)
                    break
        dmas = [o for o in self.since_barrier if o.dma]
        self.since_barrier = []
        for e in self.ENGS:
            o = Op(e, None, False, 0)
            o.deps = set(lasts) | set(dmas)
            self.stream[e].append(o)

    def emit(self, nc, stack):
        for e in self.ENGS:
            for o in self.stream[e]:
                for d in o.deps:
                    if not d.dma:
                        d.target = True
        engsem = {e: stack.enter_context(nc.semaphore("es_" + e)) for e in self.ENGS}
        keysem = {}
        for e in self.ENGS:
            cnt = 0
            for o in self.stream[e]:
                if o.dma:
                    if o.semkey not in keysem:
                        keysem[o.semkey] = stack.enter_context(nc.semaphore("ds%d" % len(keysem)))
                    o.sem = keysem[o.semkey]
                else:
                    if o.target:
                        cnt += 1
                    o.val = cnt
                    o.sem = engsem[e]
        self.nsem = len(keysem) + len(engsem)
        print('semaphores used:', self.nsem)
        block = stack.enter_context(nc.Block())
        prog = self

        def run(e):
            def body(eng):
                seen = {}
                for o in prog.stream[e]:
                    need = {}
                    for d in o.deps:
                        if e == "pe" and d.eng == "pe" and not d.dma:
                            continue
                        k = id(d.sem)
                        if k not in need or need[k][1] < d.val:
                            need[k] = (d.sem, d.val)
                    for k, (sem, val) in need.items():
                        if seen.get(k, 0) < val:
                            eng.wait_ge(sem, val)
                            seen[k] = val
                    if o.fn is None:
                        continue
                    inst = o.fn(eng)
                    if o.dma:
                        if o.inc == 1:
                            inst.then_inc(o.sem)
                        else:
                            inst.then_inc(o.sem, o.inc)
                    elif o.target:
                        inst.then_inc(o.sem, 1)
            return body

        block.tensor(run("pe"))
        block.scalar(run("act"))
        block.vector(run("dve"))
        block.gpsimd(run("pool"))
        block.sync(run("sp"))


def _dil_count(delta):
    c = np.zeros_like(delta, dtype=np.float32)
    c += ((delta >= 0) & (delta <= 128)).astype(np.float32)
    c += ((delta >= 0) & (delta <= 512) & (delta % 4 == 0)).astype(np.float32)
    c += ((delta >= 0) & (delta <= 2048) & (delta % 16 == 0)).astype(np.float32)
    return c


def build_cb():
    cb = np.zeros((128, NB), np.float32)
    j = np.arange(128)[:, None]
    k = np.arange(128)[None, :]
    cb[:, CB_ID:CB_ID + 128] = np.eye(128)
    cb[:, CB_ONES:CB_ONES + 128] = 1.0
    cb[:, CB_NUINC:CB_NUINC + 128] = -(j >= k).astype(np.float32)
    cb[:, CB_NONES:CB_NONES + 128] = -1.0
    pd = np.zeros((128, 128), np.float32)
    for d in range(32):
        partner = d + 16 if d < 16 else d - 16
        pd[partner, d] = 1.0
    cb[:, CB_PDIL:CB_PDIL + 128] = pd
    pf = np.zeros((128, 128), np.float32)
    for base in (0, 64):
        for d in range(16):
            partner = d + 8 if d < 8 else d - 8
            pf[base + partner, base + d] = 1.0
    cb[:, CB_PDIFF:CB_PDIFF + 128] = pf
    ki = np.arange(128)[:, None]
    xx = np.arange(TOEP_W)[None, :]
    cb[:, CB_TOEP:CB_TOEP + TOEP_W] = _dil_count(xx - 384 - ki)
    xx = np.arange(896)[None, :]
    cb[:, CB_CINC:CB_CINC + 896] = ((xx - 384 - ki) >= 0).astype(np.float32)
    cb[:, CB_CSTR:CB_CSTR + 896] = ((xx - 384 - ki) >= 1).astype(np.float32)
    return cb.astype(ml_dtypes.bfloat16)


def build_cf(inp):
    cf = np.zeros((128, NF), np.float32)
    cf[:, CF_ID:CF_ID + 128] = np.eye(128)

    def cols(v):
        return np.ascontiguousarray(np.asarray(v, np.float32).reshape(16, 128).T)

    for l in range(NL):
        cf[:, CF_GMIX + l * 16:CF_GMIX + (l + 1) * 16] = cols(inp["norm_mix_g"][l])
        cf[:, CF_GFFN + l * 16:CF_GFFN + (l + 1) * 16] = cols(inp["norm_ffn_g"][l])
        cf[:, CF_GSB + l] = np.asarray(inp["g_sb_out"][l], np.float32)
        cf[:, CF_GDF + l] = np.asarray(inp["g_diff_out"][l], np.float32)
        cf[:, CF_GDL + l] = np.asarray(inp["g_dil_out"][l], np.float32)
        cf[:, CF_LNA + l] = math.log(1.0 - lambda_init(l))
        for i, nm in enumerate(("lambda_q1", "lambda_k1", "lambda_q2", "lambda_k2")):
            o = CF_LAM + (l * 4 + i) * 64
            cf[:, o:o + 64] = np.asarray(inp[nm][l], np.float32)[None, :]
    cf[:, CF_GFIN:CF_GFIN + 16] = cols(inp["norm_final_g"])
    invd = np.zeros(128, np.float32)
    sgd = np.zeros(128, np.float32)
    fr = (np.float32(THETA) ** (-np.arange(16, dtype=np.float32) / np.float32(16))).astype(np.float32)
    for d in range(32):
        invd[d] = fr[d % 16]
        sgd[d] = -1.0 if d < 16 else 1.0
    invf = np.zeros(128, np.float32)
    sgf = np.zeros(128, np.float32)
    fr8 = (np.float32(THETA) ** (-np.arange(8, dtype=np.float32) / np.float32(8))).astype(np.float32)
    for base in (0, 64):
        for d in range(16):
            invf[base + d] = fr8[d % 8]
            sgf[base + d] = -1.0 if d < 8 else 1.0
    cf[:, CF_INVF] = invd
    cf[:, CF_INVF + 1] = invf
    cf[:, CF_SGN] = sgd
    cf[:, CF_SGN + 1] = sgf
    cf[:, CF_EPS] = EPS
    cf[:, CF_ONE] = 1.0
    return cf


def build_program(stop_after=None, dbg=None):
    nc = bass.Bass("TRN2", target_bir_lowering=False)
    P = Prog()
    dbg = dbg or []
    dbg_out = {}

    def dram(name, shape, dt, kind=None):
        if kind:
            return nc.dram_tensor(name, shape, dt, kind=kind)
        return nc.dram_tensor(name, shape, dt)

    x_in = dram("x", [T, D], F32, "ExternalInput")
    pos_in = dram("pos", [1, S], I32, "ExternalInput")
    cb_in = dram("cb", [128, NB], BF16, "ExternalInput")
    cf_in = dram("cf", [128, NF], F32, "ExternalInput")
    w_in = dram("w_in", [NL, D, 1536], F32, "ExternalInput")
    w_out = dram("w_out", [NL, D, D], F32, "ExternalInput")
    need_ffn = stop_after is None or stop_after.startswith("ffn") or stop_after.startswith("h1") or stop_after.startswith("att1") or stop_after.startswith("oproj1")
    if need_ffn:
        w_gate = dram("w_gate", [NL, D, DFF], F32, "ExternalInput")
        w_up = dram("w_up", [NL, D, DFF], F32, "ExternalInput")
        w_down = dram("w_down", [NL, DFF, D], F32, "ExternalInput")
    out_t = dram("out", [T, D], F32, "ExternalOutput")

    hbuf_in = [dram("hbuf_in%d" % l, [NC, 128, T], BF16) for l in range(NL)]
    hbuf_all = [dram("hbuf_all%d" % l, [NC, 512, T], BF16) for l in range(NL)]
    mbuf_in = [dram("mbuf_in%d" % l, [4, 4, 128, T], BF16) for l in range(NL)]
    mbuf_all = [dram("mbuf_all%d" % l, [4, 4, 512, T], BF16) for l in range(NL)]
    ropeC = [dram("ropeC%d" % i, [128, S], F32) for i in range(2)]
    ropeS = [dram("ropeS%d" % i, [128, S], F32) for i in range(2)]
    tk_hin = [[Tk("hin") for _ in range(NC)] for _ in range(NL)]
    tk_hall = [Tk("hall", multi=True) for _ in range(NL)]
    tk_min = [Tk("min", multi=True) for _ in range(NL)]
    tk_mall = [Tk("mall", multi=True) for _ in range(NL)]
    tk_rope = Tk("rope", multi=True)
    tk_out = Tk("out", multi=True)
    tk_dbg = Tk("dbg", multi=True)

    stack = ExitStack()
    ARENA_BYTES = int(os.environ.get("ARENA_KB", "200")) * 1024
    arena = stack.enter_context(nc.sbuf_tensor("arena", [128, ARENA_BYTES // 2], BF16))
    psum = stack.enter_context(nc.psum_tensor("psum", [128, 8 * 512], F32))
    banks = [psum[:, i * 512:(i + 1) * 512] for i in range(8)]
    bank_tk = [Tk("bank%d" % i) for i in range(8)]

    class Alloc:
        def __init__(self):
            self.base = 0
            self.top = 0

        def take(self, nbytes):
            nbytes = (nbytes + 63) // 64 * 64
            off = self.top
            self.top += nbytes
            assert self.top <= ARENA_BYTES, ("SBUF overflow", self.top)
            return off

        def persist(self):
            self.base = self.top

        def reset(self):
            self.top = self.base

    A = Alloc()

    def sb(shape, dt):
        n = int(np.prod(shape))
        esz = 2 if dt == BF16 else 4
        off = A.take(n * esz)
        v = arena[:, off // 2: off // 2 + n * esz // 2]
        if dt != BF16:
            v = v.bitcast(dt)
        if len(shape) == 2:
            v = v.rearrange("p (a b) -> p a b", a=shape[0])
        elif len(shape) == 3:
            v = v.rearrange("p (a b c) -> p a b c", a=shape[0], b=shape[1])
        return v

    bank_rr = [0]

    def nb(lo=0, hi=8):
        i = lo + bank_rr[0] % (hi - lo)
        bank_rr[0] += 1
        return i

    def phase_end():
        P.barrier()
        A.reset()

    def dump(name, ap_sb, shape, dt, reads):
        if name not in dbg:
            return
        t = dram("dbg_" + name, shape, dt, "ExternalOutput")
        dbg_out[name] = t
        P.op("sp", lambda e, t=t, a=ap_sb: e.dma_start(out=t.ap(), in_=a), reads=reads, writes=[tk_dbg], dma=True)

    cb = sb([NB], BF16)
    cf = sb([NF], F32)
    xT = sb([NC, T], F32)
    neglam = sb([NL], F32)
    tk_cb, tk_cf, tk_neglam = Tk("cb"), Tk("cf"), Tk("neglam")
    tk_x = [[Tk("xT") for _ in range(2)] for _ in range(NC)]
    A.persist()

    ident_f = cf[:, CF_ID:CF_ID + 128]
    ident_b = cb[:, CB_ID:CB_ID + 128]
    ones_b = cb[:, CB_ONES:CB_ONES + 128]

    def cfc(col):
        return cf[:, col:col + 1]

    try:
        P.op("sp", lambda e: e.dma_start(out=cb, in_=cb_in.ap()), writes=[tk_cb], dma=True)
        P.op("sp", lambda e: e.dma_start(out=cf, in_=cf_in.ap()), writes=[tk_cf], dma=True)

        lt = sb([4, 64], F32)
        ls = sb([8], F32)
        tk_lt, tk_ls = Tk(), Tk()
        for l in range(NL):
            for i in range(2):
                a0 = CF_LAM + (l * 4 + 2 * i) * 64
                P.op("dve", lambda e, i=i, a0=a0: e.tensor_tensor(out=lt[:, i, :], in0=cf[:, a0:a0 + 64],
                                                                   in1=cf[:, a0 + 64:a0 + 128], op=ALU.mult),
                     reads=[tk_cf], writes=[tk_lt])
                P.op("dve", lambda e, i=i: e.reduce_sum(out=ls[:, i:i + 1], in_=lt[:, i, :], axis=AX.X),
                     reads=[tk_lt], writes=[tk_ls])
                P.op("act", lambda e, i=i: e.activation(out=ls[:, 2 + i:3 + i], in_=ls[:, i:i + 1], func=AF.Exp),
                     reads=[tk_ls], writes=[tk_ls])
            P.op("dve", lambda e: e.tensor_tensor(out=ls[:, 4:5], in0=ls[:, 2:3], in1=ls[:, 3:4], op=ALU.subtract),
                 reads=[tk_ls], writes=[tk_ls])
            P.op("dve", lambda e, l=l: e.tensor_scalar(out=neglam[:, l:l + 1], in0=ls[:, 4:5], scalar1=-1.0,
                                                       scalar2=-lambda_init(l), op0=ALU.mult, op1=ALU.add),
                 reads=[tk_ls], writes=[tk_neglam])

        xs = [sb([D], F32) for _ in range(8)]
        tk_xs = [Tk() for _ in range(8)]
        for i in range(8):
            P.op("sp", lambda e, i=i: e.dma_start(out=xs[i], in_=x_in.ap()[i * 128:(i + 1) * 128, :]),
                 writes=[tk_xs[i]], dma=True)
        ev = 0
        for c in range(NC):
            for tg in range(2):
                b = nb()

                def tr(e, c=c, tg=tg, b=b):
                    ins = None
                    for i in range(4):
                        ins = e.transpose(out=banks[b][:, i * 128:(i + 1) * 128],
                                          in_=xs[tg * 4 + i][:, c * 128:(c + 1) * 128], identity=ident_f)
                    return ins
                P.op("pe", tr, reads=[tk_xs[tg * 4 + i] for i in range(4)] + [tk_cf], writes=[bank_tk[b]])
                dst = xT[:, c, tg * 512:(tg + 1) * 512]
                if ev % 2 == 0:
                    P.op("act", lambda e, b=b, dst=dst: e.activation(out=dst, in_=banks[b][:], func=AF.Copy),
                         reads=[bank_tk[b]], writes=[tk_x[c][tg]])
                else:
                    P.op("dve", lambda e, b=b, dst=dst: e.tensor_copy(out=dst, in_=banks[b][:]),
                         reads=[bank_tk[b]], writes=[tk_x[c][tg]])
                ev += 1
        phase_end()
        if stop_after == 'p0a':
            raise StopBuild()

        def rope_phase():
            posi = sb([S], I32)
            posf = sb([S], F32)
            tk_posi, tk_posf = Tk(), Tk()
            P.op("sp", lambda e: e.dma_start(out=posi, in_=pos_in.ap().partition_broadcast(128)), writes=[tk_posi], dma=True)
            P.op("dve", lambda e: e.tensor_copy(out=posf, in_=posi), reads=[tk_posi], writes=[tk_posf])
            RW = 1024
            rt = {k: [sb([RW], I32 if k == "ki" else F32) for _ in range(2)] for k in ("a", "y", "ki", "kf", "r", "m", "sn")}
            rtk = {k: [Tk() for _ in range(2)] for k in rt}
            it = 0
            for ty in range(2):
                for kind in range(2):
                    for ch in range(S // RW):
                        s_ = it % 2
                        it += 1
                        cs = slice(ch * RW, (ch + 1) * RW)
                        a, y, ki, kf, r, m, sn = (rt[k][s_] for k in ("a", "y", "ki", "kf", "r", "m", "sn"))
                        ta, ty_, tki, tkf, tr_, tm, tsn = (rtk[k][s_] for k in ("a", "y", "ki", "kf", "r", "m", "sn"))
                        off = PI / 2 if kind == 0 else 0.0
                        P.op("dve", lambda e, a=a, cs=cs, ty=ty, off=off: e.tensor_scalar(
                            out=a, in0=posf[:, cs], scalar1=cfc(CF_INVF + ty), scalar2=off, op0=ALU.mult, op1=ALU.add),
                            reads=[tk_posf, tk_cf], writes=[ta])
                        P.op("dve", lambda e, a=a, y=y: e.tensor_scalar(out=y, in0=a, scalar1=1.0 / (2 * PI), scalar2=None,
                                                                         op0=ALU.mult), reads=[ta], writes=[ty_])
                        P.op("dve", lambda e, y=y, ki=ki: e.tensor_copy(out=ki, in_=y), reads=[ty_], writes=[tki])
                        P.op("dve", lambda e, kf=kf, ki=ki: e.tensor_copy(out=kf, in_=ki), reads=[tki], writes=[tkf])
                        P.op("dve", lambda e, kf=kf, a=a, r=r: e.scalar_tensor_tensor(
                            out=r, in0=kf, scalar=-2 * PI, in1=a, op0=ALU.mult, op1=ALU.add), reads=[tkf, ta], writes=[tr_])
                        P.op("dve", lambda e, r=r, m=m: e.tensor_scalar(out=m, in0=r, scalar1=PI, scalar2=2 * PI,
                                                                         op0=ALU.is_gt, op1=ALU.mult), reads=[tr_], writes=[tm])
                        P.op("dve", lambda e, r=r, m=m: e.tensor_tensor(out=r, in0=r, in1=m, op=ALU.subtract),
                             reads=[tm, tr_], writes=[tr_])
                        P.op("dve", lambda e, r=r: e.tensor_scalar(out=r, in0=r, scalar1=PI, scalar2=-PI,
                                                                    op0=ALU.min, op1=ALU.max), reads=[tr_], writes=[tr_])
                        P.op("act", lambda e, r=r, sn=sn: e.activation(out=sn, in_=r, func=AF.Sin), reads=[tr_], writes=[tsn])
                        if kind == 1:
                            P.op("dve", lambda e, sn=sn, ty=ty: e.tensor_scalar(out=sn, in0=sn, scalar1=cfc(CF_SGN + ty),
                                                                                 scalar2=None, op0=ALU.mult),
                                 reads=[tsn, tk_cf], writes=[tsn])
                        dst = (ropeC if kind == 0 else ropeS)[ty]
                        P.op("sp", lambda e, dst=dst, cs=cs, sn=sn: e.dma_start(out=dst.ap()[:, cs], in_=sn),
                             reads=[tsn], writes=[tk_rope], dma=True, semkey=("sn", s_))
            phase_end()

        def norm_phase(gcol, hT, tk_h, out_dt_is_bf16=True):
            sq = [sb([T], BF16) for _ in range(2)]
            tk_sq = [Tk() for _ in range(2)]
            lnv = sb([T], F32)
            rstd = sb([T], F32)
            tk_lnv, tk_rstd = Tk(), Tk()
            b0, b1 = nb(), nb()
            bs = (b0, b1)
            for c in range(NC):
                s_ = c % 2
                P.op("act", lambda e, c=c, s_=s_: e.activation(out=sq[s_], in_=xT[:, c, :], func=AF.Square),
                     reads=[tk_x[c][0], tk_x[c][1]], writes=[tk_sq[s_]])

                def mm(e, c=c, s_=s_):
                    ins = None
                    for th in range(2):
                        ins = e.matmul(banks[bs[th]][:], lhsT=ones_b, rhs=sq[s_][:, th * 512:(th + 1) * 512],
                                       start=(c == 0), stop=(c == NC - 1))
                    return ins
                P.op("pe", mm, reads=[tk_sq[s_], tk_cb], writes=[bank_tk[b0], bank_tk[b1]])
            for th in range(2):
                P.op("act", lambda e, th=th: e.activation(out=lnv[:, th * 512:(th + 1) * 512], in_=banks[bs[th]][:],
                                                         func=AF.Ln, bias=cfc(CF_EPS), scale=1.0 / D),
                     reads=[bank_tk[bs[th]], tk_cf], writes=[tk_lnv])
            P.op("act", lambda e: e.activation(out=rstd, in_=lnv, func=AF.Exp, scale=-0.5), reads=[tk_lnv], writes=[tk_rstd])
            for c in range(NC):
                P.op("dve", lambda e, c=c: e.scalar_tensor_tensor(out=hT[:, c, :], in0=xT[:, c, :], scalar=cfc(gcol + c),
                                                                  in1=rstd, op0=ALU.mult, op1=ALU.mult),
                     reads=[tk_x[c][0], tk_x[c][1], tk_rstd, tk_cf], writes=[tk_h[c]])

        hn_calls = [0]

        def head_norm_steps(o, tk_o, gcol, lnbias_col, dst_dram_ap, tk_dst, ring, msb=4):
            sq, tk_sq, ln_, tk_ln, rs, tk_rs, mo, tk_mo = ring
            b = msb
            hn_calls[0] += 1
            skey = ("mo", hn_calls[0] % 2)

            def step_a():
                P.op("act", lambda e: e.activation(out=sq, in_=o, func=AF.Square), reads=[tk_o], writes=[tk_sq])
                P.op("pe", lambda e, b=b: e.matmul(banks[b][:], lhsT=ones_b, rhs=sq, start=True, stop=True),
                     reads=[tk_sq, tk_cb], writes=[bank_tk[b]])

            def step_b():
                P.op("act", lambda e, b=b: e.activation(out=ln_, in_=banks[b][:], func=AF.Ln, bias=cfc(CF_EPS), scale=1.0 / 128),
                     reads=[bank_tk[b], tk_cf], writes=[tk_ln])
                if lnbias_col is None:
                    P.op("act", lambda e: e.activation(out=rs, in_=ln_, func=AF.Exp, scale=-0.5), reads=[tk_ln], writes=[tk_rs])
                else:
                    P.op("act", lambda e: e.activation(out=rs, in_=ln_, func=AF.Exp, scale=-0.5, bias=cfc(lnbias_col)),
                         reads=[tk_ln, tk_cf], writes=[tk_rs])
                P.op("dve", lambda e: e.scalar_tensor_tensor(out=mo, in0=o, scalar=cfc(gcol), in1=rs, op0=ALU.mult, op1=ALU.mult),
                     reads=[tk_o, tk_rs, tk_cf], writes=[tk_mo])
                P.op("sp", lambda e: e.dma_start(out=dst_dram_ap, in_=mo), reads=[tk_mo], writes=[tk_dst], dma=True, semkey=skey)
            return [step_a, step_b]

        def head_norm(*args, **kw):
            for st in head_norm_steps(*args, **kw):
                st()

        for l in range(NL):
            saved_base = A.base
            wq_ring = [sb([3, NC, 128], BF16) for _ in range(2)]
            tk_w_ring = [[Tk() for _ in range(3)] for _ in range(2)]
            A.persist()

            def load_wq(s_, l=l, wq_ring=wq_ring, tk_w_ring=tk_w_ring):
                wcols_ = (2 * s_ * 128, (2 * s_ + 1) * 128, 1024 + s_ * 128)
                for i in range(3):
                    P.op("pool", lambda e, i=i, wcols_=wcols_, s_=s_: e.dma_start(
                        out=wq_ring[s_ % 2][:, i], in_=w_in.ap()[l, :, wcols_[i]:wcols_[i] + 128].rearrange("(c p) n -> p c n", p=128)),
                        writes=[tk_w_ring[s_ % 2][i]], dma=True, semkey=("wq", s_ % 2, i))
            load_wq(0)
            hT = sb([NC, T], BF16)
            tk_h = [Tk() for _ in range(NC)]
            norm_phase(CF_GMIX + l * 16, hT, tk_h)
            for c in range(NC):
                P.op("sp", lambda e, l=l, hT=hT, c=c: e.dma_start(out=hbuf_in[l].ap()[c], in_=hT[:, c, :]),
                     reads=[tk_h[c]], writes=[tk_hin[l][c]], dma=True, semkey=("hin", c))
                P.op("pool", lambda e, l=l, c=c: e.collective_compute("AllGather", ALU.bypass, replica_groups=[[0, 1, 2, 3], [4, 5, 6, 7]],
                                                                     ins=[hbuf_in[l].ap()[c]], outs=[hbuf_all[l].ap()[c]]),
                     reads=[tk_hin[l][c]], writes=[tk_hall[l]], dma=True, inc=1, bar=False)
            if l == 0:
                dump("h0", hT, [128, NC, T], BF16, tk_h)
            phase_end()
            if l == 0:
                rope_phase()
            if stop_after == "h%d" % l:
                break

            for s in range(4):
                kind = ("sb", "diff", "dil", "dil")[s]
                rope_ty = {"sb": None, "diff": 1, "dil": 0}[kind]
                qscale = 0.125 if kind == "diff" else 128 ** -0.5
                perm = None if rope_ty is None else cb[:, (CB_PDIL if rope_ty == 0 else CB_PDIFF):][:, 0:128]

                wq = wq_ring[s % 2]
                tk_w = tk_w_ring[s % 2]
                qT = sb([S], BF16)
                kT = sb([S], BF16)
                vv = sb([32, 128], BF16)
                tk_q = [Tk() for _ in range(8)]
                tk_k = [Tk() for _ in range(8)]
                tk_v = [Tk() for _ in range(8)]
                mark_ = A.top
                ht = [sb([NC, 512], BF16) for _ in range(2)]
                tk_ht = [Tk() for _ in range(2)]
                rC = [sb([512], F32) for _ in range(2)]
                rS = [sb([512], F32) for _ in range(2)]
                tk_rC = [Tk() for _ in range(2)]
                tk_rS = [Tk() for _ in range(2)]
                raw = [sb([512], BF16) for _ in range(2)]
                tk_raw = [Tk() for _ in range(2)]
                t1 = [sb([512], F32) for _ in range(2)]
                t2 = [sb([512], F32) for _ in range(2)]
                tk_t1 = [Tk() for _ in range(2)]
                tk_t2 = [Tk() for _ in range(2)]

                ri = 0
                for tt in range(8):
                    hs = tt % 2
                    r_, off = tt // 2, (tt % 2) * 512
                    P.op("sp", lambda e, l=l, r_=r_, off=off, hs=hs, ht=ht: e.dma_start(
                        out=ht[hs], in_=hbuf_all[l].ap()[:, r_ * 128:(r_ + 1) * 128, off:off + 512].rearrange("c p t -> p c t")),
                        reads=[tk_hall[l]], writes=[tk_ht[hs]], dma=True, semkey=("ht", hs))
                    if rope_ty is not None:
                        P.op("sp", lambda e, tt=tt, hs=hs, rC=rC, rope_ty=rope_ty: e.dma_start(
                            out=rC[hs], in_=ropeC[rope_ty].ap()[:, tt * 512:(tt + 1) * 512]),
                            reads=[tk_rope], writes=[tk_rC[hs]], dma=True, semkey=("rC", hs))
                        P.op("sp", lambda e, tt=tt, hs=hs, rS=rS, rope_ty=rope_ty: e.dma_start(
                            out=rS[hs], in_=ropeS[rope_ty].ap()[:, tt * 512:(tt + 1) * 512]),
                            reads=[tk_rope], writes=[tk_rS[hs]], dma=True, semkey=("rS", hs))
                    post = []
                    for qi, (dstT, tkd, sc) in enumerate(((qT, tk_q, qscale), (kT, tk_k, 1.0))):
                        b = nb()

                        def mm(e, b=b, qi=qi, hs=hs, wq=wq, ht=ht):
                            ins = None
                            for c in range(NC):
                                ins = e.matmul(banks[b][:], lhsT=wq[:, qi, c, :], rhs=ht[hs][:, c, :],
                                               start=(c == 0), stop=(c == NC - 1))
                            return ins
                        P.op("pe", mm, reads=[tk_w[qi], tk_ht[hs]], writes=[bank_tk[b]])
                        dst = dstT[:, tt * 512:(tt + 1) * 512]
                        if rope_ty is None:
                            P.op("act", lambda e, b=b, dst=dst, sc=sc: e.activation(out=dst, in_=banks[b][:], func=AF.Copy, scale=sc),
                                 reads=[bank_tk[b]], writes=[tkd[tt]])
                        else:
                            rs_ = ri % 2
                            ri += 1
                            P.op("act", lambda e, b=b, rs_=rs_, sc=sc, raw=raw: e.activation(out=raw[rs_], in_=banks[b][:], func=AF.Copy, scale=sc),
                                 reads=[bank_tk[b]], writes=[tk_raw[rs_]])
                            def post_fn(rs_=rs_, hs=hs, dst=dst, tkd=tkd, tt=tt, perm=perm, raw=raw, t1=t1, t2=t2, rS=rS, rC=rC,
                                        tk_raw=tk_raw, tk_t1=tk_t1, tk_t2=tk_t2, tk_rS=tk_rS, tk_rC=tk_rC):
                                b2 = nb()
                                P.op("pe", lambda e, b2=b2, rs_=rs_, perm=perm, raw=raw: e.matmul(banks[b2][:], lhsT=perm, rhs=raw[rs_], start=True, stop=True),
                                     reads=[tk_raw[rs_], tk_cb], writes=[bank_tk[b2]])
                                P.op("dve", lambda e, b2=b2, rs_=rs_, hs=hs, t1=t1, rS=rS: e.tensor_tensor(out=t1[rs_], in0=banks[b2][:], in1=rS[hs], op=ALU.mult),
                                     reads=[bank_tk[b2], tk_rS[hs]], writes=[tk_t1[rs_]])
                                P.op("pool", lambda e, rs_=rs_, hs=hs, t2=t2, raw=raw, rC=rC: e.tensor_tensor(out=t2[rs_], in0=raw[rs_], in1=rC[hs], op=ALU.mult),
                                     reads=[tk_raw[rs_], tk_rC[hs]], writes=[tk_t2[rs_]])
                                P.op("dve", lambda e, rs_=rs_, dst=dst, t1=t1, t2=t2: e.tensor_tensor(out=dst, in0=t1[rs_], in1=t2[rs_], op=ALU.add),
                                     reads=[tk_t1[rs_], tk_t2[rs_]], writes=[tkd[tt]])
                            post.append(post_fn)
                    b = nb()

                    def mmv(e, b=b, hs=hs, wq=wq, ht=ht):
                        ins = None
                        for sub in range(4):
                            for c in range(NC):
                                ins = e.matmul(banks[b][:, sub * 128:(sub + 1) * 128], lhsT=ht[hs][:, c, sub * 128:(sub + 1) * 128],
                                               rhs=wq[:, 2, c, :], start=(c == 0), stop=(c == NC - 1))
                        return ins
                    P.op("pe", mmv, reads=[tk_w[2], tk_ht[hs]], writes=[bank_tk[b]])
                    P.op("dve", lambda e, b=b, tt=tt, vv=vv: e.tensor_copy(out=vv[:, tt * 4:(tt + 1) * 4, :],
                                                                            in_=banks[b][:].rearrange("p (a b) -> p a b", a=4)),
                         reads=[bank_tk[b]], writes=[tk_v[tt]])
                    for pf_ in post:
                        pf_()
                if l == 0 and s in (0, 1, 2):
                    dump("q%d" % s, qT, [128, S], BF16, tk_q)
                    dump("k%d" % s, kT, [128, S], BF16, tk_k)
                    dump("v%d" % s, vv, [128, 32, 128], BF16, tk_v)

                P.barrier()
                A.top = mark_
                if s + 1 < 4:
                    load_wq(s + 1)
                NE2 = 3
                E2 = [sb([1024], BF16) for _ in range(NE2)]
                Em2 = [sb([1024], BF16) for _ in range(NE2)]
                tk_E2 = [Tk() for _ in range(NE2)]
                tk_Em2 = [[Tk(), Tk()] for _ in range(NE2)]
                pcnt = [0]
                oacc = [sb([512], F32) for _ in range(3)]
                tk_oacc = [Tk() for _ in range(3)]
                rl = sb([512], F32)
                tk_rl = Tk()
                hn_ring = []
                for _ in range(2):
                    hn_ring.append((sb([512], BF16), Tk(), sb([512], F32), Tk(), sb([512], F32), Tk(), sb([512], BF16), Tk()))
                if kind == "sb":
                    ef2 = [sb([1024], F32) for _ in range(2)]
                    tk_ef2 = [Tk() for _ in range(2)]
                    sp2 = [sb([1024], BF16) for _ in range(3)]
                    tk_sp2 = [Tk() for _ in range(3)]
                    spm2 = [sb([1024], BF16) for _ in range(3)]
                    tk_spm2 = [[Tk(), Tk()] for _ in range(3)]
                    Ts = [sb([512], BF16) for _ in range(4)]
                    tk_Ts = [Tk() for _ in range(4)]
                    tmpT = [sb([512], BF16) for _ in range(2)]
                    tk_tmpT = [Tk() for _ in range(2)]
                OB, LB = 6, 7
                ecnt = [0]
                hcnt = [0]

                pending = []
                olcnt = [0]
                if kind != "sb":
                    lnl = [sb([512], F32) for _ in range(2)]
                    tk_lnl = [Tk() for _ in range(2)]
                    oraw = [sb([512], F32) for _ in range(2)]
                    tk_oraw = [Tk() for _ in range(2)]

                def flush_pending():
                    while pending:
                        pending.pop(0)()

                def run_pending_step():
                    if pending:
                        pending.pop(0)()

                def softmax_pass(j, parts, ktiles, maskfn, o_dst, tk_odst, after=None):
                    p0, p1 = parts
                    qsl = qT[p0:p1, j * 512:(j + 1) * 512]
                    npair = len(ktiles) // 2
                    assert len(ktiles) % 2 == 0
                    sb_of = {}
                    OB, LB = 6, 7
                    ek = olcnt[0] % 2
                    olcnt[0] += 1

                    def issue_s(pi):
                        pb = (pcnt[0] % 2) * 2
                        pcnt[0] += 1
                        sb_of[pi] = pb
                        i0, i1 = ktiles[2 * pi], ktiles[2 * pi + 1]

                        def mm(e, pb=pb, i0=i0, i1=i1):
                            e.matmul(banks[pb][:], lhsT=kT[p0:p1, i0 * 128:(i0 + 1) * 128], rhs=qsl, start=True, stop=True)
                            return e.matmul(banks[pb + 1][:], lhsT=kT[p0:p1, i1 * 128:(i1 + 1) * 128], rhs=qsl, start=True, stop=True)
                        P.op("pe", mm, reads=[tk_k[i0 // 4], tk_k[i1 // 4], tk_q[j]], writes=[bank_tk[pb], bank_tk[pb + 1]])
                    issue_s(0)
                    if npair > 1:
                        issue_s(1)
                    for pi in range(npair):
                        if pi >= 1:
                            run_pending_step()
                        pb = sb_of.pop(pi)
                        es = ecnt[0] % NE2
                        ecnt[0] += 1
                        P.op("act", lambda e, pb=pb, es=es: e.activation(out=E2[es], in_=psum[:, pb * 512:(pb + 2) * 512], func=AF.Exp),
                             reads=[bank_tk[pb], bank_tk[pb + 1]], writes=[tk_E2[es]])
                        srcs = []
                        for h_ in range(2):
                            i = ktiles[2 * pi + h_]
                            hsl = slice(h_ * 512, (h_ + 1) * 512)
                            mk = maskfn(i)
                            if mk is not None:
                                P.op("dve", lambda e, es=es, mk=mk, hsl=hsl: e.tensor_tensor(out=Em2[es][:, hsl], in0=E2[es][:, hsl], in1=mk, op=ALU.mult),
                                     reads=[tk_E2[es], tk_cb], writes=[tk_Em2[es][h_]])
                                srcs.append((Em2[es][:, hsl], tk_Em2[es][h_], i))
                            else:
                                srcs.append((E2[es][:, hsl], tk_E2[es], i))

                        if pi + 2 < npair:
                            issue_s(pi + 2)

                        def mmo(e, srcs=srcs, pi=pi, OB=OB, LB=LB):
                            first, last = (pi == 0), (pi == npair - 1)
                            (s0, _, i0), (s1, _, i1) = srcs
                            e.matmul(banks[OB][:], lhsT=vv[:, i0, :], rhs=s0, start=first, stop=False)
                            e.matmul(banks[OB][:], lhsT=vv[:, i1, :], rhs=s1, start=False, stop=last)
                            e.matmul(banks[LB][:], lhsT=ones_b, rhs=s0, start=first, stop=False)
                            return e.matmul(banks[LB][:], lhsT=ones_b, rhs=s1, start=False, stop=last)
                        P.op("pe", mmo, reads=[srcs[0][1], srcs[1][1], tk_v[srcs[0][2] // 4], tk_v[srcs[1][2] // 4], tk_cb],
                             writes=[bank_tk[OB], bank_tk[LB]])
                    flush_pending()
                    P.op("act", lambda e, ek=ek: e.activation(out=lnl[ek], in_=banks[LB][:], func=AF.Ln), reads=[bank_tk[LB]], writes=[tk_lnl[ek]])
                    P.op("dve", lambda e, ek=ek: e.tensor_copy(out=oraw[ek], in_=banks[OB][:]), reads=[bank_tk[OB]], writes=[tk_oraw[ek]])

                    def epilogue(ek=ek):
                        P.op("act", lambda e: e.activation(out=rl, in_=lnl[ek], func=AF.Exp, scale=-1.0), reads=[tk_lnl[ek]], writes=[tk_rl])
                        P.op("dve", lambda e: e.tensor_tensor(out=o_dst, in0=oraw[ek], in1=rl, op=ALU.mult),
                             reads=[tk_oraw[ek], tk_rl], writes=[tk_odst])
                    pending.append(epilogue)
                    if after is not None:
                        pending.extend(after())

                def sb_head():
                    cstr = cb[:, CB_CSTR:CB_CSTR + 896]
                    nuinc = cb[:, CB_NUINC:CB_NUINC + 128]
                    nones = cb[:, CB_NONES:CB_NONES + 128]
                    SOB = 7
                    units = []
                    for j in range(8):
                        tiles = list(range(4 * j + 3, -1, -1))
                        npj = len(tiles) // 2
                        for p in range(npj):
                            units.append((j, p, tiles[2 * p], tiles[2 * p + 1], npj))
                    G = len(units)
                    st1 = {}
                    st2 = {}

                    def mask_of(i, j):
                        if i >= 4 * j:
                            Dq = 512 * j - 128 * i
                            return cstr[:, Dq + 384:Dq + 384 + 512]
                        return None

                    def stage1(g):
                        j, p, iA, iB, npj = units[g]
                        qsl = qT[:, j * 512:(j + 1) * 512]

                        def mmz(e, iA=iA, iB=iB, qsl=qsl):
                            e.matmul(banks[0][:], lhsT=kT[:, iA * 128:(iA + 1) * 128], rhs=qsl, start=True, stop=True)
                            return e.matmul(banks[1][:], lhsT=kT[:, iB * 128:(iB + 1) * 128], rhs=qsl, start=True, stop=True)
                        P.op("pe", mmz, reads=[tk_k[iA // 4], tk_k[iB // 4], tk_q[j]], writes=[bank_tk[0], bank_tk[1]])
                        fs = g % 2
                        P.op("act", lambda e, fs=fs: e.activation(out=ef2[fs], in_=psum[:, 0:1024], func=AF.Exp),
                             reads=[bank_tk[0], bank_tk[1]], writes=[tk_ef2[fs]])
                        ss = g % 3
                        P.op("act", lambda e, fs=fs, ss=ss: e.activation(out=sp2[ss], in_=ef2[fs], func=AF.Ln, bias=cfc(CF_ONE)),
                             reads=[tk_ef2[fs], tk_cf], writes=[tk_sp2[ss]])
                        curs = []
                        for h_, i in enumerate((iA, iB)):
                            hsl = slice(h_ * 512, (h_ + 1) * 512)
                            mk = mask_of(i, j)
                            if mk is not None:
                                P.op("dve", lambda e, ss=ss, mk=mk, hsl=hsl: e.tensor_tensor(out=spm2[ss][:, hsl], in0=sp2[ss][:, hsl], in1=mk, op=ALU.mult),
                                     reads=[tk_sp2[ss], tk_cb], writes=[tk_spm2[ss][h_]])
                                curs.append((spm2[ss][:, hsl], tk_spm2[ss][h_]))
                            else:
                                curs.append((sp2[ss][:, hsl], tk_sp2[ss]))
                        tprev, tnew = g % 4, (g + 1) % 4
                        if p + 1 < npj:
                            (cA, tA), (cB, tB) = curs
                            if p == 0:
                                P.op("dve", lambda e, cA=cA, cB=cB, tnew=tnew: e.tensor_tensor(out=Ts[tnew], in0=cA, in1=cB, op=ALU.add),
                                     reads=[tA, tB], writes=[tk_Ts[tnew]])
                            else:
                                ts_ = g % 2
                                P.op("dve", lambda e, cA=cA, cB=cB, ts_=ts_: e.tensor_tensor(out=tmpT[ts_], in0=cA, in1=cB, op=ALU.add),
                                     reads=[tA, tB], writes=[tk_tmpT[ts_]])
                                P.op("dve", lambda e, ts_=ts_, tprev=tprev, tnew=tnew: e.tensor_tensor(out=Ts[tnew], in0=Ts[tprev], in1=tmpT[ts_], op=ALU.add),
                                     reads=[tk_tmpT[ts_], tk_Ts[tprev]], writes=[tk_Ts[tnew]])
                        st1[g] = (curs, tprev)

                    def stage2(g):
                        j, p, iA, iB, npj = units[g]
                        qsl = qT[:, j * 512:(j + 1) * 512]
                        curs, tprev = st1.pop(g)
                        (cA, tA), (cB, tB) = curs
                        lb = 2 + 2 * (g % 2)

                        def mml(e, lb=lb, iA=iA, iB=iB, cA=cA, cB=cB, tprev=tprev, p=p, qsl=qsl):
                            e.matmul(banks[lb][:], lhsT=kT[:, iA * 128:(iA + 1) * 128], rhs=qsl, start=True, stop=False)
                            ins = e.matmul(banks[lb][:], lhsT=nuinc, rhs=cA, start=False, stop=(p == 0))
                            if p > 0:
                                ins = e.matmul(banks[lb][:], lhsT=nones, rhs=Ts[tprev], start=False, stop=True)
                            e.matmul(banks[lb + 1][:], lhsT=kT[:, iB * 128:(iB + 1) * 128], rhs=qsl, start=True, stop=False)
                            e.matmul(banks[lb + 1][:], lhsT=nuinc, rhs=cB, start=False, stop=False)
                            ins = e.matmul(banks[lb + 1][:], lhsT=nones, rhs=cA, start=False, stop=(p == 0))
                            if p > 0:
                                ins = e.matmul(banks[lb + 1][:], lhsT=nones, rhs=Ts[tprev], start=False, stop=True)
                            return ins
                        rd = [tk_k[iA // 4], tk_k[iB // 4], tk_q[j], tA, tB, tk_cb] + ([tk_Ts[tprev]] if p > 0 else [])
                        P.op("pe", mml, reads=rd, writes=[bank_tk[lb], bank_tk[lb + 1]])
                        es = g % NE2
                        P.op("act", lambda e, lb=lb, es=es: e.activation(out=E2[es], in_=psum[:, lb * 512:(lb + 2) * 512], func=AF.Exp),
                             reads=[bank_tk[lb], bank_tk[lb + 1]], writes=[tk_E2[es]])
                        srcs = []
                        for h_, i in enumerate((iA, iB)):
                            hsl = slice(h_ * 512, (h_ + 1) * 512)
                            mk = mask_of(i, j)
                            if mk is not None:
                                P.op("dve", lambda e, es=es, mk=mk, hsl=hsl: e.tensor_tensor(out=Em2[es][:, hsl], in0=E2[es][:, hsl], in1=mk, op=ALU.mult),
                                     reads=[tk_E2[es], tk_cb], writes=[tk_Em2[es][h_]])
                                srcs.append((Em2[es][:, hsl], tk_Em2[es][h_], i))
                            else:
                                srcs.append((E2[es][:, hsl], tk_E2[es], i))
                        st2[g] = srcs

                    def stage2b(g):
                        j, p, iA, iB, npj = units[g]
                        (s0, t0_, i0), (s1, t1_, i1) = st2.pop(g)

                        def mmav(e, s0=s0, s1=s1, i0=i0, i1=i1, p=p, npj=npj):
                            e.matmul(banks[SOB][:], lhsT=vv[:, i0, :], rhs=s0, start=(p == 0), stop=False)
                            return e.matmul(banks[SOB][:], lhsT=vv[:, i1, :], rhs=s1, start=False, stop=(p == npj - 1))
                        P.op("pe", mmav, reads=[t0_, t1_, tk_v[i0 // 4], tk_v[i1 // 4]], writes=[bank_tk[SOB]])
                        if p == npj - 1:
                            oslot = j % 2
                            P.op("act", lambda e, oslot=oslot: e.activation(out=oacc[oslot], in_=banks[SOB][:], func=AF.Copy),
                                 reads=[bank_tk[SOB]], writes=[tk_oacc[oslot]])
                            dst = mbuf_in[l].ap()[j // 2, s, :, (j % 2) * 512:(j % 2) * 512 + 512]
                            sa, sb_ = head_norm_steps(oacc[oslot], tk_oacc[oslot], CF_GSB + l, None, dst, tk_min[l], hn_ring[j % 2], msb=6)

                            def s3(sb_=sb_, j=j):
                                sb_()
                                quarter_collective(j)
                            pending.extend([sa, s3])

                    stage1(0)
                    stage1(1)
                    for g in range(G):
                        run_pending_step()
                        stage2(g)
                        if g + 2 < G:
                            stage1(g + 2)
                        if g >= 1:
                            stage2b(g - 1)
                    stage2b(G - 1)
                    flush_pending()

                cinc = cb[:, CB_CINC:CB_CINC + 896]
                toep = cb[:, CB_TOEP:CB_TOEP + TOEP_W]
                def quarter_collective(j, l=l, s=s):
                    if j % 2 == 1:
                        qt = j // 2
                        P.op("pool", lambda e, l=l, qt=qt, s=s: e.collective_compute("AllGather", ALU.bypass, replica_groups=[[0, 1, 2, 3], [4, 5, 6, 7]],
                                                                                    ins=[mbuf_in[l].ap()[qt, s]], outs=[mbuf_all[l].ap()[qt, s]]),
                             reads=[tk_min[l]], writes=[tk_mall[l]], dma=True, inc=1, bar=False)

                if kind == "sb":
                    sb_head()
                for j in range(8 if kind != "sb" else 0):
                    dst = mbuf_in[l].ap()[j // 2, s, :, (j % 2) * 512:(j % 2) * 512 + 512]
                    ring = hn_ring[hcnt[0] % 2]
                    hcnt[0] += 1
                    if kind == "diff":
                        def mk_diff(i, j=j):
                            if i >= 4 * j:
                                Dq = 512 * j - 128 * i
                                return cinc[:, Dq + 384:Dq + 384 + 512]
                            return None
                        kt = list(range(0, 4 * j + 4))

                        def after_diff(j=j, dst=dst, ring=ring, l=l):
                            sa, sb_ = head_norm_steps(oacc[2], tk_oacc[2], CF_GDF + l, CF_LNA + l, dst, tk_min[l], ring, msb=4)

                            def s2():
                                P.op("dve", lambda e, l=l: e.scalar_tensor_tensor(out=oacc[2], in0=oacc[1], scalar=neglam[:, l:l + 1], in1=oacc[0],
                                                                                 op0=ALU.mult, op1=ALU.add),
                                     reads=[tk_oacc[0], tk_oacc[1], tk_neglam], writes=[tk_oacc[2]])
                                sa()

                            def s3():
                                sb_()
                                quarter_collective(j)
                            return [s2, s3]
                        softmax_pass(j, (0, 64), kt, mk_diff, oacc[0], tk_oacc[0])
                        softmax_pass(j, (64, 128), kt, mk_diff, oacc[1], tk_oacc[1], after=after_diff)
                    else:
                        def mk_dil(i, j=j):
                            Dq = 512 * j - 128 * i
                            return toep[:, Dq + 384:Dq + 384 + 512]
                        kt = list(range(max(0, 4 * j - 16), 4 * j + 4))
                        oslot = j % 2

                        def after_dil(j=j, dst=dst, ring=ring, l=l, oslot=oslot):
                            sa, sb_ = head_norm_steps(oacc[oslot], tk_oacc[oslot], CF_GDL + l, None, dst, tk_min[l], ring, msb=4)

                            def s3():
                                sb_()
                                quarter_collective(j)
                            return [sa, s3]
                        softmax_pass(j, (0, 128), kt, mk_dil, oacc[oslot], tk_oacc[oslot], after=after_dil)
                flush_pending()
                phase_end()
                if stop_after == "att%d_%d" % (l, s):
                    break
            if stop_after is not None and stop_after.startswith("att%d" % l):
                break

            A.base = saved_base
            A.top = saved_base
            mx = sb([NC, T], BF16)
            tk_mx = Tk()

            def ld_mx(e, l=l, mx=mx):
                me = e.partition_id() % 4
                return e.dma_start(out=mx, in_=mbuf_all[l].ap()[bass.ds(me, 1)].rearrange("o s (r p) t -> p (o s r) t", p=128))
            if l == 0:
                dump("mx0", mx, [128, NC, T], BF16, [tk_mx])
            NWO = 3
            wo = [sb([NC, 128], BF16) for _ in range(NWO)]
            tk_wo = [Tk() for _ in range(NWO)]

            def mxchunk(n):
                if n < 4:
                    return n
                if n < 8:
                    return 4 + (n - 4)
                m = n - 8
                return (2 + m % 2) * 4 + m // 2

            def ld_wo(dc):
                ws = dc % NWO
                P.op("pool", lambda e, dc=dc, ws=ws, l=l: e.dma_start(
                    out=wo[ws], in_=w_out.ap()[l, :, dc * 128:(dc + 1) * 128].rearrange("(n p) c -> p n c", p=128)),
                    writes=[tk_wo[ws]], dma=True, semkey=("wo", ws))
            ld_wo(0)
            ld_wo(1)
            P.op("pool", ld_mx, reads=[tk_mall[l]], writes=[tk_mx], dma=True, semkey=("mx",))
            for dc in range(NC):
                if dc + 2 < NC:
                    ld_wo(dc + 2)
                ws = dc % NWO
                for th in range(2):
                    b = nb()

                    def mm(e, b=b, ws=ws, th=th, mx=mx):
                        ins = None
                        for n_ in range(NC):
                            ins = e.matmul(banks[b][:], lhsT=wo[ws][:, n_, :], rhs=mx[:, mxchunk(n_), th * 512:(th + 1) * 512],
                                           start=(n_ == 0), stop=(n_ == NC - 1))
                        return ins
                    P.op("pe", mm, reads=[tk_wo[ws], tk_mx], writes=[bank_tk[b]])
                    xsl = xT[:, dc, th * 512:(th + 1) * 512]
                    P.op("dve", lambda e, b=b, xsl=xsl: e.tensor_tensor(out=xsl, in0=xsl, in1=banks[b][:], op=ALU.add),
                         reads=[bank_tk[b], tk_x[dc][th]], writes=[tk_x[dc][th]])
            if l == 0:
                dump("x1", xT, [128, NC, T], F32, [t for c in tk_x for t in c])
            phase_end()
            if stop_after == "oproj%d" % l:
                break

            h2 = sb([NC, T], BF16)
            tk_h2 = [Tk() for _ in range(NC)]
            norm_phase(CF_GFFN + l * 16, h2, tk_h2)
            NWG = 3
            wg = [sb([NC, 128], BF16) for _ in range(NWG)]
            wu = [sb([NC, 128], BF16) for _ in range(NWG)]
            tk_wg = [Tk() for _ in range(NWG)]
            tk_wu = [Tk() for _ in range(NWG)]
            wd = [sb([GFF, D], BF16) for _ in range(2)]
            tk_wd = [Tk() for _ in range(2)]
            actT = [sb([GFF, T], BF16) for _ in range(2)]
            tk_act = [[Tk() for _ in range(GFF)] for _ in range(2)]
            sg = [sb([512], F32) for _ in range(2)]
            tk_sg = [Tk() for _ in range(2)]

            def ld_gu(jc):
                ws = jc % NWG
                P.op("pool", lambda e, jc=jc, ws=ws, l=l: e.dma_start(
                    out=wg[ws], in_=w_gate.ap()[l, :, jc * 128:(jc + 1) * 128].rearrange("(n p) c -> p n c", p=128)),
                    writes=[tk_wg[ws]], dma=True, semkey=("wg", ws))
                P.op("pool", lambda e, jc=jc, ws=ws, l=l: e.dma_start(
                    out=wu[ws], in_=w_up.ap()[l, :, jc * 128:(jc + 1) * 128].rearrange("(n p) c -> p n c", p=128)),
                    writes=[tk_wu[ws]], dma=True, semkey=("wu", ws))

            def ld_wd(gi):
                ws = gi % 2
                P.op("pool", lambda e, gi=gi, ws=ws, l=l: e.dma_start(
                    out=wd[ws], in_=w_down.ap()[l, gi * GFF * 128:(gi + 1) * GFF * 128, :].rearrange("(a p) c -> p a c", p=128)),
                    writes=[tk_wd[ws]], dma=True, semkey=("wd", ws))
            ld_gu(0)
            ld_gu(1)
            ld_wd(0)
            sgc = 0
            NG = NFF // GFF
            for gi in range(NG):
                as_ = gi % 2
                if gi + 1 < NG:
                    ld_wd(gi + 1)
                for jj in range(GFF):
                    jc = gi * GFF + jj
                    if jc + 2 < NFF:
                        ld_gu(jc + 2)
                    ws = jc % NWG
                    for th in range(2):
                        bg, bu = nb(), nb()

                        def mm(e, bg=bg, bu=bu, ws=ws, th=th, h2=h2):
                            ins = None
                            for c in range(NC):
                                ins = e.matmul(banks[bg][:], lhsT=wg[ws][:, c, :], rhs=h2[:, c, th * 512:(th + 1) * 512],
                                               start=(c == 0), stop=(c == NC - 1))
                            for c in range(NC):
                                ins = e.matmul(banks[bu][:], lhsT=wu[ws][:, c, :], rhs=h2[:, c, th * 512:(th + 1) * 512],
                                               start=(c == 0), stop=(c == NC - 1))
                            return ins
                        P.op("pe", mm, reads=[tk_wg[ws], tk_wu[ws]] + tk_h2, writes=[bank_tk[bg], bank_tk[bu]])
                        ss = sgc % 2
                        sgc += 1
                        P.op("act", lambda e, bg=bg, ss=ss: e.activation(out=sg[ss], in_=banks[bg][:], func=AF.Silu),
                             reads=[bank_tk[bg]], writes=[tk_sg[ss]])
                        adst = actT[as_][:, jj, th * 512:(th + 1) * 512]
                        P.op("dve", lambda e, bu=bu, ss=ss, adst=adst: e.tensor_tensor(out=adst, in0=sg[ss], in1=banks[bu][:], op=ALU.mult),
                             reads=[tk_sg[ss], bank_tk[bu]], writes=[tk_act[as_][jj]])
                wsd = gi % 2
                for dc in range(NC):
                    for th in range(2):
                        b = nb()

                        def mmd(e, b=b, dc=dc, th=th, as_=as_, wsd=wsd):
                            ins = None
                            for jj in range(GFF):
                                ins = e.matmul(banks[b][:], lhsT=wd[wsd][:, jj, dc * 128:(dc + 1) * 128],
                                               rhs=actT[as_][:, jj, th * 512:(th + 1) * 512], start=(jj == 0), stop=(jj == GFF - 1))
                            return ins
                        P.op("pe", mmd, reads=[tk_wd[wsd]] + tk_act[as_], writes=[bank_tk[b]])
                        xsl = xT[:, dc, th * 512:(th + 1) * 512]
                        P.op("dve", lambda e, b=b, xsl=xsl: e.tensor_tensor(out=xsl, in0=xsl, in1=banks[b][:], op=ALU.add),
                             reads=[bank_tk[b], tk_x[dc][th]], writes=[tk_x[dc][th]])
            if l == 0:
                dump("x2", xT, [128, NC, T], F32, [t for c in tk_x for t in c])
            phase_end()
            if stop_after == "ffn%d" % l:
                break

    except StopBuild:
        pass

    if stop_after is None:
        yT = sb([NC, T], F32)
        tk_y = [Tk() for _ in range(NC)]
        norm_phase(CF_GFIN, yT, tk_y)
        ot = [sb([D], F32) for _ in range(2)]
        tk_ot = [Tk() for _ in range(2)]
        ev = 0
        for ti in range(8):
            os_ = ti % 2
            for cg in range(4):
                b = nb()

                def tr(e, b=b, cg=cg, ti=ti):
                    ins = None
                    for i in range(4):
                        c = cg * 4 + i
                        ins = e.transpose(out=banks[b][:, i * 128:(i + 1) * 128], in_=yT[:, c, ti * 128:(ti + 1) * 128], identity=ident_f)
                    return ins
                P.op("pe", tr, reads=[tk_y[cg * 4 + i] for i in range(4)] + [tk_cf], writes=[bank_tk[b]])
                dsl = ot[os_][:, cg * 512:(cg + 1) * 512]
                if ev % 2 == 0:
                    P.op("act", lambda e, b=b, dsl=dsl: e.activation(out=dsl, in_=banks[b][:], func=AF.Copy),
                         reads=[bank_tk[b]], writes=[tk_ot[os_]])
                else:
                    P.op("dve", lambda e, b=b, dsl=dsl: e.tensor_copy(out=dsl, in_=banks[b][:]),
                         reads=[bank_tk[b]], writes=[tk_ot[os_]])
                ev += 1
            P.op("sp", lambda e, ti=ti, os_=os_: e.dma_start(out=out_t.ap()[ti * 128:(ti + 1) * 128, :], in_=ot[os_]),
                 reads=[tk_ot[os_]], writes=[tk_out], dma=True, semkey=("ot", os_))
    else:
        zt = sb([D], F32)
        tkz = Tk()
        P.op("dve", lambda e: e.memset(zt, 0.0), writes=[tkz])
        for ti in range(8):
            P.op("sp", lambda e, ti=ti: e.dma_start(out=out_t.ap()[ti * 128:(ti + 1) * 128, :], in_=zt),
                 reads=[tkz], writes=[tk_out], dma=True)
    P.barrier()

    P.emit(nc, stack)
    stack.close()
    return nc, dbg_out


def make_in_maps(inp, need_ffn=True):
    x = np.asarray(inp["x"], np.float32)
    positions = np.asarray(inp["positions"], np.int32)
    w_in = np.asarray(inp["w_in"], np.float32)
    cbv = build_cb()
    cfv = build_cf(inp)
    w_out = np.ascontiguousarray(np.asarray(inp["w_out"], np.float32))
    w_gate = np.ascontiguousarray(np.asarray(inp["w_gate"], np.float32))
    w_up = np.ascontiguousarray(np.asarray(inp["w_up"], np.float32))
    w_down = np.ascontiguousarray(np.asarray(inp["w_down"], np.float32))
    maps = []
    for c in range(8):
        b, g = c // 4, c % 4
        qk_cols, v_cols = [], []
        bases = [(0, g), (1536, g), (3072, 2 * g), (3072, 2 * g + 1)]
        widths = [512, 512, 1024, 1024]
        for (base, h), wdt in zip(bases, widths):
            qk_cols.append(np.arange(base + h * 128, base + (h + 1) * 128))
            qk_cols.append(np.arange(base + wdt + h * 128, base + wdt + (h + 1) * 128))
            v_cols.append(np.arange(base + 2 * wdt + h * 128, base + 2 * wdt + (h + 1) * 128))
        cols = np.concatenate(qk_cols + v_cols)
        maps.append({
            "x": np.ascontiguousarray(x[b, g * T:(g + 1) * T, :]),
            "pos": np.ascontiguousarray(positions[b][None, :]),
            "cb": cbv,
            "cf": cfv,
            "w_in": np.ascontiguousarray(w_in[:, :, cols]),
            "w_out": w_out,
        })
        if need_ffn:
            maps[-1].update({"w_gate": w_gate, "w_up": w_up, "w_down": w_down})
    return maps


_CACHE = {}


def kernel(**inputs):
    if "nc" not in _CACHE:
        _CACHE["nc"] = build_program()[0]
    nc = _CACHE["nc"]
    maps = make_in_maps(inputs)
    res = run_bass_kernel_spmd(nc, maps, core_ids=list(range(8)))
    out = np.zeros((2, S, D), np.float32)
    for c in range(8):
        b, g = c // 4, c % 4
        out[b, g * T:(g + 1) * T, :] = np.asarray(res.results[c]["out"], np.float32)
    return out
```

```python
import math
import os
from contextlib import ExitStack

import numpy as np
import ml_dtypes

import concourse.bass as bass
import concourse.mybir as mybir
from concourse.bass_utils import run_bass_kernel_spmd

F32 = mybir.dt.float32
BF16 = mybir.dt.bfloat16
I32 = mybir.dt.int32
AF = mybir.ActivationFunctionType
ALU = mybir.AluOpType
AX = mybir.AxisListType

D = 2048
S = 4096
T = 1024
NL = 2
DFF = 5632
NFF = DFF // 128
NC = 16
EPS = 1e-6
THETA = 500000.0
PI = math.pi
GFF = 4

CB_ID, CB_ONES, CB_NUINC, CB_NONES, CB_PDIL, CB_PDIFF = 0, 128, 256, 384, 512, 640
CB_TOEP = 768
TOEP_W = 2944
CB_CINC = CB_TOEP + TOEP_W
CB_CSTR = CB_CINC + 896
NB = CB_CSTR + 896
CF_ID = 0
CF_GMIX = 128
CF_GFFN = 160
CF_GFIN = 192
CF_GSB = 208
CF_GDF = 210
CF_GDL = 212
CF_INVF = 214
CF_SGN = 216
CF_EPS = 218
CF_ONE = 219
CF_LNA = 220
CF_LAM = 224
NF = CF_LAM + NL * 4 * 64


def lambda_init(l):
    return 0.8 - 0.6 * math.exp(-0.3 * l)


class StopBuild(Exception):
    pass


class Tk:
    __slots__ = ("name", "w", "r", "multi")

    def __init__(self, name="", multi=False):
        self.name = name
        self.w = [] if multi else None
        self.r = []
        self.multi = multi


class Op:
    __slots__ = ("eng", "fn", "deps", "dma", "semkey", "val", "inc", "target", "sem")

    def __init__(self, eng, fn, dma, inc):
        self.eng = eng
        self.fn = fn
        self.deps = set()
        self.dma = dma
        self.semkey = None
        self.val = 0
        self.inc = inc
        self.target = False
        self.sem = None


class Prog:
    ENGS = ["pe", "act", "dve", "pool", "sp"]

    def __init__(self):
        self.stream = {e: [] for e in self.ENGS}
        self.semcount = {}
        self.since_barrier = []

    def op(self, eng, fn, reads=(), writes=(), dma=False, inc=16, semkey=None, bar=True):
        o = Op(eng, fn, dma, inc)
        deps = set()
        for t in list(reads) + list(writes):
            if t.multi:
                if t not in writes:
                    deps.update(t.w)
            elif t.w is not None:
                deps.add(t.w)
        for t in writes:
            deps.update(t.r)
        o.deps = deps
        for t in writes:
            if t.multi:
                t.w.append(o)
            else:
                t.w = o
                t.r = []
        for t in reads:
            t.r.append(o)
        if dma:
            if semkey is None:
                semkey = id(writes[0])
            o.semkey = semkey
            self.semcount[semkey] = self.semcount.get(semkey, 0) + inc
            o.val = self.semcount[semkey]
        self.stream[eng].append(o)
        if bar:
            self.since_barrier.append(o)
        return o

    def barrier(self):
        lasts = []
        for e in self.ENGS:
            for o in reversed(self.stream[e]):
                if not o.dma and o.fn is not None:
                    lasts.append(o)
                    break
        dmas = [o for o in self.since_barrier if o.dma]
        self.since_barrier = []
        for e in self.ENGS:
            o = Op(e, None, False, 0)
            o.deps = set(lasts) | set(dmas)
            self.stream[e].append(o)

    def emit(self, nc, stack):
        for e in self.ENGS:
            for o in self.stream[e]:
                for d in o.deps:
                    if not d.dma:
                        d.target = True
        engsem = {e: stack.enter_context(nc.semaphore("es_" + e)) for e in self.ENGS}
        keysem = {}
        for e in self.ENGS:
            cnt = 0
            for o in self.stream[e]:
                if o.dma:
                    if o.semkey not in keysem:
                        keysem[o.semkey] = stack.enter_context(nc.semaphore("ds%d" % len(keysem)))
                    o.sem = keysem[o.semkey]
                else:
                    if o.target:
                        cnt += 1
                    o.val = cnt
                    o.sem = engsem[e]
        self.nsem = len(keysem) + len(engsem)
        print('semaphores used:', self.nsem)
        block = stack.enter_context(nc.Block())
        prog = self

        def run(e):
            def body(eng):
                seen = {}
                for o in prog.stream[e]:
                    need = {}
                    for d in o.deps:
                        if e == "pe" and d.eng == "pe" and not d.dma:
                            continue
                        k = id(d.sem)
                        if k not in need or need[k][1] < d.val:
                            need[k] = (d.sem, d.val)
                    for k, (sem, val) in need.items():
                        if seen.get(k, 0) < val:
                            eng.wait_ge(sem, val)
                            seen[k] = val
                    if o.fn is None:
                        continue
                    inst = o.fn(eng)
                    if o.dma:
                        if o.inc == 1:
                            inst.then_inc(o.sem)
                        else:
                            inst.then_inc(o.sem, o.inc)
                    elif o.target:
                        inst.then_inc(o.sem, 1)
            return body

        block.tensor(run("pe"))
        block.scalar(run("act"))
        block.vector(run("dve"))
        block.gpsimd(run("pool"))
        block.sync(run("sp"))


def _dil_count(delta):
    c = np.zeros_like(delta, dtype=np.float32)
    c += ((delta >= 0) & (delta <= 128)).astype(np.float32)
    c += ((delta >= 0) & (delta <= 512) & (delta % 4 == 0)).astype(np.float32)
    c += ((delta >= 0) & (delta <= 2048) & (delta % 16 == 0)).astype(np.float32)
    return c


def build_cb():
    cb = np.zeros((128, NB), np.float32)
    j = np.arange(128)[:, None]
    k = np.arange(128)[None, :]
    cb[:, CB_ID:CB_ID + 128] = np.eye(128)
    cb[:, CB_ONES:CB_ONES + 128] = 1.0
    cb[:, CB_NUINC:CB_NUINC + 128] = -(j >= k).astype(np.float32)
    cb[:, CB_NONES:CB_NONES + 128] = -1.0
    pd = np.zeros((128, 128), np.float32)
    for d in range(32):
        partner = d + 16 if d < 16 else d - 16
        pd[partner, d] = 1.0
    cb[:, CB_PDIL:CB_PDIL + 128] = pd
    pf = np.zeros((128, 128), np.float32)
    for base in (0, 64):
        for d in range(16):
            partner = d + 8 if d < 8 else d - 8
            pf[base + partner, base + d] = 1.0
    cb[:, CB_PDIFF:CB_PDIFF + 128] = pf
    ki = np.arange(128)[:, None]
    xx = np.arange(TOEP_W)[None, :]
    cb[:, CB_TOEP:CB_TOEP + TOEP_W] = _dil_count(xx - 384 - ki)
    xx = np.arange(896)[None, :]
    cb[:, CB_CINC:CB_CINC + 896] = ((xx - 384 - ki) >= 0).astype(np.float32)
    cb[:, CB_CSTR:CB_CSTR + 896] = ((xx - 384 - ki) >= 1).astype(np.float32)
    return cb.astype(ml_dtypes.bfloat16)


def build_cf(inp):
    cf = np.zeros((128, NF), np.float32)
    cf[:, CF_ID:CF_ID + 128] = np.eye(128)

    def cols(v):
        return np.ascontiguousarray(np.asarray(v, np.float32).reshape(16, 128).T)

    for l in range(NL):
        cf[:, CF_GMIX + l * 16:CF_GMIX + (l + 1) * 16] = cols(inp["norm_mix_g"][l])
        cf[:, CF_GFFN + l * 16:CF_GFFN + (l + 1) * 16] = cols(inp["norm_ffn_g"][l])
        cf[:, CF_GSB + l] = np.asarray(inp["g_sb_out"][l], np.float32)
        cf[:, CF_GDF + l] = np.asarray(inp["g_diff_out"][l], np.float32)
        cf[:, CF_GDL + l] = np.asarray(inp["g_dil_out"][l], np.float32)
        cf[:, CF_LNA + l] = math.log(1.0 - lambda_init(l))
        for i, nm in enumerate(("lambda_q1", "lambda_k1", "lambda_q2", "lambda_k2")):
            o = CF_LAM + (l * 4 + i) * 64
            cf[:, o:o + 64] = np.asarray(inp[nm][l], np.float32)[None, :]
    cf[:, CF_GFIN:CF_GFIN + 16] = cols(inp["norm_final_g"])
    invd = np.zeros(128, np.float32)
    sgd = np.zeros(128, np.float32)
    fr = (np.float32(THETA) ** (-np.arange(16, dtype=np.float32) / np.float32(16))).astype(np.float32)
    for d in range(32):
        invd[d] = fr[d % 16]
        sgd[d] = -1.0 if d < 16 else 1.0
    invf = np.zeros(128, np.float32)
    sgf = np.zeros(128, np.float32)
    fr8 = (np.float32(THETA) ** (-np.arange(8, dtype=np.float32) / np.float32(8))).astype(np.float32)
    for base in (0, 64):
        for d in range(16):
            invf[base + d] = fr8[d % 8]
            sgf[base + d] = -1.0 if d < 8 else 1.0
    cf[:, CF_INVF] = invd
    cf[:, CF_INVF + 1] = invf
    cf[:, CF_SGN] = sgd
    cf[:, CF_SGN + 1] = sgf
    cf[:, CF_EPS] = EPS
    cf[:, CF_ONE] = 1.0
    return cf


def build_program(stop_after=None, dbg=None):
    nc = bass.Bass("TRN2", target_bir_lowering=False)
    P = Prog()
    dbg = dbg or []
    dbg_out = {}

    def dram(name, shape, dt, kind=None):
        if kind:
            return nc.dram_tensor(name, shape, dt, kind=kind)
        return nc.dram_tensor(name, shape, dt)

    x_in = dram("x", [T, D], F32, "ExternalInput")
    pos_in = dram("pos", [1, S], I32, "ExternalInput")
    cb_in = dram("cb", [128, NB], BF16, "ExternalInput")
    cf_in = dram("cf", [128, NF], F32, "ExternalInput")
    w_in = dram("w_in", [NL, D, 1536], F32, "ExternalInput")
    w_out = dram("w_out", [NL, D, D], F32, "ExternalInput")
    need_ffn = stop_after is None or stop_after.startswith("ffn") or stop_after.startswith("h1") or stop_after.startswith("att1") or stop_after.startswith("oproj1")
    if need_ffn:
        w_gate = dram("w_gate", [NL, D, DFF], F32, "ExternalInput")
        w_up = dram("w_up", [NL, D, DFF], F32, "ExternalInput")
        w_down = dram("w_down", [NL, DFF, D], F32, "ExternalInput")
    out_t = dram("out", [T, D], F32, "ExternalOutput")

    hbuf_in = [dram("hbuf_in%d" % l, [NC, 128, T], BF16) for l in range(NL)]
    hbuf_all = [dram("hbuf_all%d" % l, [NC, 512, T], BF16) for l in range(NL)]
    mbuf_in = [dram("mbuf_in%d" % l, [4, 4, 128, T], BF16) for l in range(NL)]
    mbuf_all = [dram("mbuf_all%d" % l, [4, 4, 512, T], BF16) for l in range(NL)]
    ropeC = [dram("ropeC%d" % i, [128, S], F32) for i in range(2)]
    ropeS = [dram("ropeS%d" % i, [128, S], F32) for i in range(2)]
    tk_hin = [[Tk("hin") for _ in range(NC)] for _ in range(NL)]
    tk_hall = [Tk("hall", multi=True) for _ in range(NL)]
    tk_min = [Tk("min", multi=True) for _ in range(NL)]
    tk_mall = [Tk("mall", multi=True) for _ in range(NL)]
    tk_rope = Tk("rope", multi=True)
    tk_out = Tk("out", multi=True)
    tk_dbg = Tk("dbg", multi=True)

    stack = ExitStack()
    ARENA_BYTES = int(os.environ.get("ARENA_KB", "200")) * 1024
    arena = stack.enter_context(nc.sbuf_tensor("arena", [128, ARENA_BYTES // 2], BF16))
    psum = stack.enter_context(nc.psum_tensor("psum", [128, 8 * 512], F32))
    banks = [psum[:, i * 512:(i + 1) * 512] for i in range(8)]
    bank_tk = [Tk("bank%d" % i) for i in range(8)]

    class Alloc:
        def __init__(self):
            self.base = 0
            self.top = 0

        def take(self, nbytes):
            nbytes = (nbytes + 63) // 64 * 64
            off = self.top
            self.top += nbytes
            assert self.top <= ARENA_BYTES, ("SBUF overflow", self.top)
            return off

        def persist(self):
            self.base = self.top

        def reset(self):
            self.top = self.base

    A = Alloc()

    def sb(shape, dt):
        n = int(np.prod(shape))
        esz = 2 if dt == BF16 else 4
        off = A.take(n * esz)
        v = arena[:, off // 2: off // 2 + n * esz // 2]
        if dt != BF16:
            v = v.bitcast(dt)
        if len(shape) == 2:
            v = v.rearrange("p (a b) -> p a b", a=shape[0])
        elif len(shape) == 3:
            v = v.rearrange("p (a b c) -> p a b c", a=shape[0], b=shape[1])
        return v

    bank_rr = [0]

    def nb(lo=0, hi=8):
        i = lo + bank_rr[0] % (hi - lo)
        bank_rr[0] += 1
        return i

    def phase_end():
        P.barrier()
        A.reset()

    def dump(name, ap_sb, shape, dt, reads):
        if name not in dbg:
            return
        t = dram("dbg_" + name, shape, dt, "ExternalOutput")
        dbg_out[name] = t
        P.op("sp", lambda e, t=t, a=ap_sb: e.dma_start(out=t.ap(), in_=a), reads=reads, writes=[tk_dbg], dma=True)

    cb = sb([NB], BF16)
    cf = sb([NF], F32)
    xT = sb([NC, T], F32)
    neglam = sb([NL], F32)
    tk_cb, tk_cf, tk_neglam = Tk("cb"), Tk("cf"), Tk("neglam")
    tk_x = [[Tk("xT") for _ in range(2)] for _ in range(NC)]
    A.persist()

    ident_f = cf[:, CF_ID:CF_ID + 128]
    ident_b = cb[:, CB_ID:CB_ID + 128]
    ones_b = cb[:, CB_ONES:CB_ONES + 128]

    def cfc(col):
        return cf[:, col:col + 1]

    try:
        P.op("sp", lambda e: e.dma_start(out=cb, in_=cb_in.ap()), writes=[tk_cb], dma=True)
        P.op("sp", lambda e: e.dma_start(out=cf, in_=cf_in.ap()), writes=[tk_cf], dma=True)

        lt = sb([4, 64], F32)
        ls = sb([8], F32)
        tk_lt, tk_ls = Tk(), Tk()
        for l in range(NL):
            for i in range(2):
                a0 = CF_LAM + (l * 4 + 2 * i) * 64
                P.op("dve", lambda e, i=i, a0=a0: e.tensor_tensor(out=lt[:, i, :], in0=cf[:, a0:a0 + 64],
                                                                   in1=cf[:, a0 + 64:a0 + 128], op=ALU.mult),
                     reads=[tk_cf], writes=[tk_lt])
                P.op("dve", lambda e, i=i: e.reduce_sum(out=ls[:, i:i + 1], in_=lt[:, i, :], axis=AX.X),
                     reads=[tk_lt], writes=[tk_ls])
                P.op("act", lambda e, i=i: e.activation(out=ls[:, 2 + i:3 + i], in_=ls[:, i:i + 1], func=AF.Exp),
                     reads=[tk_ls], writes=[tk_ls])
            P.op("dve", lambda e: e.tensor_tensor(out=ls[:, 4:5], in0=ls[:, 2:3], in1=ls[:, 3:4], op=ALU.subtract),
                 reads=[tk_ls], writes=[tk_ls])
            P.op("dve", lambda e, l=l: e.tensor_scalar(out=neglam[:, l:l + 1], in0=ls[:, 4:5], scalar1=-1.0,
                                                       scalar2=-lambda_init(l), op0=ALU.mult, op1=ALU.add),
                 reads=[tk_ls], writes=[tk_neglam])

        xs = [sb([D], F32) for _ in range(8)]
        tk_xs = [Tk() for _ in range(8)]
        for i in range(8):
            P.op("sp", lambda e, i=i: e.dma_start(out=xs[i], in_=x_in.ap()[i * 128:(i + 1) * 128, :]),
                 writes=[tk_xs[i]], dma=True)
        ev = 0
        for c in range(NC):
            for tg in range(2):
                b = nb()

                def tr(e, c=c, tg=tg, b=b):
                    ins = None
                    for i in range(4):
                        ins = e.transpose(out=banks[b][:, i * 128:(i + 1) * 128],
                                          in_=xs[tg * 4 + i][:, c * 128:(c + 1) * 128], identity=ident_f)
                    return ins
                P.op("pe", tr, reads=[tk_xs[tg * 4 + i] for i in range(4)] + [tk_cf], writes=[bank_tk[b]])
                dst = xT[:, c, tg * 512:(tg + 1) * 512]
                if ev % 2 == 0:
                    P.op("act", lambda e, b=b, dst=dst: e.activation(out=dst, in_=banks[b][:], func=AF.Copy),
                         reads=[bank_tk[b]], writes=[tk_x[c][tg]])
                else:
                    P.op("dve", lambda e, b=b, dst=dst: e.tensor_copy(out=dst, in_=banks[b][:]),
                         reads=[bank_tk[b]], writes=[tk_x[c][tg]])
                ev += 1
        phase_end()
        if stop_after == 'p0a':
            raise StopBuild()

        def rope_phase():
            posi = sb([S], I32)
            posf = sb([S], F32)
            tk_posi, tk_posf = Tk(), Tk()
            P.op("sp", lambda e: e.dma_start(out=posi, in_=pos_in.ap().partition_broadcast(128)), writes=[tk_posi], dma=True)
            P.op("dve", lambda e: e.tensor_copy(out=posf, in_=posi), reads=[tk_posi], writes=[tk_posf])
            RW = 512
            rt = {k: [sb([RW], I32 if k == "ki" else F32) for _ in range(2)] for k in ("a", "y", "ki", "kf", "r", "m", "sn")}
            rtk = {k: [Tk() for _ in range(2)] for k in rt}
            it = 0
            for ty in range(2):
                for kind in range(2):
                    for ch in range(S // RW):
                        s_ = it % 2
                        it += 1
                        cs = slice(ch * RW, (ch + 1) * RW)
                        a, y, ki, kf, r, m, sn = (rt[k][s_] for k in ("a", "y", "ki", "kf", "r", "m", "sn"))
                        ta, ty_, tki, tkf, tr_, tm, tsn = (rtk[k][s_] for k in ("a", "y", "ki", "kf", "r", "m", "sn"))
                        off = PI / 2 if kind == 0 else 0.0
                        P.op("dve", lambda e, a=a, cs=cs, ty=ty, off=off: e.tensor_scalar(
                            out=a, in0=posf[:, cs], scalar1=cfc(CF_INVF + ty), scalar2=off, op0=ALU.mult, op1=ALU.add),
                            reads=[tk_posf, tk_cf], writes=[ta])
                        P.op("dve", lambda e, a=a, y=y: e.tensor_scalar(out=y, in0=a, scalar1=1.0 / (2 * PI), scalar2=None,
                                                                         op0=ALU.mult), reads=[ta], writes=[ty_])
                        P.op("dve", lambda e, y=y, ki=ki: e.tensor_copy(out=ki, in_=y), reads=[ty_], writes=[tki])
                        P.op("dve", lambda e, kf=kf, ki=ki: e.tensor_copy(out=kf, in_=ki), reads=[tki], writes=[tkf])
                        P.op("dve", lambda e, kf=kf, a=a, r=r: e.scalar_tensor_tensor(
                            out=r, in0=kf, scalar=-2 * PI, in1=a, op0=ALU.mult, op1=ALU.add), reads=[tkf, ta], writes=[tr_])
                        P.op("dve", lambda e, r=r, m=m: e.tensor_scalar(out=m, in0=r, scalar1=PI, scalar2=2 * PI,
                                                                         op0=ALU.is_gt, op1=ALU.mult), reads=[tr_], writes=[tm])
                        P.op("dve", lambda e, r=r, m=m: e.tensor_tensor(out=r, in0=r, in1=m, op=ALU.subtract),
                             reads=[tm, tr_], writes=[tr_])
                        P.op("dve", lambda e, r=r: e.tensor_scalar(out=r, in0=r, scalar1=PI, scalar2=-PI,
                                                                    op0=ALU.min, op1=ALU.max), reads=[tr_], writes=[tr_])
                        P.op("act", lambda e, r=r, sn=sn: e.activation(out=sn, in_=r, func=AF.Sin), reads=[tr_], writes=[tsn])
                        if kind == 1:
                            P.op("dve", lambda e, sn=sn, ty=ty: e.tensor_scalar(out=sn, in0=sn, scalar1=cfc(CF_SGN + ty),
                                                                                 scalar2=None, op0=ALU.mult),
                                 reads=[tsn, tk_cf], writes=[tsn])
                        dst = (ropeC if kind == 0 else ropeS)[ty]
                        P.op("sp", lambda e, dst=dst, cs=cs, sn=sn: e.dma_start(out=dst.ap()[:, cs], in_=sn),
                             reads=[tsn], writes=[tk_rope], dma=True, semkey=("sn", s_))
            phase_end()

        def norm_phase(gcol, hT, tk_h, out_dt_is_bf16=True):
            sq = [sb([T], BF16) for _ in range(2)]
            tk_sq = [Tk() for _ in range(2)]
            lnv = sb([T], F32)
            rstd = sb([T], F32)
            tk_lnv, tk_rstd = Tk(), Tk()
            b0, b1 = nb(), nb()
            bs = (b0, b1)
            for c in range(NC):
                s_ = c % 2
                P.op("act", lambda e, c=c, s_=s_: e.activation(out=sq[s_], in_=xT[:, c, :], func=AF.Square),
                     reads=[tk_x[c][0], tk_x[c][1]], writes=[tk_sq[s_]])

                def mm(e, c=c, s_=s_):
                    ins = None
                    for th in range(2):
                        ins = e.matmul(banks[bs[th]][:], lhsT=ones_b, rhs=sq[s_][:, th * 512:(th + 1) * 512],
                                       start=(c == 0), stop=(c == NC - 1))
                    return ins
                P.op("pe", mm, reads=[tk_sq[s_], tk_cb], writes=[bank_tk[b0], bank_tk[b1]])
            for th in range(2):
                P.op("act", lambda e, th=th: e.activation(out=lnv[:, th * 512:(th + 1) * 512], in_=banks[bs[th]][:],
                                                         func=AF.Ln, bias=cfc(CF_EPS), scale=1.0 / D),
                     reads=[bank_tk[bs[th]], tk_cf], writes=[tk_lnv])
            P.op("act", lambda e: e.activation(out=rstd, in_=lnv, func=AF.Exp, scale=-0.5), reads=[tk_lnv], writes=[tk_rstd])
            for c in range(NC):
                P.op("dve", lambda e, c=c: e.scalar_tensor_tensor(out=hT[:, c, :], in0=xT[:, c, :], scalar=cfc(gcol + c),
                                                                  in1=rstd, op0=ALU.mult, op1=ALU.mult),
                     reads=[tk_x[c][0], tk_x[c][1], tk_rstd, tk_cf], writes=[tk_h[c]])

        hn_calls = [0]

        def head_norm_steps(o, tk_o, gcol, lnbias_col, dst_dram_ap, tk_dst, ring, msb=4):
            sq, tk_sq, ln_, tk_ln, rs, tk_rs, mo, tk_mo = ring
            b = msb
            hn_calls[0] += 1
            skey = ("mo", hn_calls[0] % 2)

            def step_a():
                P.op("act", lambda e: e.activation(out=sq, in_=o, func=AF.Square), reads=[tk_o], writes=[tk_sq])
                P.op("pe", lambda e, b=b: e.matmul(banks[b][:], lhsT=ones_b, rhs=sq, start=True, stop=True),
                     reads=[tk_sq, tk_cb], writes=[bank_tk[b]])

            def step_b():
                P.op("act", lambda e, b=b: e.activation(out=ln_, in_=banks[b][:], func=AF.Ln, bias=cfc(CF_EPS), scale=1.0 / 128),
                     reads=[bank_tk[b], tk_cf], writes=[tk_ln])
                if lnbias_col is None:
                    P.op("act", lambda e: e.activation(out=rs, in_=ln_, func=AF.Exp, scale=-0.5), reads=[tk_ln], writes=[tk_rs])
                else:
                    P.op("act", lambda e: e.activation(out=rs, in_=ln_, func=AF.Exp, scale=-0.5, bias=cfc(lnbias_col)),
                         reads=[tk_ln, tk_cf], writes=[tk_rs])
                P.op("dve", lambda e: e.scalar_tensor_tensor(out=mo, in0=o, scalar=cfc(gcol), in1=rs, op0=ALU.mult, op1=ALU.mult),
                     reads=[tk_o, tk_rs, tk_cf], writes=[tk_mo])
                P.op("sp", lambda e: e.dma_start(out=dst_dram_ap, in_=mo), reads=[tk_mo], writes=[tk_dst], dma=True, semkey=skey)
            return [step_a, step_b]

        def head_norm(*args, **kw):
            for st in head_norm_steps(*args, **kw):
                st()

        for l in range(NL):
            saved_base = A.base
            wq_ring = [sb([3, NC, 128], BF16) for _ in range(2)]
            tk_w_ring = [[Tk() for _ in range(3)] for _ in range(2)]
            ht_pf = sb([NC, 512], BF16)
            tk_htpf = Tk()
            A.persist()

            def load_ht0(l=l, ht_pf=ht_pf, tk_htpf=tk_htpf):
                P.op("sp", lambda e: e.dma_start(out=ht_pf, in_=hbuf_all[l].ap()[:, 0:128, 0:512].rearrange("c p t -> p c t")),
                     reads=[tk_hall[l]], writes=[tk_htpf], dma=True, semkey=("htpf",), bar=False)

            def load_wq(s_, l=l, wq_ring=wq_ring, tk_w_ring=tk_w_ring):
                wcols_ = (2 * s_ * 128, (2 * s_ + 1) * 128, 1024 + s_ * 128)
                for i in range(3):
                    P.op("pool", lambda e, i=i, wcols_=wcols_, s_=s_: e.dma_start(
                        out=wq_ring[s_ % 2][:, i], in_=w_in.ap()[l, :, wcols_[i]:wcols_[i] + 128].rearrange("(c p) n -> p c n", p=128)),
                        writes=[tk_w_ring[s_ % 2][i]], dma=True, semkey=("wq", s_ % 2, i))
            load_wq(0)
            hT = sb([NC, T], BF16)
            tk_h = [Tk() for _ in range(NC)]
            norm_phase(CF_GMIX + l * 16, hT, tk_h)
            for c in range(NC):
                P.op("sp", lambda e, l=l, hT=hT, c=c: e.dma_start(out=hbuf_in[l].ap()[c], in_=hT[:, c, :]),
                     reads=[tk_h[c]], writes=[tk_hin[l][c]], dma=True, semkey=("hin", c))
                P.op("pool", lambda e, l=l, c=c: e.collective_compute("AllGather", ALU.bypass, replica_groups=[[0, 1, 2, 3], [4, 5, 6, 7]],
                                                                     ins=[hbuf_in[l].ap()[c]], outs=[hbuf_all[l].ap()[c]]),
                     reads=[tk_hin[l][c]], writes=[tk_hall[l]], dma=True, inc=1, bar=False)
            if l == 0:
                dump("h0", hT, [128, NC, T], BF16, tk_h)
            phase_end()
            if l == 0:
                rope_phase()
            load_ht0()
            if stop_after == "h%d" % l:
                break

            for s in range(4):
                kind = ("sb", "diff", "dil", "dil")[s]
                rope_ty = {"sb": None, "diff": 1, "dil": 0}[kind]
                qscale = 0.125 if kind == "diff" else 128 ** -0.5
                perm = None if rope_ty is None else cb[:, (CB_PDIL if rope_ty == 0 else CB_PDIFF):][:, 0:128]

                wq = wq_ring[s % 2]
                tk_w = tk_w_ring[s % 2]
                qT = sb([S], BF16)
                kT = sb([S], BF16)
                vv = sb([32, 128], BF16)
                tk_q = [Tk() for _ in range(8)]
                tk_k = [Tk() for _ in range(8)]
                tk_v = [Tk() for _ in range(8)]
                mark_ = A.top
                ht = [sb([NC, 512], BF16) for _ in range(2)]
                tk_ht = [Tk() for _ in range(2)]
                rC = [sb([512], F32) for _ in range(2)]
                rS = [sb([512], F32) for _ in range(2)]
                tk_rC = [Tk() for _ in range(2)]
                tk_rS = [Tk() for _ in range(2)]
                raw = [sb([512], BF16) for _ in range(2)]
                tk_raw = [Tk() for _ in range(2)]
                t1 = [sb([512], F32) for _ in range(2)]
                t2 = [sb([512], F32) for _ in range(2)]
                tk_t1 = [Tk() for _ in range(2)]
                tk_t2 = [Tk() for _ in range(2)]

                ri = 0
                for tt in range(8):
                    hs = tt % 2
                    r_, off = tt // 2, (tt % 2) * 512
                    if tt == 0:
                        hsrc, tk_hsrc = ht_pf, tk_htpf
                    else:
                        hsrc, tk_hsrc = ht[hs], tk_ht[hs]
                        P.op("sp", lambda e, l=l, r_=r_, off=off, hs=hs, ht=ht: e.dma_start(
                            out=ht[hs], in_=hbuf_all[l].ap()[:, r_ * 128:(r_ + 1) * 128, off:off + 512].rearrange("c p t -> p c t")),
                            reads=[tk_hall[l]], writes=[tk_ht[hs]], dma=True, semkey=("ht", hs))
                    if rope_ty is not None:
                        P.op("sp", lambda e, tt=tt, hs=hs, rC=rC, rope_ty=rope_ty: e.dma_start(
                            out=rC[hs], in_=ropeC[rope_ty].ap()[:, tt * 512:(tt + 1) * 512]),
                            reads=[tk_rope], writes=[tk_rC[hs]], dma=True, semkey=("rC", hs))
                        P.op("sp", lambda e, tt=tt, hs=hs, rS=rS, rope_ty=rope_ty: e.dma_start(
                            out=rS[hs], in_=ropeS[rope_ty].ap()[:, tt * 512:(tt + 1) * 512]),
                            reads=[tk_rope], writes=[tk_rS[hs]], dma=True, semkey=("rS", hs))
                    post = []
                    for qi, (dstT, tkd, sc) in enumerate(((qT, tk_q, qscale), (kT, tk_k, 1.0))):
                        b = nb()

                        def mm(e, b=b, qi=qi, wq=wq, hsrc=hsrc):
                            ins = None
                            for c in range(NC):
                                ins = e.matmul(banks[b][:], lhsT=wq[:, qi, c, :], rhs=hsrc[:, c, :],
                                               start=(c == 0), stop=(c == NC - 1))
                            return ins
                        P.op("pe", mm, reads=[tk_w[qi], tk_hsrc], writes=[bank_tk[b]])
                        dst = dstT[:, tt * 512:(tt + 1) * 512]
                        if rope_ty is None:
                            P.op("act", lambda e, b=b, dst=dst, sc=sc: e.activation(out=dst, in_=banks[b][:], func=AF.Copy, scale=sc),
                                 reads=[bank_tk[b]], writes=[tkd[tt]])
                        else:
                            rs_ = ri % 2
                            ri += 1
                            P.op("act", lambda e, b=b, rs_=rs_, sc=sc, raw=raw: e.activation(out=raw[rs_], in_=banks[b][:], func=AF.Copy, scale=sc),
                                 reads=[bank_tk[b]], writes=[tk_raw[rs_]])
                            def post_fn(rs_=rs_, hs=hs, dst=dst, tkd=tkd, tt=tt, perm=perm, raw=raw, t1=t1, t2=t2, rS=rS, rC=rC,
                                        tk_raw=tk_raw, tk_t1=tk_t1, tk_t2=tk_t2, tk_rS=tk_rS, tk_rC=tk_rC):
                                b2 = nb()
                                P.op("pe", lambda e, b2=b2, rs_=rs_, perm=perm, raw=raw: e.matmul(banks[b2][:], lhsT=perm, rhs=raw[rs_], start=True, stop=True),
                                     reads=[tk_raw[rs_], tk_cb], writes=[bank_tk[b2]])
                                P.op("dve", lambda e, b2=b2, rs_=rs_, hs=hs, t1=t1, rS=rS: e.tensor_tensor(out=t1[rs_], in0=banks[b2][:], in1=rS[hs], op=ALU.mult),
                                     reads=[bank_tk[b2], tk_rS[hs]], writes=[tk_t1[rs_]])
                                P.op("pool", lambda e, rs_=rs_, hs=hs, t2=t2, raw=raw, rC=rC: e.tensor_tensor(out=t2[rs_], in0=raw[rs_], in1=rC[hs], op=ALU.mult),
                                     reads=[tk_raw[rs_], tk_rC[hs]], writes=[tk_t2[rs_]])
                                P.op("dve", lambda e, rs_=rs_, dst=dst, t1=t1, t2=t2: e.tensor_tensor(out=dst, in0=t1[rs_], in1=t2[rs_], op=ALU.add),
                                     reads=[tk_t1[rs_], tk_t2[rs_]], writes=[tkd[tt]])
                            post.append(post_fn)
                    b = nb()

                    def mmv(e, b=b, wq=wq, hsrc=hsrc):
                        ins = None
                        for sub in range(4):
                            for c in range(NC):
                                ins = e.matmul(banks[b][:, sub * 128:(sub + 1) * 128], lhsT=hsrc[:, c, sub * 128:(sub + 1) * 128],
                                               rhs=wq[:, 2, c, :], start=(c == 0), stop=(c == NC - 1))
                        return ins
                    P.op("pe", mmv, reads=[tk_w[2], tk_hsrc], writes=[bank_tk[b]])
                    P.op("dve", lambda e, b=b, tt=tt, vv=vv: e.tensor_copy(out=vv[:, tt * 4:(tt + 1) * 4, :],
                                                                            in_=banks[b][:].rearrange("p (a b) -> p a b", a=4)),
                         reads=[bank_tk[b]], writes=[tk_v[tt]])
                    for pf_ in post:
                        pf_()
                if l == 0 and s in (0, 1, 2):
                    dump("q%d" % s, qT, [128, S], BF16, tk_q)
                    dump("k%d" % s, kT, [128, S], BF16, tk_k)
                    dump("v%d" % s, vv, [128, 32, 128], BF16, tk_v)

                P.barrier()
                A.top = mark_
                if s + 1 < 4:
                    load_wq(s + 1)
                    load_ht0()
                NE2 = 3
                E2 = [sb([1024], BF16) for _ in range(NE2)]
                Em2 = [sb([1024], BF16) for _ in range(NE2)]
                tk_E2 = [Tk() for _ in range(NE2)]
                tk_Em2 = [[Tk(), Tk()] for _ in range(NE2)]
                pcnt = [0]
                oacc = [sb([512], F32) for _ in range(3)]
                tk_oacc = [Tk() for _ in range(3)]
                rl = sb([512], F32)
                tk_rl = Tk()
                hn_ring = []
                for _ in range(2):
                    hn_ring.append((sb([512], BF16), Tk(), sb([512], F32), Tk(), sb([512], F32), Tk(), sb([512], BF16), Tk()))
                if kind == "sb":
                    ef2 = [sb([1024], F32) for _ in range(2)]
                    tk_ef2 = [Tk() for _ in range(2)]
                    sp2 = [sb([1024], BF16) for _ in range(3)]
                    tk_sp2 = [Tk() for _ in range(3)]
                    spm2 = [sb([1024], BF16) for _ in range(3)]
                    tk_spm2 = [[Tk(), Tk()] for _ in range(3)]
                    Ts = [sb([512], BF16) for _ in range(4)]
                    tk_Ts = [Tk() for _ in range(4)]
                    tmpT = [sb([512], BF16) for _ in range(2)]
                    tk_tmpT = [Tk() for _ in range(2)]
                OB, LB = 6, 7
                ecnt = [0]
                hcnt = [0]

                pending = []
                olcnt = [0]
                if kind != "sb":
                    lnl = [sb([512], F32) for _ in range(2)]
                    tk_lnl = [Tk() for _ in range(2)]
                    oraw = [sb([512], F32) for _ in range(2)]
                    tk_oraw = [Tk() for _ in range(2)]

                def flush_pending():
                    while pending:
                        pending.pop(0)()

                def run_pending_step():
                    if pending:
                        pending.pop(0)()

                def softmax_pass(j, parts, ktiles, maskfn, o_dst, tk_odst, after=None):
                    p0, p1 = parts
                    qsl = qT[p0:p1, j * 512:(j + 1) * 512]
                    npair = len(ktiles) // 2
                    assert len(ktiles) % 2 == 0
                    sb_of = {}
                    OB, LB = 6, 7
                    ek = olcnt[0] % 2
                    olcnt[0] += 1

                    def issue_s(pi):
                        pb = (pcnt[0] % 2) * 2
                        pcnt[0] += 1
                        sb_of[pi] = pb
                        i0, i1 = ktiles[2 * pi], ktiles[2 * pi + 1]

                        def mm(e, pb=pb, i0=i0, i1=i1):
                            e.matmul(banks[pb][:], lhsT=kT[p0:p1, i0 * 128:(i0 + 1) * 128], rhs=qsl, start=True, stop=True)
                            return e.matmul(banks[pb + 1][:], lhsT=kT[p0:p1, i1 * 128:(i1 + 1) * 128], rhs=qsl, start=True, stop=True)
                        P.op("pe", mm, reads=[tk_k[i0 // 4], tk_k[i1 // 4], tk_q[j]], writes=[bank_tk[pb], bank_tk[pb + 1]])
                    issue_s(0)
                    if npair > 1:
                        issue_s(1)
                    for pi in range(npair):
                        if pi >= 1:
                            run_pending_step()
                        pb = sb_of.pop(pi)
                        es = ecnt[0] % NE2
                        ecnt[0] += 1
                        P.op("act", lambda e, pb=pb, es=es: e.activation(out=E2[es], in_=psum[:, pb * 512:(pb + 2) * 512], func=AF.Exp),
                             reads=[bank_tk[pb], bank_tk[pb + 1]], writes=[tk_E2[es]])
                        srcs = []
                        for h_ in range(2):
                            i = ktiles[2 * pi + h_]
                            hsl = slice(h_ * 512, (h_ + 1) * 512)
                            mk = maskfn(i)
                            if mk is not None:
                                P.op("dve", lambda e, es=es, mk=mk, hsl=hsl: e.tensor_tensor(out=Em2[es][:, hsl], in0=E2[es][:, hsl], in1=mk, op=ALU.mult),
                                     reads=[tk_E2[es], tk_cb], writes=[tk_Em2[es][h_]])
                                srcs.append((Em2[es][:, hsl], tk_Em2[es][h_], i))
                            else:
                                srcs.append((E2[es][:, hsl], tk_E2[es], i))

                        if pi + 2 < npair:
                            issue_s(pi + 2)

                        def mmo(e, srcs=srcs, pi=pi, OB=OB, LB=LB):
                            first, last = (pi == 0), (pi == npair - 1)
                            (s0, _, i0), (s1, _, i1) = srcs
                            e.matmul(banks[OB][:], lhsT=vv[:, i0, :], rhs=s0, start=first, stop=False)
                            e.matmul(banks[OB][:], lhsT=vv[:, i1, :], rhs=s1, start=False, stop=last)
                            e.matmul(banks[LB][:], lhsT=ones_b, rhs=s0, start=first, stop=False)
                            return e.matmul(banks[LB][:], lhsT=ones_b, rhs=s1, start=False, stop=last)
                        P.op("pe", mmo, reads=[srcs[0][1], srcs[1][1], tk_v[srcs[0][2] // 4], tk_v[srcs[1][2] // 4], tk_cb],
                             writes=[bank_tk[OB], bank_tk[LB]])
                    flush_pending()
                    P.op("act", lambda e, ek=ek: e.activation(out=lnl[ek], in_=banks[LB][:], func=AF.Ln), reads=[bank_tk[LB]], writes=[tk_lnl[ek]])
                    P.op("dve", lambda e, ek=ek: e.tensor_copy(out=oraw[ek], in_=banks[OB][:]), reads=[bank_tk[OB]], writes=[tk_oraw[ek]])

                    def epilogue(ek=ek):
                        P.op("act", lambda e: e.activation(out=rl, in_=lnl[ek], func=AF.Exp, scale=-1.0), reads=[tk_lnl[ek]], writes=[tk_rl])
                        P.op("dve", lambda e: e.tensor_tensor(out=o_dst, in0=oraw[ek], in1=rl, op=ALU.mult),
                             reads=[tk_oraw[ek], tk_rl], writes=[tk_odst])
                    pending.append(epilogue)
                    if after is not None:
                        pending.extend(after())

                def sb_head():
                    cstr = cb[:, CB_CSTR:CB_CSTR + 896]
                    nuinc = cb[:, CB_NUINC:CB_NUINC + 128]
                    nones = cb[:, CB_NONES:CB_NONES + 128]
                    SOB = 7
                    units = []
                    for j in range(8):
                        tiles = list(range(4 * j + 3, -1, -1))
                        npj = len(tiles) // 2
                        for p in range(npj):
                            units.append((j, p, tiles[2 * p], tiles[2 * p + 1], npj))
                    G = len(units)
                    st1 = {}
                    st2 = {}

                    def mask_of(i, j):
                        if i >= 4 * j:
                            Dq = 512 * j - 128 * i
                            return cstr[:, Dq + 384:Dq + 384 + 512]
                        return None

                    def stage1(g):
                        j, p, iA, iB, npj = units[g]
                        qsl = qT[:, j * 512:(j + 1) * 512]

                        def mmz(e, iA=iA, iB=iB, qsl=qsl):
                            e.matmul(banks[0][:], lhsT=kT[:, iA * 128:(iA + 1) * 128], rhs=qsl, start=True, stop=True)
                            return e.matmul(banks[1][:], lhsT=kT[:, iB * 128:(iB + 1) * 128], rhs=qsl, start=True, stop=True)
                        P.op("pe", mmz, reads=[tk_k[iA // 4], tk_k[iB // 4], tk_q[j]], writes=[bank_tk[0], bank_tk[1]])
                        fs = g % 2
                        P.op("act", lambda e, fs=fs: e.activation(out=ef2[fs], in_=psum[:, 0:1024], func=AF.Exp),
                             reads=[bank_tk[0], bank_tk[1]], writes=[tk_ef2[fs]])
                        ss = g % 3
                        P.op("act", lambda e, fs=fs, ss=ss: e.activation(out=sp2[ss], in_=ef2[fs], func=AF.Ln, bias=cfc(CF_ONE)),
                             reads=[tk_ef2[fs], tk_cf], writes=[tk_sp2[ss]])
                        curs = []
                        for h_, i in enumerate((iA, iB)):
                            hsl = slice(h_ * 512, (h_ + 1) * 512)
                            mk = mask_of(i, j)
                            if mk is not None:
                                P.op("dve", lambda e, ss=ss, mk=mk, hsl=hsl: e.tensor_tensor(out=spm2[ss][:, hsl], in0=sp2[ss][:, hsl], in1=mk, op=ALU.mult),
                                     reads=[tk_sp2[ss], tk_cb], writes=[tk_spm2[ss][h_]])
                                curs.append((spm2[ss][:, hsl], tk_spm2[ss][h_]))
                            else:
                                curs.append((sp2[ss][:, hsl], tk_sp2[ss]))
                        tprev, tnew = g % 4, (g + 1) % 4
                        if p + 1 < npj:
                            (cA, tA), (cB, tB) = curs
                            if p == 0:
                                P.op("dve", lambda e, cA=cA, cB=cB, tnew=tnew: e.tensor_tensor(out=Ts[tnew], in0=cA, in1=cB, op=ALU.add),
                                     reads=[tA, tB], writes=[tk_Ts[tnew]])
                            else:
                                ts_ = g % 2
                                P.op("dve", lambda e, cA=cA, cB=cB, ts_=ts_: e.tensor_tensor(out=tmpT[ts_], in0=cA, in1=cB, op=ALU.add),
                                     reads=[tA, tB], writes=[tk_tmpT[ts_]])
                                P.op("dve", lambda e, ts_=ts_, tprev=tprev, tnew=tnew: e.tensor_tensor(out=Ts[tnew], in0=Ts[tprev], in1=tmpT[ts_], op=ALU.add),
                                     reads=[tk_tmpT[ts_], tk_Ts[tprev]], writes=[tk_Ts[tnew]])
                        st1[g] = (curs, tprev)

                    def stage2(g):
                        j, p, iA, iB, npj = units[g]
                        qsl = qT[:, j * 512:(j + 1) * 512]
                        curs, tprev = st1.pop(g)
                        (cA, tA), (cB, tB) = curs
                        lb = 2 + 2 * (g % 2)

                        def mml(e, lb=lb, iA=iA, iB=iB, cA=cA, cB=cB, tprev=tprev, p=p, qsl=qsl):
                            e.matmul(banks[lb][:], lhsT=kT[:, iA * 128:(iA + 1) * 128], rhs=qsl, start=True, stop=False)
                            ins = e.matmul(banks[lb][:], lhsT=nuinc, rhs=cA, start=False, stop=(p == 0))
                            if p > 0:
                                ins = e.matmul(banks[lb][:], lhsT=nones, rhs=Ts[tprev], start=False, stop=True)
                            e.matmul(banks[lb + 1][:], lhsT=kT[:, iB * 128:(iB + 1) * 128], rhs=qsl, start=True, stop=False)
                            e.matmul(banks[lb + 1][:], lhsT=nuinc, rhs=cB, start=False, stop=False)
                            ins = e.matmul(banks[lb + 1][:], lhsT=nones, rhs=cA, start=False, stop=(p == 0))
                            if p > 0:
                                ins = e.matmul(banks[lb + 1][:], lhsT=nones, rhs=Ts[tprev], start=False, stop=True)
                            return ins
                        rd = [tk_k[iA // 4], tk_k[iB // 4], tk_q[j], tA, tB, tk_cb] + ([tk_Ts[tprev]] if p > 0 else [])
                        P.op("pe", mml, reads=rd, writes=[bank_tk[lb], bank_tk[lb + 1]])
                        es = g % NE2
                        P.op("act", lambda e, lb=lb, es=es: e.activation(out=E2[es], in_=psum[:, lb * 512:(lb + 2) * 512], func=AF.Exp),
                             reads=[bank_tk[lb], bank_tk[lb + 1]], writes=[tk_E2[es]])
                        srcs = []
                        for h_, i in enumerate((iA, iB)):
                            hsl = slice(h_ * 512, (h_ + 1) * 512)
                            mk = mask_of(i, j)
                            if mk is not None:
                                P.op("dve", lambda e, es=es, mk=mk, hsl=hsl: e.tensor_tensor(out=Em2[es][:, hsl], in0=E2[es][:, hsl], in1=mk, op=ALU.mult),
                                     reads=[tk_E2[es], tk_cb], writes=[tk_Em2[es][h_]])
                                srcs.append((Em2[es][:, hsl], tk_Em2[es][h_], i))
                            else:
                                srcs.append((E2[es][:, hsl], tk_E2[es], i))
                        st2[g] = srcs

                    def stage2b(g):
                        j, p, iA, iB, npj = units[g]
                        (s0, t0_, i0), (s1, t1_, i1) = st2.pop(g)

                        def mmav(e, s0=s0, s1=s1, i0=i0, i1=i1, p=p, npj=npj):
                            e.matmul(banks[SOB][:], lhsT=vv[:, i0, :], rhs=s0, start=(p == 0), stop=False)
                            return e.matmul(banks[SOB][:], lhsT=vv[:, i1, :], rhs=s1, start=False, stop=(p == npj - 1))
                        P.op("pe", mmav, reads=[t0_, t1_, tk_v[i0 // 4], tk_v[i1 // 4]], writes=[bank_tk[SOB]])
                        if p == npj - 1:
                            oslot = j % 2
                            P.op("act", lambda e, oslot=oslot: e.activation(out=oacc[oslot], in_=banks[SOB][:], func=AF.Copy),
                                 reads=[bank_tk[SOB]], writes=[tk_oacc[oslot]])
                            dst = mbuf_in[l].ap()[j // 2, s, :, (j % 2) * 512:(j % 2) * 512 + 512]
                            sa, sb_ = head_norm_steps(oacc[oslot], tk_oacc[oslot], CF_GSB + l, None, dst, tk_min[l], hn_ring[j % 2], msb=6)

                            def s3(sb_=sb_, j=j):
                                sb_()
                                quarter_collective(j)
                            pending.extend([sa, s3])

                    stage1(0)
                    stage1(1)
                    for g in range(G):
                        run_pending_step()
                        stage2(g)
                        if g + 2 < G:
                            stage1(g + 2)
                        if g >= 1:
                            stage2b(g - 1)
                    stage2b(G - 1)
                    flush_pending()

                cinc = cb[:, CB_CINC:CB_CINC + 896]
                toep = cb[:, CB_TOEP:CB_TOEP + TOEP_W]
                def quarter_collective(j, l=l, s=s):
                    if j % 2 == 1:
                        qt = j // 2
                        P.op("pool", lambda e, l=l, qt=qt, s=s: e.collective_compute("AllGather", ALU.bypass, replica_groups=[[0, 1, 2, 3], [4, 5, 6, 7]],
                                                                                    ins=[mbuf_in[l].ap()[qt, s]], outs=[mbuf_all[l].ap()[qt, s]]),
                             reads=[tk_min[l]], writes=[tk_mall[l]], dma=True, inc=1, bar=False)

                if kind == "sb":
                    sb_head()
                for j in range(8 if kind != "sb" else 0):
                    dst = mbuf_in[l].ap()[j // 2, s, :, (j % 2) * 512:(j % 2) * 512 + 512]
                    ring = hn_ring[hcnt[0] % 2]
                    hcnt[0] += 1
                    if kind == "diff":
                        def mk_diff(i, j=j):
                            if i >= 4 * j:
                                Dq = 512 * j - 128 * i
                                return cinc[:, Dq + 384:Dq + 384 + 512]
                            return None
                        kt = list(range(0, 4 * j + 4))

                        def after_diff(j=j, dst=dst, ring=ring, l=l):
                            sa, sb_ = head_norm_steps(oacc[2], tk_oacc[2], CF_GDF + l, CF_LNA + l, dst, tk_min[l], ring, msb=4)

                            def s2():
                                P.op("dve", lambda e, l=l: e.scalar_tensor_tensor(out=oacc[2], in0=oacc[1], scalar=neglam[:, l:l + 1], in1=oacc[0],
                                                                                 op0=ALU.mult, op1=ALU.add),
                                     reads=[tk_oacc[0], tk_oacc[1], tk_neglam], writes=[tk_oacc[2]])
                                sa()

                            def s3():
                                sb_()
                                quarter_collective(j)
                            return [s2, s3]
                        softmax_pass(j, (0, 64), kt, mk_diff, oacc[0], tk_oacc[0])
                        softmax_pass(j, (64, 128), kt, mk_diff, oacc[1], tk_oacc[1], after=after_diff)
                    else:
                        def mk_dil(i, j=j):
                            Dq = 512 * j - 128 * i
                            return toep[:, Dq + 384:Dq + 384 + 512]
                        kt = list(range(max(0, 4 * j - 16), 4 * j + 4))
                        oslot = j % 2

                        def after_dil(j=j, dst=dst, ring=ring, l=l, oslot=oslot):
                            sa, sb_ = head_norm_steps(oacc[oslot], tk_oacc[oslot], CF_GDL + l, None, dst, tk_min[l], ring, msb=4)

                            def s3():
                                sb_()
                                quarter_collective(j)
                            return [sa, s3]
                        softmax_pass(j, (0, 128), kt, mk_dil, oacc[oslot], tk_oacc[oslot], after=after_dil)
                flush_pending()
                phase_end()
                if stop_after == "att%d_%d" % (l, s):
                    break
            if stop_after is not None and stop_after.startswith("att%d" % l):
                break

            A.base = saved_base
            A.top = saved_base
            mx = sb([NC, T], BF16)
            tk_mx = Tk()

            def ld_mx(e, l=l, mx=mx):
                me = e.partition_id() % 4
                return e.dma_start(out=mx, in_=mbuf_all[l].ap()[bass.ds(me, 1)].rearrange("o s (r p) t -> p (o s r) t", p=128))
            if l == 0:
                dump("mx0", mx, [128, NC, T], BF16, [tk_mx])
            NWO = 3
            wo = [sb([NC, 128], BF16) for _ in range(NWO)]
            tk_wo = [Tk() for _ in range(NWO)]

            def mxchunk(n):
                if n < 4:
                    return n
                if n < 8:
                    return 4 + (n - 4)
                m = n - 8
                return (2 + m % 2) * 4 + m // 2

            def ld_wo(dc):
                ws = dc % NWO
                P.op("pool", lambda e, dc=dc, ws=ws, l=l: e.dma_start(
                    out=wo[ws], in_=w_out.ap()[l, :, dc * 128:(dc + 1) * 128].rearrange("(n p) c -> p n c", p=128)),
                    writes=[tk_wo[ws]], dma=True, semkey=("wo", ws))
            ld_wo(0)
            ld_wo(1)
            P.op("pool", ld_mx, reads=[tk_mall[l]], writes=[tk_mx], dma=True, semkey=("mx",))
            for dc in range(NC):
                if dc + 2 < NC:
                    ld_wo(dc + 2)
                ws = dc % NWO
                for th in range(2):
                    b = nb()

                    def mm(e, b=b, ws=ws, th=th, mx=mx):
                        ins = None
                        for n_ in range(NC):
                            ins = e.matmul(banks[b][:], lhsT=wo[ws][:, n_, :], rhs=mx[:, mxchunk(n_), th * 512:(th + 1) * 512],
                                           start=(n_ == 0), stop=(n_ == NC - 1))
                        return ins
                    P.op("pe", mm, reads=[tk_wo[ws], tk_mx], writes=[bank_tk[b]])
                    xsl = xT[:, dc, th * 512:(th + 1) * 512]
                    P.op("dve", lambda e, b=b, xsl=xsl: e.tensor_tensor(out=xsl, in0=xsl, in1=banks[b][:], op=ALU.add),
                         reads=[bank_tk[b], tk_x[dc][th]], writes=[tk_x[dc][th]])
            if l == 0:
                dump("x1", xT, [128, NC, T], F32, [t for c in tk_x for t in c])
            phase_end()
            if stop_after == "oproj%d" % l:
                break

            h2 = sb([NC, T], BF16)
            tk_h2 = [Tk() for _ in range(NC)]
            norm_phase(CF_GFFN + l * 16, h2, tk_h2)
            NWG = 3
            wg = [sb([NC, 128], BF16) for _ in range(NWG)]
            wu = [sb([NC, 128], BF16) for _ in range(NWG)]
            tk_wg = [Tk() for _ in range(NWG)]
            tk_wu = [Tk() for _ in range(NWG)]
            wd = [sb([GFF, D], BF16) for _ in range(2)]
            tk_wd = [Tk() for _ in range(2)]
            actT = [sb([GFF, T], BF16) for _ in range(2)]
            tk_act = [[Tk() for _ in range(GFF)] for _ in range(2)]
            sg = [sb([512], F32) for _ in range(2)]
            tk_sg = [Tk() for _ in range(2)]

            def ld_gu(jc):
                ws = jc % NWG
                P.op("pool", lambda e, jc=jc, ws=ws, l=l: e.dma_start(
                    out=wg[ws], in_=w_gate.ap()[l, :, jc * 128:(jc + 1) * 128].rearrange("(n p) c -> p n c", p=128)),
                    writes=[tk_wg[ws]], dma=True, semkey=("wg", ws))
                P.op("pool", lambda e, jc=jc, ws=ws, l=l: e.dma_start(
                    out=wu[ws], in_=w_up.ap()[l, :, jc * 128:(jc + 1) * 128].rearrange("(n p) c -> p n c", p=128)),
                    writes=[tk_wu[ws]], dma=True, semkey=("wu", ws))

            def ld_wd(gi):
                ws = gi % 2
                P.op("pool", lambda e, gi=gi, ws=ws, l=l: e.dma_start(
                    out=wd[ws], in_=w_down.ap()[l, gi * GFF * 128:(gi + 1) * GFF * 128, :].rearrange("(a p) c -> p a c", p=128)),
                    writes=[tk_wd[ws]], dma=True, semkey=("wd", ws))
            ld_gu(0)
            ld_gu(1)
            ld_wd(0)
            sgc = 0
            NG = NFF // GFF
            for gi in range(NG):
                as_ = gi % 2
                if gi + 1 < NG:
                    ld_wd(gi + 1)
                for jj in range(GFF):
                    jc = gi * GFF + jj
                    if jc + 2 < NFF:
                        ld_gu(jc + 2)
                    ws = jc % NWG
                    for th in range(2):
                        bg, bu = nb(), nb()

                        def mm(e, bg=bg, bu=bu, ws=ws, th=th, h2=h2):
                            ins = None
                            for c in range(NC):
                                ins = e.matmul(banks[bg][:], lhsT=wg[ws][:, c, :], rhs=h2[:, c, th * 512:(th + 1) * 512],
                                               start=(c == 0), stop=(c == NC - 1))
                            for c in range(NC):
                                ins = e.matmul(banks[bu][:], lhsT=wu[ws][:, c, :], rhs=h2[:, c, th * 512:(th + 1) * 512],
                                               start=(c == 0), stop=(c == NC - 1))
                            return ins
                        P.op("pe", mm, reads=[tk_wg[ws], tk_wu[ws]] + tk_h2, writes=[bank_tk[bg], bank_tk[bu]])
                        ss = sgc % 2
                        sgc += 1
                        P.op("act", lambda e, bg=bg, ss=ss: e.activation(out=sg[ss], in_=banks[bg][:], func=AF.Silu),
                             reads=[bank_tk[bg]], writes=[tk_sg[ss]])
                        adst = actT[as_][:, jj, th * 512:(th + 1) * 512]
                        P.op("dve", lambda e, bu=bu, ss=ss, adst=adst: e.tensor_tensor(out=adst, in0=sg[ss], in1=banks[bu][:], op=ALU.mult),
                             reads=[tk_sg[ss], bank_tk[bu]], writes=[tk_act[as_][jj]])
                wsd = gi % 2
                for dc in range(NC):
                    for th in range(2):
                        b = nb()

                        def mmd(e, b=b, dc=dc, th=th, as_=as_, wsd=wsd):
                            ins = None
                            for jj in range(GFF):
                                ins = e.matmul(banks[b][:], lhsT=wd[wsd][:, jj, dc * 128:(dc + 1) * 128],
                                               rhs=actT[as_][:, jj, th * 512:(th + 1) * 512], start=(jj == 0), stop=(jj == GFF - 1))
                            return ins
                        P.op("pe", mmd, reads=[tk_wd[wsd]] + tk_act[as_], writes=[bank_tk[b]])
                        xsl = xT[:, dc, th * 512:(th + 1) * 512]
                        P.op("dve", lambda e, b=b, xsl=xsl: e.tensor_tensor(out=xsl, in0=xsl, in1=banks[b][:], op=ALU.add),
                             reads=[bank_tk[b], tk_x[dc][th]], writes=[tk_x[dc][th]])
            if l == 0:
                dump("x2", xT, [128, NC, T], F32, [t for c in tk_x for t in c])
            phase_end()
            if stop_after == "ffn%d" % l:
                break

    except StopBuild:
        pass

    if stop_after is None:
        yT = sb([NC, T], F32)
        tk_y = [Tk() for _ in range(NC)]
        norm_phase(CF_GFIN, yT, tk_y)
        ot = [sb([D], F32) for _ in range(2)]
        tk_ot = [Tk() for _ in range(2)]
        ev = 0
        for ti in range(8):
            os_ = ti % 2
            for cg in range(4):
                b = nb()

                def tr(e, b=b, cg=cg, ti=ti):
                    ins = None
                    for i in range(4):
                        c = cg * 4 + i
                        ins = e.transpose(out=banks[b][:, i * 128:(i + 1) * 128], in_=yT[:, c, ti * 128:(ti + 1) * 128], identity=ident_f)
                    return ins
                P.op("pe", tr, reads=[tk_y[cg * 4 + i] for i in range(4)] + [tk_cf], writes=[bank_tk[b]])
                dsl = ot[os_][:, cg * 512:(cg + 1) * 512]
                if ev % 2 == 0:
                    P.op("act", lambda e, b=b, dsl=dsl: e.activation(out=dsl, in_=banks[b][:], func=AF.Copy),
                         reads=[bank_tk[b]], writes=[tk_ot[os_]])
                else:
                    P.op("dve", lambda e, b=b, dsl=dsl: e.tensor_copy(out=dsl, in_=banks[b][:]),
                         reads=[bank_tk[b]], writes=[tk_ot[os_]])
                ev += 1
            P.op("sp", lambda e, ti=ti, os_=os_: e.dma_start(out=out_t.ap()[ti * 128:(ti + 1) * 128, :], in_=ot[os_]),
                 reads=[tk_ot[os_]], writes=[tk_out], dma=True, semkey=("ot", os_))
    else:
        zt = sb([D], F32)
        tkz = Tk()
        P.op("dve", lambda e: e.memset(zt, 0.0), writes=[tkz])
        for ti in range(8):
            P.op("sp", lambda e, ti=ti: e.dma_start(out=out_t.ap()[ti * 128:(ti + 1) * 128, :], in_=zt),
                 reads=[tkz], writes=[tk_out], dma=True)
    P.barrier()

    P.emit(nc, stack)
    stack.close()
    return nc, dbg_out


def make_in_maps(inp, need_ffn=True):
    x = np.asarray(inp["x"], np.float32)
    positions = np.asarray(inp["positions"], np.int32)
    w_in = np.asarray(inp["w_in"], np.float32)
    cbv = build_cb()
    cfv = build_cf(inp)
    w_out = np.ascontiguousarray(np.asarray(inp["w_out"], np.float32))
    w_gate = np.ascontiguousarray(np.asarray(inp["w_gate"], np.float32))
    w_up = np.ascontiguousarray(np.asarray(inp["w_up"], np.float32))
    w_down = np.ascontiguousarray(np.asarray(inp["w_down"], np.float32))
    maps = []
    for c in range(8):
        b, g = c // 4, c % 4
        qk_cols, v_cols = [], []
        bases = [(0, g), (1536, g), (3072, 2 * g), (3072, 2 * g + 1)]
        widths = [512, 512, 1024, 1024]
        for (base, h), wdt in zip(bases, widths):
            qk_cols.append(np.arange(base + h * 128, base + (h + 1) * 128))
            qk_cols.append(np.arange(base + wdt + h * 128, base + wdt + (h + 1) * 128))
            v_cols.append(np.arange(base + 2 * wdt + h * 128, base + 2 * wdt + (h + 1) * 128))
        cols = np.concatenate(qk_cols + v_cols)
        maps.append({
            "x": np.ascontiguousarray(x[b, g * T:(g + 1) * T, :]),
            "pos": np.ascontiguousarray(positions[b][None, :]),
            "cb": cbv,
            "cf": cfv,
            "w_in": np.ascontiguousarray(w_in[:, :, cols]),
            "w_out": w_out,
        })
        if need_ffn:
            maps[-1].update({"w_gate": w_gate, "w_up": w_up, "w_down": w_down})
    return maps


_CACHE = {}


def kernel(**inputs):
    if "nc" not in _CACHE:
        _CACHE["nc"] = build_program()[0]
    nc = _CACHE["nc"]
    maps = make_in_maps(inputs)
    res = run_bass_kernel_spmd(nc, maps, core_ids=list(range(8)))
    out = np.zeros((2, S, D), np.float32)
    for c in range(8):
        b, g = c // 4, c % 4
        out[b, g * T:(g + 1) * T, :] = np.asarray(res.results[c]["out"], np.float32)
    return out
```

```python
import math
import os
from contextlib import ExitStack

import numpy as np
import ml_dtypes

import concourse.bass as bass
import concourse.mybir as mybir
from concourse.bass_utils import run_bass_kernel_spmd

F32 = mybir.dt.float32
BF16 = mybir.dt.bfloat16
I32 = mybir.dt.int32
AF = mybir.ActivationFunctionType
ALU = mybir.AluOpType
AX = mybir.AxisListType

D = 2048
S = 4096
T = 1024
NL = 2
DFF = 5632
NFF = DFF // 128
NC = 16
EPS = 1e-6
THETA = 500000.0
PI = math.pi
GFF = 4

CB_ID, CB_ONES, CB_NUINC, CB_NONES, CB_PDIL, CB_PDIFF = 0, 128, 256, 384, 512, 640
CB_TOEP = 768
TOEP_W = 2944
CB_CINC = CB_TOEP + TOEP_W
CB_CSTR = CB_CINC + 896
NB = CB_CSTR + 896
CF_ID = 0
CF_GMIX = 128
CF_GFFN = 160
CF_GFIN = 192
CF_GSB = 208
CF_GDF = 210
CF_GDL = 212
CF_INVF = 214
CF_SGN = 216
CF_EPS = 218
CF_ONE = 219
CF_LNA = 220
CF_LAM = 224
NF = CF_LAM + NL * 4 * 64


def lambda_init(l):
    return 0.8 - 0.6 * math.exp(-0.3 * l)


class StopBuild(Exception):
    pass


class Tk:
    __slots__ = ("name", "w", "r", "multi")

    def __init__(self, name="", multi=False):
        self.name = name
        self.w = [] if multi else None
        self.r = []
        self.multi = multi


class Op:
    __slots__ = ("eng", "fn", "deps", "dma", "semkey", "val", "inc", "target", "sem")

    def __init__(self, eng, fn, dma, inc):
        self.eng = eng
        self.fn = fn
        self.deps = set()
        self.dma = dma
        self.semkey = None
        self.val = 0
        self.inc = inc
        self.target = False
        self.sem = None


class Prog:
    ENGS = ["pe", "act", "dve", "pool", "sp"]

    def __init__(self):
        self.stream = {e: [] for e in self.ENGS}
        self.semcount = {}
        self.since_barrier = []

    def op(self, eng, fn, reads=(), writes=(), dma=False, inc=16, semkey=None, bar=True):
        o = Op(eng, fn, dma, inc)
        deps = set()
        for t in list(reads) + list(writes):
            if t.multi:
                if t not in writes:
                    deps.update(t.w)
            elif t.w is not None:
                deps.add(t.w)
        for t in writes:
            deps.update(t.r)
        o.deps = deps
        for t in writes:
            if t.multi:
                t.w.append(o)
            else:
                t.w = o
                t.r = []
        for t in reads:
            t.r.append(o)
        if dma:
            if semkey is None:
                semkey = id(writes[0])
            o.semkey = semkey
            self.semcount[semkey] = self.semcount.get(semkey, 0) + inc
            o.val = self.semcount[semkey]
        self.stream[eng].append(o)
        if bar:
            self.since_barrier.append(o)
        return o

    def barrier(self):
        lasts = []
        for e in self.ENGS:
            for o in reversed(self.stream[e]):
                if not o.dma and o.fn is not None:
                    lasts.append(o)
                    break
        dmas = [o for o in self.since_barrier if o.dma]
        self.since_barrier = []
        for e in self.ENGS:
            o = Op(e, None, False, 0)
            o.deps = set(lasts) | set(dmas)
            self.stream[e].append(o)

    def emit(self, nc, stack):
        for e in self.ENGS:
            for o in self.stream[e]:
                for d in o.deps:
                    if not d.dma:
                        d.target = True
        engsem = {e: stack.enter_context(nc.semaphore("es_" + e)) for e in self.ENGS}
        keysem = {}
        for e in self.ENGS:
            cnt = 0
            for o in self.stream[e]:
                if o.dma:
                    if o.semkey not in keysem:
                        keysem[o.semkey] = stack.enter_context(nc.semaphore("ds%d" % len(keysem)))
                    o.sem = keysem[o.semkey]
                else:
                    if o.target:
                        cnt += 1
                    o.val = cnt
                    o.sem = engsem[e]
        self.nsem = len(keysem) + len(engsem)
        print('semaphores used:', self.nsem)
        block = stack.enter_context(nc.Block())
        prog = self

        def run(e):
            def body(eng):
                seen = {}
                for o in prog.stream[e]:
                    need = {}
                    for d in o.deps:
                        if e == "pe" and d.eng == "pe" and not d.dma:
                            continue
                        k = id(d.sem)
                        if k not in need or need[k][1] < d.val:
                            need[k] = (d.sem, d.val)
                    for k, (sem, val) in need.items():
                        if seen.get(k, 0) < val:
                            eng.wait_ge(sem, val)
                            seen[k] = val
                    if o.fn is None:
                        continue
                    inst = o.fn(eng)
                    if o.dma:
                        if o.inc == 1:
                            inst.then_inc(o.sem)
                        else:
                            inst.then_inc(o.sem, o.inc)
                    elif o.target:
                        inst.then_inc(o.sem, 1)
            return body

        block.tensor(run("pe"))
        block.scalar(run("act"))
        block.vector(run("dve"))
        block.gpsimd(run("pool"))
        block.sync(run("sp"))


def _dil_count(delta):
    c = np.zeros_like(delta, dtype=np.float32)
    c += ((delta >= 0) & (delta <= 128)).astype(np.float32)
    c += ((delta >= 0) & (delta <= 512) & (delta % 4 == 0)).astype(np.float32)
    c += ((delta >= 0) & (delta <= 2048) & (delta % 16 == 0)).astype(np.float32)
    return c


def build_cb():
    cb = np.zeros((128, NB), np.float32)
    j = np.arange(128)[:, None]
    k = np.arange(128)[None, :]
    cb[:, CB_ID:CB_ID + 128] = np.eye(128)
    cb[:, CB_ONES:CB_ONES + 128] = 1.0
    cb[:, CB_NUINC:CB_NUINC + 128] = -(j >= k).astype(np.float32)
    cb[:, CB_NONES:CB_NONES + 128] = -1.0
    pd = np.zeros((128, 128), np.float32)
    for d in range(32):
        partner = d + 16 if d < 16 else d - 16
        pd[partner, d] = 1.0
    cb[:, CB_PDIL:CB_PDIL + 128] = pd
    pf = np.zeros((128, 128), np.float32)
    for base in (0, 64):
        for d in range(16):
            partner = d + 8 if d < 8 else d - 8
            pf[base + partner, base + d] = 1.0
    cb[:, CB_PDIFF:CB_PDIFF + 128] = pf
    ki = np.arange(128)[:, None]
    xx = np.arange(TOEP_W)[None, :]
    cb[:, CB_TOEP:CB_TOEP + TOEP_W] = _dil_count(xx - 384 - ki)
    xx = np.arange(896)[None, :]
    cb[:, CB_CINC:CB_CINC + 896] = ((xx - 384 - ki) >= 0).astype(np.float32)
    cb[:, CB_CSTR:CB_CSTR + 896] = ((xx - 384 - ki) >= 1).astype(np.float32)
    return cb.astype(ml_dtypes.bfloat16)


def build_cf(inp):
    cf = np.zeros((128, NF), np.float32)
    cf[:, CF_ID:CF_ID + 128] = np.eye(128)

    def cols(v):
        return np.ascontiguousarray(np.asarray(v, np.float32).reshape(16, 128).T)

    for l in range(NL):
        cf[:, CF_GMIX + l * 16:CF_GMIX + (l + 1) * 16] = cols(inp["norm_mix_g"][l])
        cf[:, CF_GFFN + l * 16:CF_GFFN + (l + 1) * 16] = cols(inp["norm_ffn_g"][l])
        cf[:, CF_GSB + l] = np.asarray(inp["g_sb_out"][l], np.float32)
        cf[:, CF_GDF + l] = np.asarray(inp["g_diff_out"][l], np.float32)
        cf[:, CF_GDL + l] = np.asarray(inp["g_dil_out"][l], np.float32)
        cf[:, CF_LNA + l] = math.log(1.0 - lambda_init(l))
        for i, nm in enumerate(("lambda_q1", "lambda_k1", "lambda_q2", "lambda_k2")):
            o = CF_LAM + (l * 4 + i) * 64
            cf[:, o:o + 64] = np.asarray(inp[nm][l], np.float32)[None, :]
    cf[:, CF_GFIN:CF_GFIN + 16] = cols(inp["norm_final_g"])
    invd = np.zeros(128, np.float32)
    sgd = np.zeros(128, np.float32)
    fr = (np.float32(THETA) ** (-np.arange(16, dtype=np.float32) / np.float32(16))).astype(np.float32)
    for d in range(32):
        invd[d] = fr[d % 16]
        sgd[d] = -1.0 if d < 16 else 1.0
    invf = np.zeros(128, np.float32)
    sgf = np.zeros(128, np.float32)
    fr8 = (np.float32(THETA) ** (-np.arange(8, dtype=np.float32) / np.float32(8))).astype(np.float32)
    for base in (0, 64):
        for d in range(16):
            invf[base + d] = fr8[d % 8]
            sgf[base + d] = -1.0 if d < 8 else 1.0
    cf[:, CF_INVF] = invd
    cf[:, CF_INVF + 1] = invf
    cf[:, CF_SGN] = sgd
    cf[:, CF_SGN + 1] = sgf
    cf[:, CF_EPS] = EPS
    cf[:, CF_ONE] = 1.0
    return cf


def build_program(stop_after=None, dbg=None):
    nc = bass.Bass("TRN2", target_bir_lowering=False)
    P = Prog()
    dbg = dbg or []
    dbg_out = {}

    def dram(name, shape, dt, kind=None):
        if kind:
            return nc.dram_tensor(name, shape, dt, kind=kind)
        return nc.dram_tensor(name, shape, dt)

    x_in = dram("x", [T, D], F32, "ExternalInput")
    pos_in = dram("pos", [1, S], I32, "ExternalInput")
    cb_in = dram("cb", [128, NB], BF16, "ExternalInput")
    cf_in = dram("cf", [128, NF], F32, "ExternalInput")
    w_in = dram("w_in", [NL, D, 1536], F32, "ExternalInput")
    w_out = dram("w_out", [NL, D, D], F32, "ExternalInput")
    need_ffn = stop_after is None or stop_after.startswith("ffn") or stop_after.startswith("h1") or stop_after.startswith("att1") or stop_after.startswith("oproj1")
    if need_ffn:
        w_gate = dram("w_gate", [NL, D, DFF], F32, "ExternalInput")
        w_up = dram("w_up", [NL, D, DFF], F32, "ExternalInput")
        w_down = dram("w_down", [NL, DFF, D], F32, "ExternalInput")
    out_t = dram("out", [T, D], F32, "ExternalOutput")

    hbuf_in = [dram("hbuf_in%d" % l, [NC, 128, T], BF16) for l in range(NL)]
    hbuf_all = [dram("hbuf_all%d" % l, [NC, 512, T], BF16) for l in range(NL)]
    mbuf_in = [dram("mbuf_in%d" % l, [4, 4, 128, T], BF16) for l in range(NL)]
    mbuf_all = [dram("mbuf_all%d" % l, [4, 4, 512, T], BF16) for l in range(NL)]
    ropeC = [dram("ropeC%d" % i, [128, S], F32) for i in range(2)]
    ropeS = [dram("ropeS%d" % i, [128, S], F32) for i in range(2)]
    tk_hin = [[Tk("hin") for _ in range(NC)] for _ in range(NL)]
    tk_hall = [Tk("hall", multi=True) for _ in range(NL)]
    tk_min = [Tk("min", multi=True) for _ in range(NL)]
    tk_mall = [Tk("mall", multi=True) for _ in range(NL)]
    tk_rope = Tk("rope", multi=True)
    tk_out = Tk("out", multi=True)
    tk_dbg = Tk("dbg", multi=True)

    stack = ExitStack()
    ARENA_BYTES = int(os.environ.get("ARENA_KB", "200")) * 1024
    arena = stack.enter_context(nc.sbuf_tensor("arena", [128, ARENA_BYTES // 2], BF16))
    psum = stack.enter_context(nc.psum_tensor("psum", [128, 8 * 512], F32))
    banks = [psum[:, i * 512:(i + 1) * 512] for i in range(8)]
    bank_tk = [Tk("bank%d" % i) for i in range(8)]

    class Alloc:
        def __init__(self):
            self.base = 0
            self.top = 0

        def take(self, nbytes):
            nbytes = (nbytes + 63) // 64 * 64
            off = self.top
            self.top += nbytes
            assert self.top <= ARENA_BYTES, ("SBUF overflow", self.top)
            return off

        def persist(self):
            self.base = self.top

        def reset(self):
            self.top = self.base

    A = Alloc()

    def sb(shape, dt):
        n = int(np.prod(shape))
        esz = 2 if dt == BF16 else 4
        off = A.take(n * esz)
        v = arena[:, off // 2: off // 2 + n * esz // 2]
        if dt != BF16:
            v = v.bitcast(dt)
        if len(shape) == 2:
            v = v.rearrange("p (a b) -> p a b", a=shape[0])
        elif len(shape) == 3:
            v = v.rearrange("p (a b c) -> p a b c", a=shape[0], b=shape[1])
        return v

    bank_rr = [0]

    def nb(lo=0, hi=8):
        i = lo + bank_rr[0] % (hi - lo)
        bank_rr[0] += 1
        return i

    def phase_end():
        P.barrier()
        A.reset()

    def dump(name, ap_sb, shape, dt, reads):
        if name not in dbg:
            return
        t = dram("dbg_" + name, shape, dt, "ExternalOutput")
        dbg_out[name] = t
        P.op("sp", lambda e, t=t, a=ap_sb: e.dma_start(out=t.ap(), in_=a), reads=reads, writes=[tk_dbg], dma=True)

    cb = sb([NB], BF16)
    cf = sb([NF], F32)
    xT = sb([NC, T], F32)
    neglam = sb([NL], F32)
    tk_cb, tk_cf, tk_neglam = Tk("cb"), Tk("cf"), Tk("neglam")
    tk_x = [[Tk("xT") for _ in range(2)] for _ in range(NC)]
    A.persist()

    ident_f = cf[:, CF_ID:CF_ID + 128]
    ident_b = cb[:, CB_ID:CB_ID + 128]
    ones_b = cb[:, CB_ONES:CB_ONES + 128]

    def cfc(col):
        return cf[:, col:col + 1]

    try:
        P.op("sp", lambda e: e.dma_start(out=cb, in_=cb_in.ap()), writes=[tk_cb], dma=True)
        P.op("sp", lambda e: e.dma_start(out=cf, in_=cf_in.ap()), writes=[tk_cf], dma=True)

        lt = sb([4, 64], F32)
        ls = sb([8], F32)
        tk_lt, tk_ls = Tk(), Tk()
        for l in range(NL):
            for i in range(2):
                a0 = CF_LAM + (l * 4 + 2 * i) * 64
                P.op("dve", lambda e, i=i, a0=a0: e.tensor_tensor(out=lt[:, i, :], in0=cf[:, a0:a0 + 64],
                                                                   in1=cf[:, a0 + 64:a0 + 128], op=ALU.mult),
                     reads=[tk_cf], writes=[tk_lt])
                P.op("dve", lambda e, i=i: e.reduce_sum(out=ls[:, i:i + 1], in_=lt[:, i, :], axis=AX.X),
                     reads=[tk_lt], writes=[tk_ls])
                P.op("act", lambda e, i=i: e.activation(out=ls[:, 2 + i:3 + i], in_=ls[:, i:i + 1], func=AF.Exp),
                     reads=[tk_ls], writes=[tk_ls])
            P.op("dve", lambda e: e.tensor_tensor(out=ls[:, 4:5], in0=ls[:, 2:3], in1=ls[:, 3:4], op=ALU.subtract),
                 reads=[tk_ls], writes=[tk_ls])
            P.op("dve", lambda e, l=l: e.tensor_scalar(out=neglam[:, l:l + 1], in0=ls[:, 4:5], scalar1=-1.0,
                                                       scalar2=-lambda_init(l), op0=ALU.mult, op1=ALU.add),
                 reads=[tk_ls], writes=[tk_neglam])

        xs = [sb([D], F32) for _ in range(8)]
        tk_xs = [Tk() for _ in range(8)]
        for i in range(8):
            P.op("sp", lambda e, i=i: e.dma_start(out=xs[i], in_=x_in.ap()[i * 128:(i + 1) * 128, :]),
                 writes=[tk_xs[i]], dma=True)
        ev = 0
        for c in range(NC):
            for tg in range(2):
                b = nb()

                def tr(e, c=c, tg=tg, b=b):
                    ins = None
                    for i in range(4):
                        ins = e.transpose(out=banks[b][:, i * 128:(i + 1) * 128],
                                          in_=xs[tg * 4 + i][:, c * 128:(c + 1) * 128], identity=ident_f)
                    return ins
                P.op("pe", tr, reads=[tk_xs[tg * 4 + i] for i in range(4)] + [tk_cf], writes=[bank_tk[b]])
                dst = xT[:, c, tg * 512:(tg + 1) * 512]
                if ev % 2 == 0:
                    P.op("act", lambda e, b=b, dst=dst: e.activation(out=dst, in_=banks[b][:], func=AF.Copy),
                         reads=[bank_tk[b]], writes=[tk_x[c][tg]])
                else:
                    P.op("dve", lambda e, b=b, dst=dst: e.tensor_copy(out=dst, in_=banks[b][:]),
                         reads=[bank_tk[b]], writes=[tk_x[c][tg]])
                ev += 1
        phase_end()
        if stop_after == 'p0a':
            raise StopBuild()

        def rope_phase():
            posi = sb([S], I32)
            posf = sb([S], F32)
            tk_posi, tk_posf = Tk(), Tk()
            P.op("sp", lambda e: e.dma_start(out=posi, in_=pos_in.ap().partition_broadcast(128)), writes=[tk_posi], dma=True)
            P.op("dve", lambda e: e.tensor_copy(out=posf, in_=posi), reads=[tk_posi], writes=[tk_posf])
            RW = 512
            rt = {k: [sb([RW], I32 if k == "ki" else F32) for _ in range(2)] for k in ("a", "y", "ki", "kf", "r", "m", "sn")}
            rtk = {k: [Tk() for _ in range(2)] for k in rt}
            it = 0
            for ty in range(2):
                for kind in range(2):
                    for ch in range(S // RW):
                        s_ = it % 2
                        it += 1
                        cs = slice(ch * RW, (ch + 1) * RW)
                        a, y, ki, kf, r, m, sn = (rt[k][s_] for k in ("a", "y", "ki", "kf", "r", "m", "sn"))
                        ta, ty_, tki, tkf, tr_, tm, tsn = (rtk[k][s_] for k in ("a", "y", "ki", "kf", "r", "m", "sn"))
                        off = PI / 2 if kind == 0 else 0.0
                        P.op("dve", lambda e, a=a, cs=cs, ty=ty, off=off: e.tensor_scalar(
                            out=a, in0=posf[:, cs], scalar1=cfc(CF_INVF + ty), scalar2=off, op0=ALU.mult, op1=ALU.add),
                            reads=[tk_posf, tk_cf], writes=[ta])
                        P.op("dve", lambda e, a=a, y=y: e.tensor_scalar(out=y, in0=a, scalar1=1.0 / (2 * PI), scalar2=None,
                                                                         op0=ALU.mult), reads=[ta], writes=[ty_])
                        P.op("dve", lambda e, y=y, ki=ki: e.tensor_copy(out=ki, in_=y), reads=[ty_], writes=[tki])
                        P.op("dve", lambda e, kf=kf, ki=ki: e.tensor_copy(out=kf, in_=ki), reads=[tki], writes=[tkf])
                        P.op("dve", lambda e, kf=kf, a=a, r=r: e.scalar_tensor_tensor(
                            out=r, in0=kf, scalar=-2 * PI, in1=a, op0=ALU.mult, op1=ALU.add), reads=[tkf, ta], writes=[tr_])
                        P.op("dve", lambda e, r=r, m=m: e.tensor_scalar(out=m, in0=r, scalar1=PI, scalar2=2 * PI,
                                                                         op0=ALU.is_gt, op1=ALU.mult), reads=[tr_], writes=[tm])
                        P.op("dve", lambda e, r=r, m=m: e.tensor_tensor(out=r, in0=r, in1=m, op=ALU.subtract),
                             reads=[tm, tr_], writes=[tr_])
                        P.op("dve", lambda e, r=r: e.tensor_scalar(out=r, in0=r, scalar1=PI, scalar2=-PI,
                                                                    op0=ALU.min, op1=ALU.max), reads=[tr_], writes=[tr_])
                        P.op("act", lambda e, r=r, sn=sn: e.activation(out=sn, in_=r, func=AF.Sin), reads=[tr_], writes=[tsn])
                        if kind == 1:
                            P.op("dve", lambda e, sn=sn, ty=ty: e.tensor_scalar(out=sn, in0=sn, scalar1=cfc(CF_SGN + ty),
                                                                                 scalar2=None, op0=ALU.mult),
                                 reads=[tsn, tk_cf], writes=[tsn])
                        dst = (ropeC if kind == 0 else ropeS)[ty]
                        P.op("sp", lambda e, dst=dst, cs=cs, sn=sn: e.dma_start(out=dst.ap()[:, cs], in_=sn),
                             reads=[tsn], writes=[tk_rope], dma=True, semkey=("sn", s_))
            phase_end()

        def norm_phase(gcol, hT, tk_h, out_dt_is_bf16=True):
            sq = [sb([T], BF16) for _ in range(2)]
            tk_sq = [Tk() for _ in range(2)]
            lnv = sb([T], F32)
            rstd = sb([T], F32)
            tk_lnv, tk_rstd = Tk(), Tk()
            b0, b1 = nb(), nb()
            bs = (b0, b1)
            for c in range(NC):
                s_ = c % 2
                if c % 2 == 0:
                    P.op("act", lambda e, c=c, s_=s_: e.activation(out=sq[s_], in_=xT[:, c, :], func=AF.Square),
                         reads=[tk_x[c][0], tk_x[c][1]], writes=[tk_sq[s_]])
                else:
                    P.op("dve", lambda e, c=c, s_=s_: e.tensor_tensor(out=sq[s_], in0=xT[:, c, :], in1=xT[:, c, :], op=ALU.mult),
                         reads=[tk_x[c][0], tk_x[c][1]], writes=[tk_sq[s_]])

                def mm(e, c=c, s_=s_):
                    ins = None
                    for th in range(2):
                        ins = e.matmul(banks[bs[th]][:], lhsT=ones_b, rhs=sq[s_][:, th * 512:(th + 1) * 512],
                                       start=(c == 0), stop=(c == NC - 1))
                    return ins
                P.op("pe", mm, reads=[tk_sq[s_], tk_cb], writes=[bank_tk[b0], bank_tk[b1]])
            for th in range(2):
                P.op("act", lambda e, th=th: e.activation(out=lnv[:, th * 512:(th + 1) * 512], in_=banks[bs[th]][:],
                                                         func=AF.Ln, bias=cfc(CF_EPS), scale=1.0 / D),
                     reads=[bank_tk[bs[th]], tk_cf], writes=[tk_lnv])
            P.op("act", lambda e: e.activation(out=rstd, in_=lnv, func=AF.Exp, scale=-0.5), reads=[tk_lnv], writes=[tk_rstd])
            for c in range(NC):
                eng_ = "dve"
                P.op(eng_, lambda e, c=c: e.scalar_tensor_tensor(out=hT[:, c, :], in0=xT[:, c, :], scalar=cfc(gcol + c),
                                                                 in1=rstd, op0=ALU.mult, op1=ALU.mult),
                     reads=[tk_x[c][0], tk_x[c][1], tk_rstd, tk_cf], writes=[tk_h[c]])

        hn_calls = [0]

        def head_norm_steps(o, tk_o, gcol, lnbias_col, dst_dram_ap, tk_dst, ring, msb=4):
            sq, tk_sq, ln_, tk_ln, rs, tk_rs, mo, tk_mo = ring
            b = msb
            hn_calls[0] += 1
            skey = ("mo", hn_calls[0] % 2)

            def step_a():
                P.op("act", lambda e: e.activation(out=sq, in_=o, func=AF.Square), reads=[tk_o], writes=[tk_sq])
                P.op("pe", lambda e, b=b: e.matmul(banks[b][:], lhsT=ones_b, rhs=sq, start=True, stop=True),
                     reads=[tk_sq, tk_cb], writes=[bank_tk[b]])

            def step_b():
                P.op("act", lambda e, b=b: e.activation(out=ln_, in_=banks[b][:], func=AF.Ln, bias=cfc(CF_EPS), scale=1.0 / 128),
                     reads=[bank_tk[b], tk_cf], writes=[tk_ln])
                if lnbias_col is None:
                    P.op("act", lambda e: e.activation(out=rs, in_=ln_, func=AF.Exp, scale=-0.5), reads=[tk_ln], writes=[tk_rs])
                else:
                    P.op("act", lambda e: e.activation(out=rs, in_=ln_, func=AF.Exp, scale=-0.5, bias=cfc(lnbias_col)),
                         reads=[tk_ln, tk_cf], writes=[tk_rs])
                P.op("dve", lambda e: e.scalar_tensor_tensor(out=mo, in0=o, scalar=cfc(gcol), in1=rs, op0=ALU.mult, op1=ALU.mult),
                     reads=[tk_o, tk_rs, tk_cf], writes=[tk_mo])
                P.op("sp", lambda e: e.dma_start(out=dst_dram_ap, in_=mo), reads=[tk_mo], writes=[tk_dst], dma=True, semkey=skey)
            return [step_a, step_b]

        def head_norm(*args, **kw):
            for st in head_norm_steps(*args, **kw):
                st()

        for l in range(NL):
            saved_base = A.base
            wq_ring = [sb([3, NC, 128], BF16) for _ in range(2)]
            tk_w_ring = [[Tk() for _ in range(3)] for _ in range(2)]
            ht_pf = sb([NC, 512], BF16)
            tk_htpf = Tk()
            A.persist()

            def load_ht0(l=l, ht_pf=ht_pf, tk_htpf=tk_htpf):
                P.op("sp", lambda e: e.dma_start(out=ht_pf, in_=hbuf_all[l].ap()[:, 0:128, 0:512].rearrange("c p t -> p c t")),
                     reads=[tk_hall[l]], writes=[tk_htpf], dma=True, semkey=("htpf",), bar=False)

            def load_wq(s_, l=l, wq_ring=wq_ring, tk_w_ring=tk_w_ring):
                wcols_ = (2 * s_ * 128, (2 * s_ + 1) * 128, 1024 + s_ * 128)
                for i in range(3):
                    P.op("pool", lambda e, i=i, wcols_=wcols_, s_=s_: e.dma_start(
                        out=wq_ring[s_ % 2][:, i], in_=w_in.ap()[l, :, wcols_[i]:wcols_[i] + 128].rearrange("(c p) n -> p c n", p=128)),
                        writes=[tk_w_ring[s_ % 2][i]], dma=True, semkey=("wq", s_ % 2, i))
            load_wq(0)
            hT = sb([NC, T], BF16)
            tk_h = [Tk() for _ in range(NC)]
            norm_phase(CF_GMIX + l * 16, hT, tk_h)
            for c in range(NC):
                P.op("sp", lambda e, l=l, hT=hT, c=c: e.dma_start(out=hbuf_in[l].ap()[c], in_=hT[:, c, :]),
                     reads=[tk_h[c]], writes=[tk_hin[l][c]], dma=True, semkey=("hin", c))
                P.op("pool", lambda e, l=l, c=c: e.collective_compute("AllGather", ALU.bypass, replica_groups=[[0, 1, 2, 3], [4, 5, 6, 7]],
                                                                     ins=[hbuf_in[l].ap()[c]], outs=[hbuf_all[l].ap()[c]]),
                     reads=[tk_hin[l][c]], writes=[tk_hall[l]], dma=True, inc=1, bar=False)
            if l == 0:
                dump("h0", hT, [128, NC, T], BF16, tk_h)
            phase_end()
            if l == 0:
                rope_phase()
            load_ht0()
            if stop_after == "h%d" % l:
                break

            for s in range(4):
                kind = ("sb", "diff", "dil", "dil")[s]
                rope_ty = {"sb": None, "diff": 1, "dil": 0}[kind]
                qscale = 0.125 if kind == "diff" else 128 ** -0.5
                perm = None if rope_ty is None else cb[:, (CB_PDIL if rope_ty == 0 else CB_PDIFF):][:, 0:128]

                wq = wq_ring[s % 2]
                tk_w = tk_w_ring[s % 2]
                qT = sb([S], BF16)
                kT = sb([S], BF16)
                vv = sb([32, 128], BF16)
                tk_q = [Tk() for _ in range(8)]
                tk_k = [Tk() for _ in range(8)]
                tk_v = [Tk() for _ in range(8)]
                mark_ = A.top
                ht = [sb([NC, 512], BF16) for _ in range(2)]
                tk_ht = [Tk() for _ in range(2)]
                rC = [sb([512], F32) for _ in range(2)]
                rS = [sb([512], F32) for _ in range(2)]
                tk_rC = [Tk() for _ in range(2)]
                tk_rS = [Tk() for _ in range(2)]
                raw = [sb([512], BF16) for _ in range(2)]
                tk_raw = [Tk() for _ in range(2)]
                t1 = [sb([512], F32) for _ in range(2)]
                t2 = [sb([512], F32) for _ in range(2)]
                tk_t1 = [Tk() for _ in range(2)]
                tk_t2 = [Tk() for _ in range(2)]

                ri = 0
                for tt in range(8):
                    hs = tt % 2
                    r_, off = tt // 2, (tt % 2) * 512
                    if tt == 0:
                        hsrc, tk_hsrc = ht_pf, tk_htpf
                    else:
                        hsrc, tk_hsrc = ht[hs], tk_ht[hs]
                        P.op("sp", lambda e, l=l, r_=r_, off=off, hs=hs, ht=ht: e.dma_start(
                            out=ht[hs], in_=hbuf_all[l].ap()[:, r_ * 128:(r_ + 1) * 128, off:off + 512].rearrange("c p t -> p c t")),
                            reads=[tk_hall[l]], writes=[tk_ht[hs]], dma=True, semkey=("ht", hs))
                    if rope_ty is not None:
                        P.op("sp", lambda e, tt=tt, hs=hs, rC=rC, rope_ty=rope_ty: e.dma_start(
                            out=rC[hs], in_=ropeC[rope_ty].ap()[:, tt * 512:(tt + 1) * 512]),
                            reads=[tk_rope], writes=[tk_rC[hs]], dma=True, semkey=("rC", hs))
                        P.op("sp", lambda e, tt=tt, hs=hs, rS=rS, rope_ty=rope_ty: e.dma_start(
                            out=rS[hs], in_=ropeS[rope_ty].ap()[:, tt * 512:(tt + 1) * 512]),
                            reads=[tk_rope], writes=[tk_rS[hs]], dma=True, semkey=("rS", hs))
                    post = []
                    for qi, (dstT, tkd, sc) in enumerate(((qT, tk_q, qscale), (kT, tk_k, 1.0))):
                        b = nb()

                        def mm(e, b=b, qi=qi, wq=wq, hsrc=hsrc):
                            ins = None
                            for c in range(NC):
                                ins = e.matmul(banks[b][:], lhsT=wq[:, qi, c, :], rhs=hsrc[:, c, :],
                                               start=(c == 0), stop=(c == NC - 1))
                            return ins
                        P.op("pe", mm, reads=[tk_w[qi], tk_hsrc], writes=[bank_tk[b]])
                        dst = dstT[:, tt * 512:(tt + 1) * 512]
                        if rope_ty is None:
                            P.op("act", lambda e, b=b, dst=dst, sc=sc: e.activation(out=dst, in_=banks[b][:], func=AF.Copy, scale=sc),
                                 reads=[bank_tk[b]], writes=[tkd[tt]])
                        else:
                            rs_ = ri % 2
                            ri += 1
                            P.op("act", lambda e, b=b, rs_=rs_, sc=sc, raw=raw: e.activation(out=raw[rs_], in_=banks[b][:], func=AF.Copy, scale=sc),
                                 reads=[bank_tk[b]], writes=[tk_raw[rs_]])
                            def post_fn(rs_=rs_, hs=hs, dst=dst, tkd=tkd, tt=tt, perm=perm, raw=raw, t1=t1, t2=t2, rS=rS, rC=rC,
                                        tk_raw=tk_raw, tk_t1=tk_t1, tk_t2=tk_t2, tk_rS=tk_rS, tk_rC=tk_rC):
                                b2 = nb()
                                P.op("pe", lambda e, b2=b2, rs_=rs_, perm=perm, raw=raw: e.matmul(banks[b2][:], lhsT=perm, rhs=raw[rs_], start=True, stop=True),
                                     reads=[tk_raw[rs_], tk_cb], writes=[bank_tk[b2]])
                                P.op("dve", lambda e, b2=b2, rs_=rs_, hs=hs, t1=t1, rS=rS: e.tensor_tensor(out=t1[rs_], in0=banks[b2][:], in1=rS[hs], op=ALU.mult),
                                     reads=[bank_tk[b2], tk_rS[hs]], writes=[tk_t1[rs_]])
                                P.op("pool", lambda e, rs_=rs_, hs=hs, t2=t2, raw=raw, rC=rC: e.tensor_tensor(out=t2[rs_], in0=raw[rs_], in1=rC[hs], op=ALU.mult),
                                     reads=[tk_raw[rs_], tk_rC[hs]], writes=[tk_t2[rs_]])
                                P.op("dve", lambda e, rs_=rs_, dst=dst, t1=t1, t2=t2: e.tensor_tensor(out=dst, in0=t1[rs_], in1=t2[rs_], op=ALU.add),
                                     reads=[tk_t1[rs_], tk_t2[rs_]], writes=[tkd[tt]])
                            post.append(post_fn)
                    b = nb()

                    def mmv(e, b=b, wq=wq, hsrc=hsrc):
                        ins = None
                        for sub in range(4):
                            for c in range(NC):
                                ins = e.matmul(banks[b][:, sub * 128:(sub + 1) * 128], lhsT=hsrc[:, c, sub * 128:(sub + 1) * 128],
                                               rhs=wq[:, 2, c, :], start=(c == 0), stop=(c == NC - 1))
                        return ins
                    P.op("pe", mmv, reads=[tk_w[2], tk_hsrc], writes=[bank_tk[b]])
                    P.op("dve", lambda e, b=b, tt=tt, vv=vv: e.tensor_copy(out=vv[:, tt * 4:(tt + 1) * 4, :],
                                                                            in_=banks[b][:].rearrange("p (a b) -> p a b", a=4)),
                         reads=[bank_tk[b]], writes=[tk_v[tt]])
                    for pf_ in post:
                        pf_()
                if l == 0 and s in (0, 1, 2):
                    dump("q%d" % s, qT, [128, S], BF16, tk_q)
                    dump("k%d" % s, kT, [128, S], BF16, tk_k)
                    dump("v%d" % s, vv, [128, 32, 128], BF16, tk_v)

                P.barrier()
                A.top = mark_
                if s + 1 < 4:
                    load_wq(s + 1)
                    load_ht0()
                NE2 = 3
                E2 = [sb([1024], BF16) for _ in range(NE2)]
                Em2 = [sb([1024], BF16) for _ in range(NE2)]
                tk_E2 = [Tk() for _ in range(NE2)]
                tk_Em2 = [[Tk(), Tk()] for _ in range(NE2)]
                pcnt = [0]
                oacc = [sb([512], F32) for _ in range(3)]
                tk_oacc = [Tk() for _ in range(3)]
                rl = sb([512], F32)
                tk_rl = Tk()
                hn_ring = []
                for _ in range(2):
                    hn_ring.append((sb([512], BF16), Tk(), sb([512], F32), Tk(), sb([512], F32), Tk(), sb([512], BF16), Tk()))
                if kind == "sb":
                    ef2 = [sb([1024], F32) for _ in range(2)]
                    tk_ef2 = [Tk() for _ in range(2)]
                    sp2 = [sb([1024], BF16) for _ in range(3)]
                    tk_sp2 = [Tk() for _ in range(3)]
                    spm2 = [sb([1024], BF16) for _ in range(3)]
                    tk_spm2 = [[Tk(), Tk()] for _ in range(3)]
                    Ts = [sb([512], BF16) for _ in range(4)]
                    tk_Ts = [Tk() for _ in range(4)]
                    tmpT = [sb([512], BF16) for _ in range(2)]
                    tk_tmpT = [Tk() for _ in range(2)]
                OB, LB = 6, 7
                ecnt = [0]
                hcnt = [0]

                pending = []
                olcnt = [0]
                if kind != "sb":
                    lnl = [sb([512], F32) for _ in range(2)]
                    tk_lnl = [Tk() for _ in range(2)]
                    oraw = [sb([512], F32) for _ in range(2)]
                    tk_oraw = [Tk() for _ in range(2)]

                def flush_pending():
                    while pending:
                        pending.pop(0)()

                def run_pending_step():
                    if pending:
                        pending.pop(0)()

                def softmax_pass(j, parts, ktiles, maskfn, o_dst, tk_odst, after=None):
                    p0, p1 = parts
                    qsl = qT[p0:p1, j * 512:(j + 1) * 512]
                    npair = len(ktiles) // 2
                    assert len(ktiles) % 2 == 0
                    sb_of = {}
                    OB, LB = 6, 7
                    ek = olcnt[0] % 2
                    olcnt[0] += 1

                    def issue_s(pi):
                        pb = (pcnt[0] % 2) * 2
                        pcnt[0] += 1
                        sb_of[pi] = pb
                        i0, i1 = ktiles[2 * pi], ktiles[2 * pi + 1]

                        def mm(e, pb=pb, i0=i0, i1=i1):
                            e.matmul(banks[pb][:], lhsT=kT[p0:p1, i0 * 128:(i0 + 1) * 128], rhs=qsl, start=True, stop=True)
                            return e.matmul(banks[pb + 1][:], lhsT=kT[p0:p1, i1 * 128:(i1 + 1) * 128], rhs=qsl, start=True, stop=True)
                        P.op("pe", mm, reads=[tk_k[i0 // 4], tk_k[i1 // 4], tk_q[j]], writes=[bank_tk[pb], bank_tk[pb + 1]])
                    issue_s(0)
                    if npair > 1:
                        issue_s(1)
                    for pi in range(npair):
                        if pi >= 1:
                            run_pending_step()
                        pb = sb_of.pop(pi)
                        es = ecnt[0] % NE2
                        ecnt[0] += 1
                        P.op("act", lambda e, pb=pb, es=es: e.activation(out=E2[es], in_=psum[:, pb * 512:(pb + 2) * 512], func=AF.Exp),
                             reads=[bank_tk[pb], bank_tk[pb + 1]], writes=[tk_E2[es]])
                        srcs = []
                        for h_ in range(2):
                            i = ktiles[2 * pi + h_]
                            hsl = slice(h_ * 512, (h_ + 1) * 512)
                            mk = maskfn(i)
                            if mk is not None:
                                P.op("dve", lambda e, es=es, mk=mk, hsl=hsl: e.tensor_tensor(out=Em2[es][:, hsl], in0=E2[es][:, hsl], in1=mk, op=ALU.mult),
                                     reads=[tk_E2[es], tk_cb], writes=[tk_Em2[es][h_]])
                                srcs.append((Em2[es][:, hsl], tk_Em2[es][h_], i))
                            else:
                                srcs.append((E2[es][:, hsl], tk_E2[es], i))

                        if pi + 2 < npair:
                            issue_s(pi + 2)

                        def mmo(e, srcs=srcs, pi=pi, OB=OB, LB=LB):
                            first, last = (pi == 0), (pi == npair - 1)
                            (s0, _, i0), (s1, _, i1) = srcs
                            e.matmul(banks[OB][:], lhsT=vv[:, i0, :], rhs=s0, start=first, stop=False)
                            e.matmul(banks[OB][:], lhsT=vv[:, i1, :], rhs=s1, start=False, stop=last)
                            e.matmul(banks[LB][:], lhsT=ones_b, rhs=s0, start=first, stop=False)
                            return e.matmul(banks[LB][:], lhsT=ones_b, rhs=s1, start=False, stop=last)
                        P.op("pe", mmo, reads=[srcs[0][1], srcs[1][1], tk_v[srcs[0][2] // 4], tk_v[srcs[1][2] // 4], tk_cb],
                             writes=[bank_tk[OB], bank_tk[LB]])
                    flush_pending()
                    P.op("act", lambda e, ek=ek: e.activation(out=lnl[ek], in_=banks[LB][:], func=AF.Ln), reads=[bank_tk[LB]], writes=[tk_lnl[ek]])
                    P.op("dve", lambda e, ek=ek: e.tensor_copy(out=oraw[ek], in_=banks[OB][:]), reads=[bank_tk[OB]], writes=[tk_oraw[ek]])

                    def epilogue(ek=ek):
                        P.op("act", lambda e: e.activation(out=rl, in_=lnl[ek], func=AF.Exp, scale=-1.0), reads=[tk_lnl[ek]], writes=[tk_rl])
                        P.op("dve", lambda e: e.tensor_tensor(out=o_dst, in0=oraw[ek], in1=rl, op=ALU.mult),
                             reads=[tk_oraw[ek], tk_rl], writes=[tk_odst])
                    pending.append(epilogue)
                    if after is not None:
                        pending.extend(after())

                def sb_head():
                    cstr = cb[:, CB_CSTR:CB_CSTR + 896]
                    nuinc = cb[:, CB_NUINC:CB_NUINC + 128]
                    nones = cb[:, CB_NONES:CB_NONES + 128]
                    SOB = 7
                    units = []
                    for j in range(8):
                        tiles = list(range(4 * j + 3, -1, -1))
                        npj = len(tiles) // 2
                        for p in range(npj):
                            units.append((j, p, tiles[2 * p], tiles[2 * p + 1], npj))
                    G = len(units)
                    st1 = {}
                    st2 = {}

                    def mask_of(i, j):
                        if i >= 4 * j:
                            Dq = 512 * j - 128 * i
                            return cstr[:, Dq + 384:Dq + 384 + 512]
                        return None

                    def stage1(g):
                        j, p, iA, iB, npj = units[g]
                        qsl = qT[:, j * 512:(j + 1) * 512]

                        def mmz(e, iA=iA, iB=iB, qsl=qsl):
                            e.matmul(banks[0][:], lhsT=kT[:, iA * 128:(iA + 1) * 128], rhs=qsl, start=True, stop=True)
                            return e.matmul(banks[1][:], lhsT=kT[:, iB * 128:(iB + 1) * 128], rhs=qsl, start=True, stop=True)
                        P.op("pe", mmz, reads=[tk_k[iA // 4], tk_k[iB // 4], tk_q[j]], writes=[bank_tk[0], bank_tk[1]])
                        fs = g % 2
                        P.op("act", lambda e, fs=fs: e.activation(out=ef2[fs], in_=psum[:, 0:1024], func=AF.Exp),
                             reads=[bank_tk[0], bank_tk[1]], writes=[tk_ef2[fs]])
                        ss = g % 3
                        P.op("act", lambda e, fs=fs, ss=ss: e.activation(out=sp2[ss], in_=ef2[fs], func=AF.Ln, bias=cfc(CF_ONE)),
                             reads=[tk_ef2[fs], tk_cf], writes=[tk_sp2[ss]])
                        curs = []
                        for h_, i in enumerate((iA, iB)):
                            hsl = slice(h_ * 512, (h_ + 1) * 512)
                            mk = mask_of(i, j)
                            if mk is not None:
                                P.op("dve", lambda e, ss=ss, mk=mk, hsl=hsl: e.tensor_tensor(out=spm2[ss][:, hsl], in0=sp2[ss][:, hsl], in1=mk, op=ALU.mult),
                                     reads=[tk_sp2[ss], tk_cb], writes=[tk_spm2[ss][h_]])
                                curs.append((spm2[ss][:, hsl], tk_spm2[ss][h_]))
                            else:
                                curs.append((sp2[ss][:, hsl], tk_sp2[ss]))
                        tprev, tnew = g % 4, (g + 1) % 4
                        if p + 1 < npj:
                            (cA, tA), (cB, tB) = curs
                            if p == 0:
                                P.op("dve", lambda e, cA=cA, cB=cB, tnew=tnew: e.tensor_tensor(out=Ts[tnew], in0=cA, in1=cB, op=ALU.add),
                                     reads=[tA, tB], writes=[tk_Ts[tnew]])
                            else:
                                ts_ = g % 2
                                P.op("dve", lambda e, cA=cA, cB=cB, ts_=ts_: e.tensor_tensor(out=tmpT[ts_], in0=cA, in1=cB, op=ALU.add),
                                     reads=[tA, tB], writes=[tk_tmpT[ts_]])
                                P.op("dve", lambda e, ts_=ts_, tprev=tprev, tnew=tnew: e.tensor_tensor(out=Ts[tnew], in0=Ts[tprev], in1=tmpT[ts_], op=ALU.add),
                                     reads=[tk_tmpT[ts_], tk_Ts[tprev]], writes=[tk_Ts[tnew]])
                        st1[g] = (curs, tprev)

                    def stage2(g):
                        j, p, iA, iB, npj = units[g]
                        qsl = qT[:, j * 512:(j + 1) * 512]
                        curs, tprev = st1.pop(g)
                        (cA, tA), (cB, tB) = curs
                        lb = 2 + 2 * (g % 2)

                        def mml(e, lb=lb, iA=iA, iB=iB, cA=cA, cB=cB, tprev=tprev, p=p, qsl=qsl):
                            e.matmul(banks[lb][:], lhsT=kT[:, iA * 128:(iA + 1) * 128], rhs=qsl, start=True, stop=False)
                            ins = e.matmul(banks[lb][:], lhsT=nuinc, rhs=cA, start=False, stop=(p == 0))
                            if p > 0:
                                ins = e.matmul(banks[lb][:], lhsT=nones, rhs=Ts[tprev], start=False, stop=True)
                            e.matmul(banks[lb + 1][:], lhsT=kT[:, iB * 128:(iB + 1) * 128], rhs=qsl, start=True, stop=False)
                            e.matmul(banks[lb + 1][:], lhsT=nuinc, rhs=cB, start=False, stop=False)
                            ins = e.matmul(banks[lb + 1][:], lhsT=nones, rhs=cA, start=False, stop=(p == 0))
                            if p > 0:
                                ins = e.matmul(banks[lb + 1][:], lhsT=nones, rhs=Ts[tprev], start=False, stop=True)
                            return ins
                        rd = [tk_k[iA // 4], tk_k[iB // 4], tk_q[j], tA, tB, tk_cb] + ([tk_Ts[tprev]] if p > 0 else [])
                        P.op("pe", mml, reads=rd, writes=[bank_tk[lb], bank_tk[lb + 1]])
                        es = g % NE2
                        P.op("act", lambda e, lb=lb, es=es: e.activation(out=E2[es], in_=psum[:, lb * 512:(lb + 2) * 512], func=AF.Exp),
                             reads=[bank_tk[lb], bank_tk[lb + 1]], writes=[tk_E2[es]])
                        srcs = []
                        for h_, i in enumerate((iA, iB)):
                            hsl = slice(h_ * 512, (h_ + 1) * 512)
                            mk = mask_of(i, j)
                            if mk is not None:
                                P.op("dve", lambda e, es=es, mk=mk, hsl=hsl: e.tensor_tensor(out=Em2[es][:, hsl], in0=E2[es][:, hsl], in1=mk, op=ALU.mult),
                                     reads=[tk_E2[es], tk_cb], writes=[tk_Em2[es][h_]])
                                srcs.append((Em2[es][:, hsl], tk_Em2[es][h_], i))
                            else:
                                srcs.append((E2[es][:, hsl], tk_E2[es], i))
                        st2[g] = srcs

                    def stage2b(g):
                        j, p, iA, iB, npj = units[g]
                        (s0, t0_, i0), (s1, t1_, i1) = st2.pop(g)

                        def mmav(e, s0=s0, s1=s1, i0=i0, i1=i1, p=p, npj=npj):
                            e.matmul(banks[SOB][:], lhsT=vv[:, i0, :], rhs=s0, start=(p == 0), stop=False)
                            return e.matmul(banks[SOB][:], lhsT=vv[:, i1, :], rhs=s1, start=False, stop=(p == npj - 1))
                        P.op("pe", mmav, reads=[t0_, t1_, tk_v[i0 // 4], tk_v[i1 // 4]], writes=[bank_tk[SOB]])
                        if p == npj - 1:
                            oslot = j % 2
                            P.op("act", lambda e, oslot=oslot: e.activation(out=oacc[oslot], in_=banks[SOB][:], func=AF.Copy),
                                 reads=[bank_tk[SOB]], writes=[tk_oacc[oslot]])
                            dst = mbuf_in[l].ap()[j // 2, s, :, (j % 2) * 512:(j % 2) * 512 + 512]
                            sa, sb_ = head_norm_steps(oacc[oslot], tk_oacc[oslot], CF_GSB + l, None, dst, tk_min[l], hn_ring[j % 2], msb=6)

                            def s3(sb_=sb_, j=j):
                                sb_()
                                quarter_collective(j)
                            pending.extend([sa, s3])

                    stage1(0)
                    stage1(1)
                    for g in range(G):
                        run_pending_step()
                        stage2(g)
                        if g + 2 < G:
                            stage1(g + 2)
                        if g >= 1:
                            stage2b(g - 1)
                    stage2b(G - 1)
                    flush_pending()

                cinc = cb[:, CB_CINC:CB_CINC + 896]
                toep = cb[:, CB_TOEP:CB_TOEP + TOEP_W]
                def quarter_collective(j, l=l, s=s):
                    if j % 2 == 1:
                        qt = j // 2
                        P.op("pool", lambda e, l=l, qt=qt, s=s: e.collective_compute("AllGather", ALU.bypass, replica_groups=[[0, 1, 2, 3], [4, 5, 6, 7]],
                                                                                    ins=[mbuf_in[l].ap()[qt, s]], outs=[mbuf_all[l].ap()[qt, s]]),
                             reads=[tk_min[l]], writes=[tk_mall[l]], dma=True, inc=1, bar=False)

                if kind == "sb":
                    sb_head()
                for j in range(8 if kind != "sb" else 0):
                    dst = mbuf_in[l].ap()[j // 2, s, :, (j % 2) * 512:(j % 2) * 512 + 512]
                    ring = hn_ring[hcnt[0] % 2]
                    hcnt[0] += 1
                    if kind == "diff":
                        def mk_diff(i, j=j):
                            if i >= 4 * j:
                                Dq = 512 * j - 128 * i
                                return cinc[:, Dq + 384:Dq + 384 + 512]
                            return None
                        kt = list(range(0, 4 * j + 4))

                        def after_diff(j=j, dst=dst, ring=ring, l=l):
                            sa, sb_ = head_norm_steps(oacc[2], tk_oacc[2], CF_GDF + l, CF_LNA + l, dst, tk_min[l], ring, msb=4)

                            def s2():
                                P.op("dve", lambda e, l=l: e.scalar_tensor_tensor(out=oacc[2], in0=oacc[1], scalar=neglam[:, l:l + 1], in1=oacc[0],
                                                                                 op0=ALU.mult, op1=ALU.add),
                                     reads=[tk_oacc[0], tk_oacc[1], tk_neglam], writes=[tk_oacc[2]])
                                sa()

                            def s3():
                                sb_()
                                quarter_collective(j)
                            return [s2, s3]
                        softmax_pass(j, (0, 64), kt, mk_diff, oacc[0], tk_oacc[0])
                        softmax_pass(j, (64, 128), kt, mk_diff, oacc[1], tk_oacc[1], after=after_diff)
                    else:
                        def mk_dil(i, j=j):
                            Dq = 512 * j - 128 * i
                            return toep[:, Dq + 384:Dq + 384 + 512]
                        kt = list(range(max(0, 4 * j - 16), 4 * j + 4))
                        oslot = j % 2

                        def after_dil(j=j, dst=dst, ring=ring, l=l, oslot=oslot):
                            sa, sb_ = head_norm_steps(oacc[oslot], tk_oacc[oslot], CF_GDL + l, None, dst, tk_min[l], ring, msb=4)

                            def s3():
                                sb_()
                                quarter_collective(j)
                            return [sa, s3]
                        softmax_pass(j, (0, 128), kt, mk_dil, oacc[oslot], tk_oacc[oslot], after=after_dil)
                flush_pending()
                phase_end()
                if stop_after == "att%d_%d" % (l, s):
                    break
            if stop_after is not None and stop_after.startswith("att%d" % l):
                break

            A.base = saved_base
            A.top = saved_base
            mx = sb([NC, T], BF16)
            tk_mx = Tk()

            def ld_mx(e, l=l, mx=mx):
                me = e.partition_id() % 4
                return e.dma_start(out=mx, in_=mbuf_all[l].ap()[bass.ds(me, 1)].rearrange("o s (r p) t -> p (o s r) t", p=128))
            if l == 0:
                dump("mx0", mx, [128, NC, T], BF16, [tk_mx])
            NWO = 3
            wo = [sb([NC, 128], BF16) for _ in range(NWO)]
            tk_wo = [Tk() for _ in range(NWO)]

            def mxchunk(n):
                if n < 4:
                    return n
                if n < 8:
                    return 4 + (n - 4)
                m = n - 8
                return (2 + m % 2) * 4 + m // 2

            def ld_wo(dc):
                ws = dc % NWO
                P.op("pool", lambda e, dc=dc, ws=ws, l=l: e.dma_start(
                    out=wo[ws], in_=w_out.ap()[l, :, dc * 128:(dc + 1) * 128].rearrange("(n p) c -> p n c", p=128)),
                    writes=[tk_wo[ws]], dma=True, semkey=("wo", ws))
            ld_wo(0)
            ld_wo(1)
            P.op("pool", ld_mx, reads=[tk_mall[l]], writes=[tk_mx], dma=True, semkey=("mx",))
            for dc in range(NC):
                if dc + 2 < NC:
                    ld_wo(dc + 2)
                ws = dc % NWO
                for th in range(2):
                    b = nb()

                    def mm(e, b=b, ws=ws, th=th, mx=mx):
                        ins = None
                        for n_ in range(NC):
                            ins = e.matmul(banks[b][:], lhsT=wo[ws][:, n_, :], rhs=mx[:, mxchunk(n_), th * 512:(th + 1) * 512],
                                           start=(n_ == 0), stop=(n_ == NC - 1))
                        return ins
                    P.op("pe", mm, reads=[tk_wo[ws], tk_mx], writes=[bank_tk[b]])
                    xsl = xT[:, dc, th * 512:(th + 1) * 512]
                    P.op("dve", lambda e, b=b, xsl=xsl: e.tensor_tensor(out=xsl, in0=xsl, in1=banks[b][:], op=ALU.add),
                         reads=[bank_tk[b], tk_x[dc][th]], writes=[tk_x[dc][th]])
            if l == 0:
                dump("x1", xT, [128, NC, T], F32, [t for c in tk_x for t in c])
            phase_end()
            if stop_after == "oproj%d" % l:
                break

            h2 = sb([NC, T], BF16)
            tk_h2 = [Tk() for _ in range(NC)]
            norm_phase(CF_GFFN + l * 16, h2, tk_h2)
            NWG = 3
            wg = [sb([NC, 128], BF16) for _ in range(NWG)]
            wu = [sb([NC, 128], BF16) for _ in range(NWG)]
            tk_wg = [Tk() for _ in range(NWG)]
            tk_wu = [Tk() for _ in range(NWG)]
            wd = [sb([GFF, D], BF16) for _ in range(2)]
            tk_wd = [Tk() for _ in range(2)]
            actT = [sb([GFF, T], BF16) for _ in range(2)]
            tk_act = [[Tk() for _ in range(GFF)] for _ in range(2)]
            sg = [sb([512], F32) for _ in range(2)]
            tk_sg = [Tk() for _ in range(2)]

            def ld_gu(jc):
                ws = jc % NWG
                P.op("pool", lambda e, jc=jc, ws=ws, l=l: e.dma_start(
                    out=wg[ws], in_=w_gate.ap()[l, :, jc * 128:(jc + 1) * 128].rearrange("(n p) c -> p n c", p=128)),
                    writes=[tk_wg[ws]], dma=True, semkey=("wg", ws))
                P.op("pool", lambda e, jc=jc, ws=ws, l=l: e.dma_start(
                    out=wu[ws], in_=w_up.ap()[l, :, jc * 128:(jc + 1) * 128].rearrange("(n p) c -> p n c", p=128)),
                    writes=[tk_wu[ws]], dma=True, semkey=("wu", ws))

            def ld_wd(gi):
                ws = gi % 2
                P.op("pool", lambda e, gi=gi, ws=ws, l=l: e.dma_start(
                    out=wd[ws], in_=w_down.ap()[l, gi * GFF * 128:(gi + 1) * GFF * 128, :].rearrange("(a p) c -> p a c", p=128)),
                    writes=[tk_wd[ws]], dma=True, semkey=("wd", ws))
            ld_gu(0)
            ld_gu(1)
            ld_wd(0)
            sgc = 0
            NG = NFF // GFF
            for gi in range(NG):
                as_ = gi % 2
                if gi + 1 < NG:
                    ld_wd(gi + 1)
                for jj in range(GFF):
                    jc = gi * GFF + jj
                    if jc + 2 < NFF:
                        ld_gu(jc + 2)
                    ws = jc % NWG
                    for th in range(2):
                        bg, bu = nb(), nb()

                        def mm(e, bg=bg, bu=bu, ws=ws, th=th, h2=h2):
                            ins = None
                            for c in range(NC):
                                ins = e.matmul(banks[bg][:], lhsT=wg[ws][:, c, :], rhs=h2[:, c, th * 512:(th + 1) * 512],
                                               start=(c == 0), stop=(c == NC - 1))
                            for c in range(NC):
                                ins = e.matmul(banks[bu][:], lhsT=wu[ws][:, c, :], rhs=h2[:, c, th * 512:(th + 1) * 512],
                                               start=(c == 0), stop=(c == NC - 1))
                            return ins
                        P.op("pe", mm, reads=[tk_wg[ws], tk_wu[ws]] + tk_h2, writes=[bank_tk[bg], bank_tk[bu]])
                        ss = sgc % 2
                        sgc += 1
                        P.op("act", lambda e, bg=bg, ss=ss: e.activation(out=sg[ss], in_=banks[bg][:], func=AF.Silu),
                             reads=[bank_tk[bg]], writes=[tk_sg[ss]])
                        adst = actT[as_][:, jj, th * 512:(th + 1) * 512]
                        P.op("dve", lambda e, bu=bu, ss=ss, adst=adst: e.tensor_tensor(out=adst, in0=sg[ss], in1=banks[bu][:], op=ALU.mult),
                             reads=[tk_sg[ss], bank_tk[bu]], writes=[tk_act[as_][jj]])
                wsd = gi % 2
                for dc in range(NC):
                    for th in range(2):
                        b = nb()

                        def mmd(e, b=b, dc=dc, th=th, as_=as_, wsd=wsd):
                            ins = None
                            for jj in range(GFF):
                                ins = e.matmul(banks[b][:], lhsT=wd[wsd][:, jj, dc * 128:(dc + 1) * 128],
                                               rhs=actT[as_][:, jj, th * 512:(th + 1) * 512], start=(jj == 0), stop=(jj == GFF - 1))
                            return ins
                        P.op("pe", mmd, reads=[tk_wd[wsd]] + tk_act[as_], writes=[bank_tk[b]])
                        xsl = xT[:, dc, th * 512:(th + 1) * 512]
                        P.op("dve", lambda e, b=b, xsl=xsl: e.tensor_tensor(out=xsl, in0=xsl, in1=banks[b][:], op=ALU.add),
                             reads=[bank_tk[b], tk_x[dc][th]], writes=[tk_x[dc][th]])
            if l == 0:
                dump("x2", xT, [128, NC, T], F32, [t for c in tk_x for t in c])
            phase_end()
            if stop_after == "ffn%d" % l:
                break

    except StopBuild:
        pass

    if stop_after is None:
        yT = sb([NC, T], F32)
        tk_y = [Tk() for _ in range(NC)]
        norm_phase(CF_GFIN, yT, tk_y)
        ot = [sb([D], F32) for _ in range(2)]
        tk_ot = [Tk() for _ in range(2)]
        ev = 0
        for ti in range(8):
            os_ = ti % 2
            for cg in range(4):
                b = nb()

                def tr(e, b=b, cg=cg, ti=ti):
                    ins = None
                    for i in range(4):
                        c = cg * 4 + i
                        ins = e.transpose(out=banks[b][:, i * 128:(i + 1) * 128], in_=yT[:, c, ti * 128:(ti + 1) * 128], identity=ident_f)
                    return ins
                P.op("pe", tr, reads=[tk_y[cg * 4 + i] for i in range(4)] + [tk_cf], writes=[bank_tk[b]])
                dsl = ot[os_][:, cg * 512:(cg + 1) * 512]
                if ev % 2 == 0:
                    P.op("act", lambda e, b=b, dsl=dsl: e.activation(out=dsl, in_=banks[b][:], func=AF.Copy),
                         reads=[bank_tk[b]], writes=[tk_ot[os_]])
                else:
                    P.op("dve", lambda e, b=b, dsl=dsl: e.tensor_copy(out=dsl, in_=banks[b][:]),
                         reads=[bank_tk[b]], writes=[tk_ot[os_]])
                ev += 1
            P.op("sp", lambda e, ti=ti, os_=os_: e.dma_start(out=out_t.ap()[ti * 128:(ti + 1) * 128, :], in_=ot[os_]),
                 reads=[tk_ot[os_]], writes=[tk_out], dma=True, semkey=("ot", os_))
    else:
        zt = sb([D], F32)
        tkz = Tk()
        P.op("dve", lambda e: e.memset(zt, 0.0), writes=[tkz])
        for ti in range(8):
            P.op("sp", lambda e, ti=ti: e.dma_start(out=out_t.ap()[ti * 128:(ti + 1) * 128, :], in_=zt),
                 reads=[tkz], writes=[tk_out], dma=True)
    P.barrier()

    P.emit(nc, stack)
    stack.close()
    return nc, dbg_out


def make_in_maps(inp, need_ffn=True):
    x = np.asarray(inp["x"], np.float32)
    positions = np.asarray(inp["positions"], np.int32)
    w_in = np.asarray(inp["w_in"], np.float32)
    cbv = build_cb()
    cfv = build_cf(inp)
    w_out = np.ascontiguousarray(np.asarray(inp["w_out"], np.float32))
    w_gate = np.ascontiguousarray(np.asarray(inp["w_gate"], np.float32))
    w_up = np.ascontiguousarray(np.asarray(inp["w_up"], np.float32))
    w_down = np.ascontiguousarray(np.asarray(inp["w_down"], np.float32))
    maps = []
    for c in range(8):
        b, g = c // 4, c % 4
        qk_cols, v_cols = [], []
        bases = [(0, g), (1536, g), (3072, 2 * g), (3072, 2 * g + 1)]
        widths = [512, 512, 1024, 1024]
        for (base, h), wdt in zip(bases, widths):
            qk_cols.append(np.arange(base + h * 128, base + (h + 1) * 128))
            qk_cols.append(np.arange(base + wdt + h * 128, base + wdt + (h + 1) * 128))
            v_cols.append(np.arange(base + 2 * wdt + h * 128, base + 2 * wdt + (h + 1) * 128))
        cols = np.concatenate(qk_cols + v_cols)
        maps.append({
            "x": np.ascontiguousarray(x[b, g * T:(g + 1) * T, :]),
            "pos": np.ascontiguousarray(positions[b][None, :]),
            "cb": cbv,
            "cf": cfv,
            "w_in": np.ascontiguousarray(w_in[:, :, cols]),
            "w_out": w_out,
        })
        if need_ffn:
            maps[-1].update({"w_gate": w_gate, "w_up": w_up, "w_down": w_down})
    return maps


_CACHE = {}


def kernel(**inputs):
    if "nc" not in _CACHE:
        _CACHE["nc"] = build_program()[0]
    nc = _CACHE["nc"]
    maps = make_in_maps(inputs)
    res = run_bass_kernel_spmd(nc, maps, core_ids=list(range(8)))
    out = np.zeros((2, S, D), np.float32)
    for c in range(8):
        b, g = c // 4, c % 4
        out[b, g * T:(g + 1) * T, :] = np.asarray(res.results[c]["out"], np.float32)
    return out
```
